# Optimizing a Trainium2 kernel written in Bass

```python
import math
import jax, jax.numpy as jnp
from jax import lax
import numpy as np

D_MODEL = 1024
BATCH = 8
SEQ = 2048
DEPTH = 1

SC_WIDTH = D_MODEL
SC_GROUPS = 16
SC_KERNEL = 3
SSM_EXPAND = 2
SSM_INNER = SSM_EXPAND * D_MODEL
SSM_HEADDIM = 64
SSM_HEADS = SSM_INNER // SSM_HEADDIM
SSM_GROUPS = 8
SSM_STATE = 128
SSM_CONV = 4
SSM_CHUNK = 128
SSM_CONV_DIM = SSM_INNER + 2 * SSM_GROUPS * SSM_STATE
D_FF = 4 * D_MODEL
EPS = 1e-6

COL_SC = 3 * SC_WIDTH
COL_SSM = SSM_INNER + SSM_CONV_DIM + SSM_HEADS
COL_GATE = 2 * D_MODEL
D_IN_PROJ = COL_SC + COL_SSM + COL_GATE

kernel_name = "hybrid_shortconv_ssd_gated_merge"


def rmsnorm(x, w):
    xf = x.astype(jnp.float32)
    xf = xf * lax.rsqrt(jnp.mean(xf * xf, axis=-1, keepdims=True) + EPS)
    return xf.astype(x.dtype) * w


def causal_depthwise_conv(u, w):
    K = w.shape[0]
    L = u.shape[1]
    up = jnp.pad(u, ((0, 0), (K - 1, 0), (0, 0)))
    y = up[:, 0:L] * w[0]
    for k in range(1, K):
        y = y + up[:, k:k + L] * w[k]
    return y


def ssd_chunked(xh, dt, A, Bg, Cg):
    b, l, h, p = xh.shape
    g, n = Bg.shape[2], Bg.shape[3]
    r = h // g
    q = SSM_CHUNK
    c = l // q
    x = xh.astype(jnp.float32).reshape(b, c, q, g, r, p)
    dt = dt.astype(jnp.float32).reshape(b, c, q, g, r)
    B = Bg.astype(jnp.float32).reshape(b, c, q, g, n)
    C = Cg.astype(jnp.float32).reshape(b, c, q, g, n)
    dA = dt * A.astype(jnp.float32).reshape(g, r)
    dA_cs = jnp.cumsum(dA, axis=2)
    xdt = x * dt[..., None]
    seg = dA_cs[:, :, :, None] - dA_cs[:, :, None, :]
    mask = jnp.tril(jnp.ones((q, q), dtype=bool))[:, :, None, None]
    Lmat = jnp.exp(jnp.where(mask, seg, -jnp.inf))
    CB = jnp.einsum('bcign,bcjgn->bcijg', C, B)
    W = CB[..., None] * Lmat
    y_diag = jnp.einsum('bcijgr,bcjgrp->bcigrp', W, xdt)
    decay = jnp.exp(dA_cs[:, :, -1:] - dA_cs)
    states = jnp.einsum('bcjgn,bcjgr,bcjgrp->bcgrpn', B, decay, xdt)
    chunk_decay = jnp.exp(dA_cs[:, :, -1])

    def step(carry, inp):
        s_c, d_c = inp
        new = carry * d_c[..., None, None] + s_c
        return new, carry

    init = jnp.zeros((b, g, r, p, n), jnp.float32)
    _, prev = lax.scan(step, init, (jnp.moveaxis(states, 1, 0), jnp.moveaxis(chunk_decay, 1, 0)))
    prev = jnp.moveaxis(prev, 0, 1)
    y_off = jnp.einsum('bcign,bcgrpn,bcigr->bcigrp', C, prev, jnp.exp(dA_cs))
    return (y_diag + y_off).reshape(b, l, h, p)


def setup_inputs(seed: int = 0) -> dict:
    key = jax.random.key(seed)
    ks = jax.random.split(key, 20)
    f32 = jnp.float32
    nrm = lambda k, shape, fan: jax.random.normal(k, shape, f32) * (fan ** -0.5)
    dt0 = jnp.exp(jax.random.uniform(ks[9], (SSM_HEADS,), f32, math.log(1e-3), math.log(1e-1)))
    return {
        "x": jax.random.normal(ks[0], (BATCH, SEQ, D_MODEL), f32),
        "norm_mix": 1.0 + 0.02 * jax.random.normal(ks[1], (D_MODEL,), f32),
        "w_in": nrm(ks[2], (D_MODEL, D_IN_PROJ), D_MODEL),
        "b_gate": 0.02 * jax.random.normal(ks[3], (COL_GATE,), f32),
        "sc_conv_w": nrm(ks[4], (SC_KERNEL, SC_WIDTH), SC_KERNEL),
        "ssm_conv_w": nrm(ks[5], (SSM_CONV, SSM_CONV_DIM), SSM_CONV),
        "ssm_conv_b": 0.02 * jax.random.normal(ks[6], (SSM_CONV_DIM,), f32),
        "dt_bias": dt0 + jnp.log(-jnp.expm1(-dt0)),
        "A_log": jnp.log(jax.random.uniform(ks[7], (SSM_HEADS,), f32, 1.0, 16.0)),
        "D_skip": 1.0 + 0.02 * jax.random.normal(ks[8], (SSM_HEADS,), f32),
        "ssm_norm_w": 1.0 + 0.02 * jax.random.normal(ks[10], (SSM_INNER,), f32),
        "w_branch_sc": nrm(ks[11], (SC_WIDTH, D_MODEL), SC_WIDTH),
        "w_branch_ssm": nrm(ks[12], (SSM_INNER, D_MODEL), SSM_INNER),
        "w_out": nrm(ks[13], (D_MODEL, D_MODEL), D_MODEL),
        "norm_mlp": 1.0 + 0.02 * jax.random.normal(ks[14], (D_MODEL,), f32),
        "w_mlp1": nrm(ks[15], (D_MODEL, D_FF), D_MODEL),
        "w_mlp2": nrm(ks[16], (D_FF, D_MODEL), D_FF),
        "norm_final": 1.0 + 0.02 * jax.random.normal(ks[17], (D_MODEL,), f32),
    }


def reference(x, norm_mix, w_in, b_gate, sc_conv_w, ssm_conv_w, ssm_conv_b, dt_bias, A_log,
              D_skip, ssm_norm_w, w_branch_sc, w_branch_ssm, w_out, norm_mlp, w_mlp1, w_mlp2,
              norm_final):
    b, l, _ = x.shape
    for _layer in range(DEPTH):
        h = rmsnorm(x, norm_mix)
        proj = h @ w_in
        sc_part = proj[..., :COL_SC]
        ssm_part = proj[..., COL_SC:COL_SC + COL_SSM]
        gate_part = proj[..., COL_SC + COL_SSM:] + b_gate

        B_sc = sc_part[..., :SC_WIDTH]
        C_sc = sc_part[..., SC_WIDTH:2 * SC_WIDTH]
        x_sc = sc_part[..., 2 * SC_WIDTH:]
        y_a = B_sc * causal_depthwise_conv(C_sc * x_sc, sc_conv_w)
        br_a = y_a @ w_branch_sc

        z = ssm_part[..., :SSM_INNER]
        xBC = ssm_part[..., SSM_INNER:SSM_INNER + SSM_CONV_DIM]
        dt_raw = ssm_part[..., SSM_INNER + SSM_CONV_DIM:]
        xBC = jax.nn.silu(causal_depthwise_conv(xBC, ssm_conv_w) + ssm_conv_b)
        xs = xBC[..., :SSM_INNER].reshape(b, l, SSM_HEADS, SSM_HEADDIM)
        Bg = xBC[..., SSM_INNER:SSM_INNER + SSM_GROUPS * SSM_STATE].reshape(b, l, SSM_GROUPS, SSM_STATE)
        Cg = xBC[..., SSM_INNER + SSM_GROUPS * SSM_STATE:].reshape(b, l, SSM_GROUPS, SSM_STATE)
        dt = jax.nn.softplus(dt_raw.astype(jnp.float32) + dt_bias.astype(jnp.float32))
        A = -jnp.exp(A_log.astype(jnp.float32))
        y = ssd_chunked(xs, dt, A, Bg, Cg) + D_skip.astype(jnp.float32)[:, None] * xs.astype(jnp.float32)
        y = y.reshape(b, l, SSM_INNER)
        yz = (y * jax.nn.silu(z.astype(jnp.float32))).reshape(b, l, SSM_GROUPS, SSM_INNER // SSM_GROUPS)
        yz = yz * lax.rsqrt(jnp.mean(yz * yz, axis=-1, keepdims=True) + EPS)
        y_b = yz.reshape(b, l, SSM_INNER).astype(x.dtype) * ssm_norm_w
        br_b = y_b @ w_branch_ssm

        g = jax.nn.sigmoid(gate_part)
        merged = g[..., :D_MODEL] * br_a + g[..., D_MODEL:] * br_b
        x = x + merged @ w_out

        h2 = rmsnorm(x, norm_mlp)
        x = x + jnp.square(jax.nn.relu(h2 @ w_mlp1)) @ w_mlp2
    return rmsnorm(x, norm_final)
```

```python
import numpy as np
import concourse.bass as bass
import concourse.mybir as mybir
from concourse.bass_utils import run_bass_kernel_spmd

F32 = mybir.dt.float32
BF16 = mybir.dt.bfloat16
AF = mybir.ActivationFunctionType
ALU = mybir.AluOpType

D = 1024
T = 2048
TB = 512
NTB = T // TB
DFF = 4096
DIN = 11296
EPS = 1e-6
C_SC = 0
C_Z = 3072
C_X = 5120
C_B = 7168
C_C = 8192
C_DT = 9216
C_G = 9248
NW = 4
SAME_ENG_SYNC = True
USE_POOL = True

V_NMIX, V_NMLP, V_NFIN, V_BG, V_SCW, V_SSW, V_SSB, V_SNW, V_DCOL, V_DTB, V_ALOG = (
    0, 8, 16, 24, 40, 64, 192, 224, 240, 256, 288)
NV = 320

ALIAS = {"sq": "G2", "mgT": "G2", "hT": "G3", "h2T": "G3", "zs": "BIG1", "u4": "BIG1", "u4_h": "BIG1", "acc4": "BIG1", "xsT": "BIG1", "yT": "BIG1",
         "BT": "BIG1", "CT": "BIG1", "sq2": "BIG1", "aT": "BIG1"}


class Prog:
    def __init__(self):
        self.ops = []

    def _norm(self, keys):
        out = []
        for k in keys:
            out.append(k)
            base = k[0] if isinstance(k, tuple) else k
            if base in ALIAS:
                out.append(ALIAS[base])
        return tuple(dict.fromkeys(out))

    def _alias(self, keys):
        out = []
        for k in keys:
            base = k[0] if isinstance(k, tuple) else k
            if base in ALIAS:
                out.append(ALIAS[base])
        return out

    def op(self, eng, fn, reads=(), writes=()):
        rd = self._norm(tuple(reads) + tuple(self._alias(writes)))
        self.ops.append(dict(eng=eng, fn=fn, reads=rd, writes=tuple(writes), sem=None))

    def dma(self, queue, sem, fn, reads=(), writes=()):
        rd = self._norm(tuple(reads) + tuple(self._alias(writes)))
        self.ops.append(dict(eng=queue, fn=fn, reads=rd, writes=tuple(writes), sem=sem))

    def emit(self, nc, block, stack):
        ops = self.ops
        last_writer, readers = {}, {}
        for i, o in enumerate(ops):
            deps = set()
            for k in o["reads"]:
                if k in last_writer:
                    deps.add(last_writer[k])
            for k in o["writes"]:
                if k in last_writer:
                    deps.add(last_writer[k])
                deps.update(readers.get(k, ()))
            deps.discard(i)
            o["deps"] = deps
            for k in o["reads"]:
                readers.setdefault(k, []).append(i)
            for k in o["writes"]:
                last_writer[k] = i
                readers[k] = []
        for o in ops:
            o["signal"] = o["sem"] is not None
        for o in ops:
            for d in o["deps"]:
                p = ops[d]
                if p["sem"] is None:
                    if p["eng"] == o["eng"] and (p["eng"] == "pe" or not SAME_ENG_SYNC) and o["sem"] is None:
                        continue
                    p["signal"] = True
        cnt = {}
        semnames = set()
        LIM = 1000
        for o in ops:
            if o["sem"] is not None:
                s = o["sem"]
                o["semname"] = s
                cnt[s] = cnt.get(s, 0) + 16
                o["tick"] = cnt[s]
                semnames.add(s)
            elif o["signal"]:
                base = "eng_" + o["eng"]
                c = cnt.get(base, 0)
                cnt[base] = c + 1
                s = "%s_%d" % (base, c // LIM)
                o["semname"] = s
                o["tick"] = c % LIM + 1
                semnames.add(s)
            else:
                o["semname"] = None
        sems = {s: stack.enter_context(nc.semaphore(s)) for s in sorted(semnames)}
        self.maxcnt = cnt

        def run_engine(engname, e):
            known = {}
            for o in ops:
                if o["eng"] != engname:
                    continue
                need = {}
                for d in o["deps"]:
                    p = ops[d]
                    if p["sem"] is None and p["eng"] == engname and o["sem"] is None:
                        if engname == "pe" or not SAME_ENG_SYNC:
                            continue
                    if not p["signal"]:
                        continue
                    s = p["semname"]
                    need[s] = max(need.get(s, 0), p["tick"])
                for s, v in need.items():
                    if known.get(s, 0) < v:
                        e.wait_ge(sems[s], v)
                        known[s] = v
                inst = o["fn"](e)
                if o["signal"]:
                    assert inst is not None
                    inst.then_inc(sems[o["semname"]], 16 if o["sem"] is not None else 1)

        @block.tensor
        def _(e):
            run_engine("pe", e)

        @block.scalar
        def _(e):
            run_engine("act", e)

        @block.vector
        def _(e):
            run_engine("dve", e)

        @block.gpsimd
        def _(e):
            run_engine("pool", e)

        @block.sync
        def _(e):
            run_engine("sp", e)


def build_nc(debug=False, ntb=NTB):
    import contextlib
    nc = bass.Bass("TRN2", target_bir_lowering=False)
    P = Prog()
    dr = {}
    dr["xT"] = nc.dram_tensor("xT", [D, T], F32, kind="ExternalInput").ap()
    dr["w_in"] = nc.dram_tensor("w_in", [D, DIN], F32, kind="ExternalInput").ap()
    dr["w_bsc"] = nc.dram_tensor("w_bsc", [D, D], F32, kind="ExternalInput").ap()
    dr["w_bssm"] = nc.dram_tensor("w_bssm", [2 * D, D], F32, kind="ExternalInput").ap()
    dr["w_out"] = nc.dram_tensor("w_out", [D, D], F32, kind="ExternalInput").ap()
    dr["w_m1"] = nc.dram_tensor("w_m1", [D, DFF], F32, kind="ExternalInput").ap()
    dr["w_m2"] = nc.dram_tensor("w_m2", [DFF, D], F32, kind="ExternalInput").ap()
    dr["vecs"] = nc.dram_tensor("vecs", [128, NV], F32, kind="ExternalInput").ap()
    dr["consts"] = nc.dram_tensor("consts", [128, 512], F32, kind="ExternalInput").ap()
    outT = nc.dram_tensor("outT", [D, T], F32, kind="ExternalOutput").ap()
    s_in = nc.dram_tensor("s_in", [D, DIN], BF16, kind="Internal").ap()
    s_bsc = nc.dram_tensor("s_bsc", [D, D], BF16, kind="Internal").ap()
    s_bssm = nc.dram_tensor("s_bssm", [2 * D, D], BF16, kind="Internal").ap()
    s_out = nc.dram_tensor("s_out", [D, D], BF16, kind="Internal").ap()
    s_m1 = nc.dram_tensor("s_m1", [D, DFF], BF16, kind="Internal").ap()
    s_m2 = nc.dram_tensor("s_m2", [DFF, D], BF16, kind="Internal").ap()
    dbg_out = {}

    with contextlib.ExitStack() as st:
        def sb(name, shape, dt):
            return st.enter_context(nc.sbuf_tensor(name, shape, dt))

        vecs = sb("vecs_sb", [128, NV], F32)
        cst = sb("cst", [128, 512], F32)
        cbf = sb("cbf", [128, 256], BF16)
        Abc = sb("Abc", [128, 32], F32)
        XB = sb("XB", [128, 8, TB], F32)
        sq = sb("sq", [128, 8, TB], BF16)
        hT = sb("hT", [128, 8, TB], BF16)
        rt = sb("rt", [128, TB], F32)
        rstd = sb("rstd", [128, TB], F32)
        Bsb = sb("Bsb", [128, TB], F32)
        Csb = sb("Csb", [128, TB], F32)
        usc = sb("usc", [128, TB + 2], F32)
        accs = sb("accs", [128, TB], F32)
        car_sc = sb("car_sc", [128, 8, 2], F32)
        car_ss = sb("car_ss", [128, 8, 4, 3], F32)
        yaT = sb("yaT", [128, 8, TB], BF16)
        tbl = sb("tbl", [128, 8, 128], F32)
        BIG1 = sb("BIG1", [128, 8192], F32)
        BIG2 = sb("BIG2", [128, 6672], F32)
        R_ = [sb(f"R{i}", [128, 4, 128], F32) for i in range(2)]
        LT_ = [sb(f"LT{i}", [128, 4, 128], F32) for i in range(2)]
        ecs_ = [sb(f"ecs{i}", [128, 4, 128], F32) for i in range(2)]
        CBm_ = [sb(f"CBm{i}", [128, 128], F32) for i in range(2)]
        WT_ = [sb(f"WT{i}", [128, 4, 128], BF16) for i in range(2)]
        CsT_ = [sb(f"CsT{i}", [128, 4, 128], BF16) for i in range(2)]
        xdt_ = [sb(f"xdt{i}", [128, 4, 64], BF16) for i in range(2)]
        xdtd_ = [sb(f"xdtd{i}", [128, 4, 64], BF16) for i in range(2)]
        Btok_ = [sb(f"Btok{i}", [128, 128], BF16) for i in range(2)]
        stf = sb("stf", [128, 8, 256], F32)
        stb = sb("stb", [128, 8, 256], BF16)
        ybT = sb("ybT", [128, 16, TB], BF16)
        mgT = sq
        h2T = hT
        gsb = sb("gsb", [128, TB], F32)
        m1sb = sb("m1sb", [128, TB], F32)
        m2sb = sb("m2sb", [128, TB], F32)
        rl = sb("rl", [128, TB], F32)
        wdt = sb("wdt", [128, 8, 32], BF16)
        dummy = sb("fence_t", [128, 4], F32)
        WP = [sb(f"WP{i}", [128, 4096], BF16) for i in range(NW)]
        mm = st.enter_context(nc.psum_tensor("mm", [128, 4, 512], F32))
        S0 = st.enter_context(nc.psum_tensor("S0", [128, 512], F32))
        S1 = st.enter_context(nc.psum_tensor("S1", [128, 512], F32))
        S2 = st.enter_context(nc.psum_tensor("S2", [128, 512], F32))
        S3 = st.enter_context(nc.psum_tensor("S3", [128, 512], F32))

        ident = cst[:, 0:128]
        tri = cst[:, 128:256]
        Umat = cst[:, 256:384]
        ones = cst[:, 384:512]
        ident_bf = cbf[:, 0:128]
        ones_bf = cbf[:, 128:256]
        GS = []
        for si_, big in enumerate((BIG1, BIG2)):
            GS.append(dict(
                zs=big[:, 0:1024].rearrange("p (j t) -> p j t", j=2),
                u4=big[:, 1024:1024 + 2060].rearrange("p (b t) -> p b t", b=4),
                acc4=big[:, 3088:3600],
                xsT=big[:, 3600:4624].rearrange("p (j t) -> p j t", j=2),
                yT=big[:, 4624:5648].rearrange("p (j t) -> p j t", j=2),
                BT=big[:, 5648:5904].bitcast(BF16),
                CT=big[:, 5904:6160].bitcast(BF16),
                sq2=big[:, 6160:6672].bitcast(BF16).rearrange("p (j t) -> p j t", j=2),
                rt=(rt, gsb)[si_], rstd=(rstd, m1sb)[si_], rtk=("rt", "gsb")[si_], rstdk=("rstd", "m1sb")[si_]))
        PS = [(S0, S1, S2, S3), (S0, S1, S2, S3)]
        aT = BIG1[:, :].bitcast(BF16).rearrange("p (f t) -> p f t", f=32)

        T1, TE, TDT, TDA, TCS, T2, TDEC, TCD = [tbl[:, i, :].rearrange("p (c h) -> p c h", c=4) for i in range(8)]

        mmc = [0]

        def mmbank():
            i = mmc[0] % 4
            mmc[0] += 1
            return mm[:, i, :], ("mm", i)

        wpc = [0]

        def wload(src, shape, reads):
            i = wpc[0] % NW
            wpc[0] += 1
            n = int(np.prod(shape[1:]))
            dst = WP[i][:, 0:n]
            ka, kb = ("WP", i, "a"), ("WP", i, "b")
            if len(shape) == 3:
                dst = dst.rearrange("p (a b) -> p a b", a=shape[1])
                P.dma("sp", f"wp{i}a", lambda e, dst=dst, src=src: e.dma_start(out=dst, in_=src),
                      reads=reads, writes=[ka, kb])
            else:
                dst = dst.rearrange("p (s a b) -> p s a b", s=shape[1], a=shape[2])
                for s_, kk in ((0, ka), (1, kb)):
                    P.dma("sp", f"wp{i}" + "ab"[s_], lambda e, d=dst[:, s_], r=src[:, s_]: e.dma_start(out=d, in_=r),
                          reads=reads, writes=[kk])
            return dst, [ka, kb]

        def proj(ps, lhs, rhs, nk):
            def fn(e):
                last = None
                for k in range(nk):
                    last = e.matmul(ps, lhs(k), rhs(k), start=(k == 0), stop=(k == nk - 1))
                return last
            return fn

        def dbg(name, ap, shape, reads):
            if not debug:
                return
            d = nc.dram_tensor("dbg_" + name, list(shape), ap.dtype, kind="ExternalOutput").ap()
            dbg_out[name] = d
            P.dma("sp", "dbg_" + name, lambda e: e.dma_start(out=d, in_=ap), reads=reads, writes=[("dbg", name)])

        P.dma("sp", "ld_vecs", lambda e: e.dma_start(out=vecs[:], in_=dr["vecs"]), writes=["vecs"])
        P.dma("sp", "ld_cst", lambda e: e.dma_start(out=cst[:], in_=dr["consts"]), writes=["cst"])
        P.op("dve", lambda e: e.tensor_copy(out=cbf[:, 0:128], in_=ident), reads=["cst"], writes=["cbf0"])
        P.op("dve", lambda e: e.tensor_copy(out=cbf[:, 128:256], in_=ones), reads=["cst"], writes=["cbf1"])
        CB = ["cbf0", "cbf1", "cst"]
        P.op("act", lambda e: e.activation(out=Abc[:], in_=vecs[:, V_ALOG:V_ALOG + 32], func=AF.Exp),
             reads=["vecs"], writes=["Abc"])
        P.op("dve", lambda e: e.tensor_scalar(out=Abc[:], in0=Abc[:], scalar1=-1.0, scalar2=None, op0=ALU.mult),
             reads=["Abc"], writes=["Abc"])
        P.op("dve", lambda e: e.memset(stf[:], 0.0), writes=[("stf", g) for g in range(8)])
        P.op("dve", lambda e: e.memset(stb[:], 0.0), writes=[("stb", g) for g in range(8)])
        P.op("dve", lambda e: e.memset(car_sc[:], 0.0), writes=["car_sc"])
        P.op("dve", lambda e: e.memset(car_ss[:], 0.0), writes=["car_ss"])

        castc = [0]

        def cast(dst, src, key):
            i = castc[0]
            castc[0] += 1
            P.dma("pool", f"cast{i}",
                  lambda e: e.dma_start(out=dst, in_=src, max_dma_last_dim=4096),
                  writes=[key])

        K_SC = [("scr", "in_sc", r) for r in range(2)]
        for r in range(2):
            cast(s_in[r * 512:(r + 1) * 512, 0:3072], dr["w_in"][r * 512:(r + 1) * 512, 0:3072], K_SC[r])
        K_DT = [("scr", "in_dt")]
        cast(s_in[:, C_DT:C_DT + 32], dr["w_in"][:, C_DT:C_DT + 32], K_DT[0])
        K_SSD = [("scr", "in_ssd", r) for r in range(4)]
        for r in range(4):
            cast(s_in[r * 256:(r + 1) * 256, C_Z:C_DT], dr["w_in"][r * 256:(r + 1) * 256, C_Z:C_DT], K_SSD[r])
        K_G = [("scr", "in_g", r) for r in range(2)]
        for r in range(2):
            cast(s_in[r * 512:(r + 1) * 512, C_G:DIN], dr["w_in"][r * 512:(r + 1) * 512, C_G:DIN], K_G[r])
        K_BSC = [("scr", "bsc")]
        cast(s_bsc, dr["w_bsc"], K_BSC[0])
        K_BSSM = [("scr", "bssm", r) for r in range(2)]
        for r in range(2):
            cast(s_bssm[r * 1024:(r + 1) * 1024, :], dr["w_bssm"][r * 1024:(r + 1) * 1024, :], K_BSSM[r])
        K_OUT = [("scr", "out")]
        cast(s_out, dr["w_out"], K_OUT[0])
        K_M1 = [("scr", "m1", r) for r in range(4)]
        for r in range(4):
            cast(s_m1[r * 256:(r + 1) * 256, :], dr["w_m1"][r * 256:(r + 1) * 256, :], K_M1[r])
        K_M2 = [("scr", "m2", r) for r in range(4)]
        for r in range(4):
            cast(s_m2[r * 1024:(r + 1) * 1024, :], dr["w_m2"][r * 1024:(r + 1) * 1024, :], K_M2[r])

        v_in = s_in.rearrange("(k p) n -> p k n", p=128)
        P.dma("sp", "ld_wdt", lambda e: e.dma_start(out=wdt[:], in_=v_in[:, :, C_DT:C_DT + 32]),
              reads=K_DT, writes=["wdt"])

        xT_v = dr["xT"].rearrange("(k p) t -> p k t", p=128)
        outT_v = outT.rearrange("(k p) t -> p k t", p=128)

        def rmsnorm_to(dst_bf, vcol, ndiv, tag):
            P.op("act", lambda e: e.activation(out=sq[:], in_=XB[:], func=AF.Square),
                 reads=[("XB", k) for k in range(8)], writes=["sq"])
            ps, pk = mmbank()
            P.op("pe", proj(ps, lambda k: ones_bf, lambda k: sq[:, k, :], 8), reads=["sq"] + CB, writes=[pk])
            P.op("act", lambda e: e.activation(out=rt[:], in_=ps, func=AF.Sqrt, scale=1.0 / ndiv, bias=EPS),
                 reads=[pk], writes=["rt"])
            P.op("dve", lambda e: e.reciprocal(out=rstd[:], in_=rt[:]), reads=["rt"], writes=["rstd"])
            for k in range(8):
                P.op("dve", lambda e, k=k: e.scalar_tensor_tensor(
                    out=dst_bf[:, k, :], in0=XB[:, k, :], scalar=vecs[:, vcol + k:vcol + k + 1], in1=rstd[:],
                    op0=ALU.mult, op1=ALU.mult),
                    reads=[("XB", k), "rstd", "vecs"], writes=[(tag, k)])

        chunkc = [0]

        for tb in range(ntb):
            t0 = tb * TB
            PL = "pool" if (USE_POOL and tb > 0) else "dve"
            P.op("dve", lambda e: e.memset(dummy[:, 2:3], 0.0), writes=["G3", "dummy2"])
            P.dma("sp", "ld_x", lambda e, t0=t0: e.dma_start(out=XB[:], in_=xT_v[:, :, t0:t0 + TB]),
                  writes=[("XB", k) for k in range(8)])
            rmsnorm_to(hT, V_NMIX, float(D), "hT")
            HT = [("hT", k) for k in range(8)]
            if tb == 0:
                dbg("hT", hT[:], [128, 8, TB], HT)

            for half in range(2):
                c0 = half * 512
                wB, kB = wload(v_in[:, :, C_SC + c0:C_SC + c0 + 512], [128, 8, 512], K_SC)
                wC, kC = wload(v_in[:, :, C_SC + 1024 + c0:C_SC + 1024 + c0 + 512], [128, 8, 512], K_SC)
                wX, kX = wload(v_in[:, :, C_SC + 2048 + c0:C_SC + 2048 + c0 + 512], [128, 8, 512], K_SC)
                for c4 in range(4):
                    cb = half * 4 + c4
                    cs_ = slice(c4 * 128, (c4 + 1) * 128)
                    psB, pkB = mmbank()
                    P.op("pe", proj(psB, lambda k, w=wB, s=cs_: w[:, k, s], lambda k: hT[:, k, :], 8),
                         reads=HT + kB, writes=[pkB])
                    psC, pkC = mmbank()
                    P.op("pe", proj(psC, lambda k, w=wC, s=cs_: w[:, k, s], lambda k: hT[:, k, :], 8),
                         reads=HT + kC, writes=[pkC])
                    psX, pkX = mmbank()
                    P.op("pe", proj(psX, lambda k, w=wX, s=cs_: w[:, k, s], lambda k: hT[:, k, :], 8),
                         reads=HT + kX, writes=[pkX])
                    P.op("act", lambda e, ps=psB: e.activation(out=Bsb[:], in_=ps, func=AF.Identity),
                         reads=[pkB], writes=["Bsb"])
                    P.op("act", lambda e, ps=psC: e.activation(out=Csb[:], in_=ps, func=AF.Identity),
                         reads=[pkC], writes=["Csb"])
                    P.op("dve", lambda e, cb=cb: e.tensor_copy(out=usc[:, 0:2], in_=car_sc[:, cb, :]),
                         reads=["car_sc"], writes=["usc_h"])
                    P.op("dve", lambda e, ps=psX: e.tensor_tensor(out=usc[:, 2:TB + 2], in0=ps, in1=Csb[:], op=ALU.mult),
                         reads=[pkX, "Csb"], writes=["usc"])
                    P.op("dve", lambda e, cb=cb: e.tensor_copy(out=car_sc[:, cb, :], in_=usc[:, TB:TB + 2]),
                         reads=["usc"], writes=["car_sc"])
                    wv = V_SCW + cb * 3

                    P.op("dve", lambda e, wv=wv: e.tensor_scalar(out=accs[:], in0=usc[:, 0:TB],
                                                                 scalar1=vecs[:, wv:wv + 1], scalar2=None, op0=ALU.mult),
                         reads=["usc", "usc_h", "vecs"], writes=["accs"])
                    for tap in (1, 2):
                        P.op("dve", lambda e, wv=wv, tap=tap: e.scalar_tensor_tensor(
                            out=accs[:], in0=usc[:, tap:tap + TB], scalar=vecs[:, wv + tap:wv + tap + 1], in1=accs[:],
                            op0=ALU.mult, op1=ALU.add),
                            reads=["usc", "usc_h", "accs", "vecs"], writes=["accs"])
                    P.op("dve", lambda e, cb=cb: e.tensor_tensor(out=yaT[:, cb, :], in0=accs[:], in1=Bsb[:], op=ALU.mult),
                         reads=["accs", "Bsb"], writes=[("yaT", cb)])
            YA = [("yaT", k) for k in range(8)]
            if tb == 0:
                dbg("yaT", yaT[:], [128, 8, TB], YA)

            ps, pk = mmbank()

            def dtmm(e, ps=ps):
                last = None
                for cc in range(4):
                    for k in range(8):
                        last = e.matmul(ps[:, cc * 32:(cc + 1) * 32], hT[:, k, cc * 128:(cc + 1) * 128], wdt[:, k, :],
                                        start=(k == 0), stop=(k == 7))
                return last
            P.op("pe", dtmm, reads=HT + ["wdt"], writes=[pk])
            dtb_b = vecs[:, V_DTB:V_DTB + 32].unsqueeze(1).broadcast_to([128, 4, 32])
            A_b = Abc[:].unsqueeze(1).broadcast_to([128, 4, 32])
            P.op("dve", lambda e, ps=ps: e.tensor_tensor(out=T1, in0=ps[:, 0:128].rearrange("p (c h) -> p c h", c=4),
                                                         in1=dtb_b, op=ALU.add),
                 reads=[pk, "vecs"], writes=["T1"])
            P.op("act", lambda e: e.activation(out=TE, in_=T1, func=AF.Exp), reads=["T1"], writes=["TE"])
            P.op("act", lambda e: e.activation(out=TDT, in_=TE, func=AF.Ln, bias=1.0), reads=["TE"], writes=["TDT"])
            P.op("dve", lambda e: e.tensor_tensor(out=TDA, in0=TDT, in1=A_b, op=ALU.mult),
                 reads=["TDT", "Abc"], writes=["TDA"])
            ps2, pk2 = mmbank()

            def csmm(e, ps2=ps2):
                e.matmul(ps2[:, 0:128], tri, tbl[:, 3, :], start=True, stop=True)
                return e.matmul(ps2[:, 128:256], ones, tbl[:, 3, :], start=True, stop=True)
            P.op("pe", csmm, reads=["TDA", "cst"], writes=[pk2])
            P.op("act", lambda e, ps2=ps2: e.activation(out=tbl[:, 4, :], in_=ps2[:, 0:128], func=AF.Identity),
                 reads=[pk2], writes=["TCS"])
            P.op("dve", lambda e, ps2=ps2: e.tensor_tensor(out=tbl[:, 5, :], in0=ps2[:, 128:256], in1=tbl[:, 4, :],
                                                           op=ALU.subtract),
                 reads=[pk2, "TCS"], writes=["T2"])
            P.op("act", lambda e: e.activation(out=tbl[:, 6, :], in_=tbl[:, 5, :], func=AF.Exp),
                 reads=["T2"], writes=["TDEC"])
            P.op("act", lambda e, ps2=ps2: e.activation(out=tbl[:, 7, :], in_=ps2[:, 128:256], func=AF.Exp),
                 reads=[pk2], writes=["TCD"])
            if tb == 0:
                dbg("tbl", tbl[:], [128, 8, 128], ["T1", "TE", "TDT", "TDA", "TCS", "T2", "TDEC", "TCD"])

            P.op("dve", lambda e: e.memset(dummy[:, 0:1], 0.0), writes=["BIG1", "dummy0"])
            v_zx = s_in[:, C_Z:C_B].rearrange("(k p) (s n) -> p s k n", p=128, s=2)
            v_bc = s_in[:, C_B:C_DT].rearrange("(k p) (s n) -> p s k n", p=128, s=2)

            def group_gen(g, si, wBC, kBC, wZX, kZX):
                B_ = GS[si]
                zs, u4, acc4, xsT, yT, BT, CT, sq2 = (B_["zs"], B_["u4"], B_["acc4"], B_["xsT"], B_["yT"], B_["BT"],
                                                       B_["CT"], B_["sq2"])
                rt_, rstd_ = B_["rt"], B_["rstd"]
                S0, S1, S2, S3 = PS[si]
                kS = lambda n: "S%d" % n
                K = lambda n, *a: (n, si) + tuple(a)
                go = (g % 2) * 128
                for j in range(2):
                    ps, pk = mmbank()
                    P.op("pe", proj(ps, lambda k, j=j, w=wZX: w[:, 0, k, j * 128:(j + 1) * 128], lambda k: hT[:, k, :], 8),
                         reads=HT + kZX, writes=[pk])
                    P.op("act", lambda e, ps=ps, j=j: e.activation(out=zs[:, j, :], in_=ps, func=AF.Silu),
                         reads=[pk], writes=[K("zs", j)])
                    yield
                P.op(PL, lambda e: e.tensor_copy(out=u4[:, :, 0:3], in_=car_ss[:, g, :, :]),
                     reads=[("car_ss", g)], writes=[K("u4_h")])
                for b in range(4):
                    ps, pk = mmbank()
                    if b < 2:
                        lhs = (lambda k, b=b, w=wZX: w[:, 1, k, b * 128:(b + 1) * 128])
                        rk = kZX
                    else:
                        lhs = (lambda k, b=b, go=go, w=wBC: w[:, b - 2, k, go:go + 128])
                        rk = kBC
                    P.op("pe", proj(ps, lhs, lambda k: hT[:, k, :], 8), reads=HT + rk, writes=[pk])
                    P.op("act", lambda e, ps=ps, b=b: e.activation(out=u4[:, b, 3:TB + 3], in_=ps, func=AF.Identity),
                         reads=[pk], writes=[K("u4", b)])
                    yield
                P.op(PL, lambda e: e.tensor_copy(out=car_ss[:, g, :, :], in_=u4[:, :, TB:TB + 3]),
                     reads=[K("u4", b) for b in range(4)], writes=[("car_ss", g)])
                for b in range(4):
                    ch = (2 * g + b) if b < 2 else (16 + g if b == 2 else 24 + g)
                    wv = V_SSW + ch * 4
                    bv = V_SSB + ch
                    P.op("dve", lambda e, b=b, wv=wv, bv=bv: e.tensor_scalar(
                        out=acc4, in0=u4[:, b, 0:TB], scalar1=vecs[:, wv:wv + 1], scalar2=vecs[:, bv:bv + 1],
                        op0=ALU.mult, op1=ALU.add),
                        reads=[K("u4", b), K("u4_h"), "vecs"], writes=[K("acc4")])
                    for tap in (1, 2, 3):
                        P.op("dve", lambda e, b=b, wv=wv, tap=tap: e.scalar_tensor_tensor(
                            out=acc4, in0=u4[:, b, tap:tap + TB], scalar=vecs[:, wv + tap:wv + tap + 1], in1=acc4,
                            op0=ALU.mult, op1=ALU.add),
                            reads=[K("u4", b), K("u4_h"), K("acc4"), "vecs"], writes=[K("acc4")])
                    if b < 2:
                        P.op("act", lambda e, b=b: e.activation(out=xsT[:, b, :], in_=acc4, func=AF.Silu),
                             reads=[K("acc4")], writes=[K("xsT", b)])
                    elif b == 2:
                        P.op("act", lambda e: e.activation(out=BT, in_=acc4, func=AF.Silu), reads=[K("acc4")], writes=[K("BT")])
                    else:
                        P.op("act", lambda e: e.activation(out=CT, in_=acc4, func=AF.Silu), reads=[K("acc4")], writes=[K("CT")])
                    yield
                if tb == 0 and g == 0:
                    dbg("xsT", xsT, [128, 2, TB], [K("xsT", 0), K("xsT", 1)])
                    dbg("BT", BT, [128, TB], [K("BT")])
                    dbg("CT", CT, [128, TB], [K("CT")])
                    dbg("zs", zs, [128, 2, TB], [K("zs", 0), K("zs", 1)])
                par = si
                R, LT, ecs, CBm, WT, CsT, xdt, xdtd, Btok = (R_[par], LT_[par], ecs_[par], CBm_[par], WT_[par],
                                                             CsT_[par], xdt_[par], xdtd_[par], Btok_[par])
                for cc in range(4):
                    tc_ = slice(cc * 128, (cc + 1) * 128)

                    def rbuild(e, cc=cc):
                        last = None
                        for h in range(4):
                            last = e.tensor_scalar(out=R[:, h, :], in0=tri, scalar1=TDA[:, cc, 4 * g + h:4 * g + h + 1],
                                                   scalar2=None, op0=ALU.mult)
                        return last
                    P.op(PL, rbuild, reads=["TDA", "cst"], writes=[("R", par)])

                    def pe1(e, tc_=tc_):
                        e.transpose(S2[:, 0:128], xsT[:, 0, tc_], ident)
                        e.transpose(S2[:, 128:256], xsT[:, 1, tc_], ident)
                        e.matmul(S2[:, 256:384], BT[:, tc_], CT[:, tc_], start=True, stop=True)
                        return e.matmul(S2[:, 384:512], BT[:, tc_], ident_bf, start=True, stop=True)
                    P.op("pe", pe1, reads=[K("xsT", 0), K("xsT", 1), K("BT"), K("CT")] + CB, writes=[kS(2)])

                    def pe2(e):
                        rf = R[:].rearrange("p h i -> p (h i)")
                        e.matmul(S0[:], Umat, rf, start=True, stop=True)
                        return e.matmul(S1[:], ones, rf, start=True, stop=True)
                    P.op("pe", pe2, reads=[("R", par), "cst"], writes=[kS(0), kS(1)])
                    P.op("act", lambda e: e.activation(out=LT[:].rearrange("p h i -> p (h i)"), in_=S0[:], func=AF.Exp),
                         reads=[kS(0)], writes=[("LT", par)])
                    P.op("act", lambda e: e.activation(out=ecs[:].rearrange("p h i -> p (h i)"), in_=S1[:], func=AF.Exp),
                         reads=[kS(1)], writes=[("ecs", par)])
                    P.op("dve", lambda e: e.tensor_tensor(out=CBm[:], in0=S2[:, 256:384], in1=tri, op=ALU.mult),
                         reads=[kS(2), "cst"], writes=[("CBm", par)])
                    P.op("dve", lambda e, cc=cc: e.tensor_tensor(
                        out=xdt[:], in0=S2[:, 0:256].rearrange("p (h q) -> p h q", h=4),
                        in1=TDT[:, cc, 4 * g:4 * g + 4].unsqueeze(2).broadcast_to([128, 4, 64]), op=ALU.mult),
                        reads=[kS(2), "TDT"], writes=[("xdt", par)])
                    P.op("act", lambda e: e.activation(out=Btok[:], in_=S2[:, 384:512], func=AF.Identity),
                         reads=[kS(2)], writes=[("Btok", par)])
                    yield
                    P.op("dve", lambda e: e.tensor_tensor(
                        out=WT[:], in0=LT[:], in1=CBm[:].unsqueeze(1).broadcast_to([128, 4, 128]), op=ALU.mult),
                        reads=[("LT", par), ("CBm", par)], writes=[("WT", par)])
                    P.op(PL, lambda e, tc_=tc_: e.tensor_tensor(
                        out=CsT[:], in0=ecs[:], in1=CT[:, tc_].unsqueeze(1).broadcast_to([128, 4, 128]), op=ALU.mult),
                        reads=[("ecs", par), K("CT")], writes=[("CsT", par)])
                    P.op(PL, lambda e, cc=cc: e.tensor_tensor(
                        out=xdtd[:], in0=xdt[:],
                        in1=TDEC[:, cc, 4 * g:4 * g + 4].unsqueeze(2).broadcast_to([128, 4, 64]), op=ALU.mult),
                        reads=[("xdt", par), "TDEC"], writes=[("xdtd", par)])
                    yield

                    def pe3(e):
                        for h in range(4):
                            o = S3[64 * (h % 2):64 * (h % 2) + 64, (h // 2) * 128:(h // 2) * 128 + 128]
                            e.matmul(o, xdt[:, h, :], WT[:, h, :], start=True, stop=False)
                            e.matmul(o, stb[:, g, h * 64:(h + 1) * 64], CsT[:, h, :], start=False, stop=True)
                        return e.matmul(S3[:, 256:512], Btok[:], xdtd[:].rearrange("p h q -> p (h q)"), start=True, stop=True)
                    P.op("pe", pe3, reads=[("xdt", par), ("WT", par), ("CsT", par), ("Btok", par), ("xdtd", par), ("stb", g)],
                         writes=[kS(3)])

                    def yev(e, tc_=tc_):
                        last = None
                        for j in range(2):
                            dv = V_DCOL + 2 * g + j
                            last = e.scalar_tensor_tensor(out=yT[:, j, tc_], in0=xsT[:, j, tc_], scalar=vecs[:, dv:dv + 1],
                                                          in1=S3[:, j * 128:(j + 1) * 128], op0=ALU.mult, op1=ALU.add)
                        return last
                    P.op("dve", yev, reads=[kS(3), K("xsT", 0), K("xsT", 1), "vecs"], writes=[K("yT", cc)])
                    P.op(PL, lambda e, cc=cc: e.tensor_tensor(
                        out=stf[:, g, :].rearrange("p (h q) -> p h q", h=4),
                        in0=stf[:, g, :].rearrange("p (h q) -> p h q", h=4),
                        in1=TCD[:, cc, 4 * g:4 * g + 4].unsqueeze(2).broadcast_to([128, 4, 64]), op=ALU.mult),
                        reads=[("stf", g), "TCD"], writes=[("stf", g)])
                    P.op("dve", lambda e: e.tensor_tensor(out=stf[:, g, :], in0=stf[:, g, :], in1=S3[:, 256:512], op=ALU.add),
                         reads=[("stf", g), kS(3)], writes=[("stf", g)])
                    P.op("act", lambda e: e.activation(out=stb[:, g, :], in_=stf[:, g, :], func=AF.Identity),
                         reads=[("stf", g)], writes=[("stb", g)])
                    yield
                YT = [K("yT", cc) for cc in range(4)]
                if tb == 0 and g == 0:
                    dbg("yT", yT, [128, 2, TB], YT)
                P.op(PL, lambda e: e.tensor_tensor(out=yT, in0=yT, in1=zs, op=ALU.mult),
                     reads=YT + [K("zs", 0), K("zs", 1)], writes=YT)
                P.op("act", lambda e: e.activation(out=sq2, in_=yT, func=AF.Square), reads=YT, writes=[K("sq2")])
                ps, pk = mmbank()
                P.op("pe", proj(ps, lambda k: ones_bf, lambda k: sq2[:, k, :], 2), reads=[K("sq2")] + CB, writes=[pk])
                yield
                P.op("act", lambda e, ps=ps: e.activation(out=rt_[:], in_=ps, func=AF.Sqrt, scale=1.0 / 256.0, bias=EPS),
                     reads=[pk], writes=[B_["rtk"]])
                P.op("dve", lambda e: e.reciprocal(out=rstd_[:], in_=rt_[:]), reads=[B_["rtk"]], writes=[B_["rstdk"]])
                for j in range(2):
                    nv = V_SNW + 2 * g + j
                    P.op("dve", lambda e, j=j, nv=nv: e.scalar_tensor_tensor(
                        out=ybT[:, 2 * g + j, :], in0=yT[:, j, :], scalar=vecs[:, nv:nv + 1], in1=rstd_[:],
                        op0=ALU.mult, op1=ALU.mult),
                        reads=YT + [B_["rstdk"], "vecs"], writes=[("ybT", 2 * g + j)])
                yield

            for gp in range(4):
                g0 = 2 * gp
                wBC, kBC = wload(v_bc[:, :, :, 128 * g0:128 * g0 + 256], [128, 2, 8, 256], K_SSD)
                gens = []
                for si in range(2):
                    g = g0 + si
                    wZX, kZX = wload(v_zx[:, :, :, 256 * g:256 * g + 256], [128, 2, 8, 256], K_SSD)
                    gens.append(group_gen(g, si, wBC, kBC, wZX, kZX))
                alive = list(gens)
                while alive:
                    for gn in list(alive):
                        try:
                            next(gn)
                        except StopIteration:
                            alive.remove(gn)
            YB = [("ybT", k) for k in range(16)]
            if tb == 0:
                dbg("ybT", ybT[:], [128, 16, TB], YB)

            P.op("dve", lambda e: e.memset(dummy[:, 3:4], 0.0), writes=["G2", "dummy3"])
            v_g = s_in[:, C_G:DIN].rearrange("(k p) (s n) -> p s k n", p=128, s=2)
            v_bsc = s_bsc.rearrange("(k p) n -> p k n", p=128)
            v_bssm = s_bssm.rearrange("(k p) n -> p k n", p=128)
            for obp in range(4):
                o0 = obp * 256
                wG, kG = wload(v_g[:, :, :, o0:o0 + 256], [128, 2, 8, 256], K_G)
                wA, kA = wload(v_bsc[:, :, o0:o0 + 256], [128, 8, 256], K_BSC)
                wS, kS = wload(v_bssm[:, :, o0:o0 + 256], [128, 16, 256], K_BSSM)
                for o2 in range(2):
                    ob = obp * 2 + o2
                    os_ = slice(o2 * 128, (o2 + 1) * 128)
                    for s in range(2):
                        psg, pkg = mmbank()
                        P.op("pe", proj(psg, lambda k, s=s, os_=os_, w=wG: w[:, s, k, os_], lambda k: hT[:, k, :], 8),
                             reads=HT + kG, writes=[pkg])
                        psb, pkb = mmbank()
                        if s == 0:
                            P.op("pe", proj(psb, lambda k, os_=os_, w=wA: w[:, k, os_], lambda k: yaT[:, k, :], 8),
                                 reads=YA + kA, writes=[pkb])
                        else:
                            P.op("pe", proj(psb, lambda k, os_=os_, w=wS: w[:, k, os_], lambda k: ybT[:, k, :], 16),
                                 reads=YB + kS, writes=[pkb])
                        bg = V_BG + s * 8 + ob
                        P.op("act", lambda e, psg=psg, bg=bg: e.activation(out=gsb[:], in_=psg, func=AF.Sigmoid,
                                                                            bias=vecs[:, bg:bg + 1]),
                             reads=[pkg, "vecs"], writes=["gsb"])
                        msb = m1sb if s == 0 else m2sb
                        P.op("dve", lambda e, psb=psb, msb=msb: e.tensor_tensor(out=msb[:], in0=psb, in1=gsb[:], op=ALU.mult),
                             reads=[pkb, "gsb"], writes=["m1sb" if s == 0 else "m2sb"])
                    P.op(PL, lambda e, ob=ob: e.tensor_tensor(out=mgT[:, ob, :], in0=m1sb[:], in1=m2sb[:], op=ALU.add),
                         reads=["m1sb", "m2sb"], writes=[("mgT", ob)])
            MG = [("mgT", k) for k in range(8)]
            if tb == 0:
                dbg("mgT", mgT[:], [128, 8, TB], MG)

            v_out = s_out.rearrange("(k p) n -> p k n", p=128)
            for oh in range(2):
                wO, kO = wload(v_out[:, :, oh * 512:(oh + 1) * 512], [128, 8, 512], K_OUT)
                for o4 in range(4):
                    ob = oh * 4 + o4
                    ps, pk = mmbank()
                    P.op("pe", proj(ps, lambda k, o4=o4, wO=wO: wO[:, k, o4 * 128:(o4 + 1) * 128], lambda k: mgT[:, k, :], 8),
                         reads=MG + kO, writes=[pk])
                    P.op("dve", lambda e, ps=ps, ob=ob: e.tensor_tensor(out=XB[:, ob, :], in0=ps, in1=XB[:, ob, :], op=ALU.add),
                         reads=[pk, ("XB", ob)], writes=[("XB", ob)])
            if tb == 0:
                dbg("x1T", XB[:], [128, 8, TB], [("XB", k) for k in range(8)])

            P.op("dve", lambda e: e.memset(dummy[:, 2:3], 0.0), writes=["G2", "G3", "dummy2"])
            rmsnorm_to(h2T, V_NMLP, float(D), "h2T")
            H2 = [("h2T", k) for k in range(8)]
            P.op("dve", lambda e: e.memset(dummy[:, 1:2], 0.0), writes=["BIG1", "dummy1"])
            v_m1 = s_m1.rearrange("(k p) n -> p k n", p=128)
            for fb in range(8):
                wM, kM = wload(v_m1[:, :, fb * 512:(fb + 1) * 512], [128, 8, 512], K_M1)
                for f4 in range(4):
                    f = fb * 4 + f4
                    ps, pk = mmbank()
                    P.op("pe", proj(ps, lambda k, f4=f4, wM=wM: wM[:, k, f4 * 128:(f4 + 1) * 128], lambda k: h2T[:, k, :], 8),
                         reads=H2 + kM, writes=[pk])
                    P.op("act", lambda e, ps=ps: e.activation(out=rl[:], in_=ps, func=AF.Relu), reads=[pk], writes=["rl"])
                    P.op("act", lambda e, f=f: e.activation(out=aT[:, f, :], in_=rl[:], func=AF.Square),
                         reads=["rl"], writes=[("aT", f)])
            AT = [("aT", f) for f in range(32)]
            v_m2 = s_m2.rearrange("(k p) n -> p k n", p=128)
            for obp in range(4):
                o0 = obp * 256
                wh = []
                for fh in range(2):
                    wh.append(wload(v_m2[:, fh * 16:(fh + 1) * 16, o0:o0 + 256], [128, 16, 256], K_M2))
                for o2 in range(2):
                    ob = obp * 2 + o2
                    os_ = slice(o2 * 128, (o2 + 1) * 128)
                    ps, pk = mmbank()
                    P.op("pe", proj(ps, lambda k, os_=os_, wh=wh: wh[k // 16][0][:, k % 16, os_], lambda k: aT[:, k, :], 32),
                         reads=AT + wh[0][1] + wh[1][1], writes=[pk])
                    P.op("dve", lambda e, ps=ps, ob=ob: e.tensor_tensor(out=XB[:, ob, :], in0=ps, in1=XB[:, ob, :], op=ALU.add),
                         reads=[pk, ("XB", ob)], writes=[("XB", ob)])

            P.op("act", lambda e: e.activation(out=sq[:], in_=XB[:], func=AF.Square),
                 reads=[("XB", k) for k in range(8)], writes=["sq"])
            ps, pk = mmbank()
            P.op("pe", proj(ps, lambda k: ones_bf, lambda k: sq[:, k, :], 8), reads=["sq"] + CB, writes=[pk])
            P.op("act", lambda e, ps=ps: e.activation(out=rt[:], in_=ps, func=AF.Sqrt, scale=1.0 / D, bias=EPS),
                 reads=[pk], writes=["rt"])
            P.op("dve", lambda e: e.reciprocal(out=rstd[:], in_=rt[:]), reads=["rt"], writes=["rstd"])
            for k in range(8):
                P.op("dve", lambda e, k=k: e.scalar_tensor_tensor(
                    out=XB[:, k, :], in0=XB[:, k, :], scalar=vecs[:, V_NFIN + k:V_NFIN + k + 1], in1=rstd[:],
                    op0=ALU.mult, op1=ALU.mult),
                    reads=[("XB", k), "rstd", "vecs"], writes=[("XB", k)])
            P.dma("sp", "st_out", lambda e, t0=t0: e.dma_start(out=outT_v[:, :, t0:t0 + TB], in_=XB[:]),
                  reads=[("XB", k) for k in range(8)], writes=[("outT", tb)])

        fin_reads = [("outT", tb) for tb in range(ntb)] + [("dbg", n) for n in dbg_out]
        P.op("sp", lambda e: None, reads=fin_reads, writes=["fin"])

        block = st.enter_context(nc.Block())
        P.emit(nc, block, st)
    return nc, dbg_out


def _pack_inputs(inputs):
    f = lambda a: np.ascontiguousarray(np.asarray(a, dtype=np.float32))
    vecs = np.zeros((128, NV), np.float32)
    vecs[:, V_NMIX:V_NMIX + 8] = f(inputs["norm_mix"]).reshape(8, 128).T
    vecs[:, V_NMLP:V_NMLP + 8] = f(inputs["norm_mlp"]).reshape(8, 128).T
    vecs[:, V_NFIN:V_NFIN + 8] = f(inputs["norm_final"]).reshape(8, 128).T
    vecs[:, V_BG:V_BG + 16] = f(inputs["b_gate"]).reshape(16, 128).T
    vecs[:, V_SCW:V_SCW + 24] = f(inputs["sc_conv_w"]).reshape(3, 8, 128).transpose(2, 1, 0).reshape(128, 24)
    vecs[:, V_SSW:V_SSW + 128] = f(inputs["ssm_conv_w"]).reshape(4, 32, 128).transpose(2, 1, 0).reshape(128, 128)
    vecs[:, V_SSB:V_SSB + 32] = f(inputs["ssm_conv_b"]).reshape(32, 128).T
    vecs[:, V_SNW:V_SNW + 16] = f(inputs["ssm_norm_w"]).reshape(16, 128).T
    vecs[:, V_DCOL:V_DCOL + 16] = np.repeat(f(inputs["D_skip"]), 64).reshape(16, 128).T
    vecs[:, V_DTB:V_DTB + 32] = np.tile(f(inputs["dt_bias"])[None, :], (128, 1))
    vecs[:, V_ALOG:V_ALOG + 32] = np.tile(f(inputs["A_log"])[None, :], (128, 1))
    k = np.arange(128)
    consts = np.zeros((128, 512), np.float32)
    consts[:, 0:128] = np.eye(128, dtype=np.float32)
    consts[:, 128:256] = (k[:, None] <= k[None, :]).astype(np.float32)
    consts[:, 256:384] = (k[:, None] > k[None, :]).astype(np.float32)
    consts[:, 384:512] = 1.0
    x = f(inputs["x"])
    xT = np.ascontiguousarray(x.transpose(0, 2, 1))
    common = {
        "w_in": f(inputs["w_in"]), "w_bsc": f(inputs["w_branch_sc"]), "w_bssm": f(inputs["w_branch_ssm"]),
        "w_out": f(inputs["w_out"]), "w_m1": f(inputs["w_mlp1"]), "w_m2": f(inputs["w_mlp2"]),
        "vecs": vecs, "consts": consts,
    }
    return [dict(common, xT=xT[i]) for i in range(8)]


def kernel(**inputs):
    in_maps = _pack_inputs(inputs)
    nc, _ = build_nc()
    res = run_bass_kernel_spmd(nc, in_maps, core_ids=list(range(8)))
    out = np.stack([np.ascontiguousarray(res.results[i]["outT"].T) for i in range(8)], axis=0)
    return out.astype(np.float32)
```

```python
import numpy as np
import concourse.bass as bass
import concourse.mybir as mybir
from concourse.bass_utils import run_bass_kernel_spmd

F32 = mybir.dt.float32
BF16 = mybir.dt.bfloat16
AF = mybir.ActivationFunctionType
ALU = mybir.AluOpType

D = 1024
T = 2048
TB = 512
NTB = T // TB
DFF = 4096
DIN = 11296
EPS = 1e-6
C_SC = 0
C_Z = 3072
C_X = 5120
C_B = 7168
C_C = 8192
C_DT = 9216
C_G = 9248
NW = 4
SAME_ENG_SYNC = True
USE_POOL = False

V_NMIX, V_NMLP, V_NFIN, V_BG, V_SCW, V_SSW, V_SSB, V_SNW, V_DCOL, V_DTB, V_ALOG = (
    0, 8, 16, 24, 40, 64, 192, 224, 240, 256, 288)
NV = 320

ALIAS = {"sq": "G2", "mgT": "G2", "hT": "G3", "h2T": "G3", "zs": "BIG1", "u4": "BIG1", "u4_h": "BIG1", "dg": "BIG1", "xsT": "BIG1", "yT": "BIG1",
         "BT": "BIG1", "CT": "BIG1", "sq2": "BIG1", "aT": "BIG1"}


class Prog:
    def __init__(self):
        self.ops = []

    def _norm(self, keys):
        out = []
        for k in keys:
            out.append(k)
            base = k[0] if isinstance(k, tuple) else k
            if base in ALIAS:
                out.append(ALIAS[base])
        return tuple(dict.fromkeys(out))

    def _alias(self, keys):
        out = []
        for k in keys:
            base = k[0] if isinstance(k, tuple) else k
            if base in ALIAS:
                out.append(ALIAS[base])
        return out

    def op(self, eng, fn, reads=(), writes=()):
        rd = self._norm(tuple(reads) + tuple(self._alias(writes)))
        self.ops.append(dict(eng=eng, fn=fn, reads=rd, writes=tuple(writes), sem=None))

    def dma(self, queue, sem, fn, reads=(), writes=()):
        rd = self._norm(tuple(reads) + tuple(self._alias(writes)))
        self.ops.append(dict(eng=queue, fn=fn, reads=rd, writes=tuple(writes), sem=sem))

    def emit(self, nc, block, stack):
        ops = self.ops
        last_writer, readers = {}, {}
        for i, o in enumerate(ops):
            deps = set()
            for k in o["reads"]:
                if k in last_writer:
                    deps.add(last_writer[k])
            for k in o["writes"]:
                if k in last_writer:
                    deps.add(last_writer[k])
                deps.update(readers.get(k, ()))
            deps.discard(i)
            o["deps"] = deps
            for k in o["reads"]:
                readers.setdefault(k, []).append(i)
            for k in o["writes"]:
                last_writer[k] = i
                readers[k] = []
        for o in ops:
            o["signal"] = o["sem"] is not None
        for o in ops:
            for d in o["deps"]:
                p = ops[d]
                if p["sem"] is None:
                    if p["eng"] == o["eng"] and (p["eng"] == "pe" or not SAME_ENG_SYNC) and o["sem"] is None:
                        continue
                    p["signal"] = True
        cnt = {}
        semnames = set()
        LIM = 1000
        for o in ops:
            if o["sem"] is not None:
                s = o["sem"]
                o["semname"] = s
                cnt[s] = cnt.get(s, 0) + 16
                o["tick"] = cnt[s]
                semnames.add(s)
            elif o["signal"]:
                base = "eng_" + o["eng"]
                c = cnt.get(base, 0)
                cnt[base] = c + 1
                s = "%s_%d" % (base, c // LIM)
                o["semname"] = s
                o["tick"] = c % LIM + 1
                semnames.add(s)
            else:
                o["semname"] = None
        sems = {s: stack.enter_context(nc.semaphore(s)) for s in sorted(semnames)}
        self.maxcnt = cnt

        def run_engine(engname, e):
            known = {}
            for o in ops:
                if o["eng"] != engname:
                    continue
                need = {}
                for d in o["deps"]:
                    p = ops[d]
                    if p["sem"] is None and p["eng"] == engname and o["sem"] is None:
                        if engname == "pe" or not SAME_ENG_SYNC:
                            continue
                    if not p["signal"]:
                        continue
                    s = p["semname"]
                    need[s] = max(need.get(s, 0), p["tick"])
                for s, v in need.items():
                    if known.get(s, 0) < v:
                        e.wait_ge(sems[s], v)
                        known[s] = v
                inst = o["fn"](e)
                if o["signal"]:
                    assert inst is not None
                    inst.then_inc(sems[o["semname"]], 16 if o["sem"] is not None else 1)

        @block.tensor
        def _(e):
            run_engine("pe", e)

        @block.scalar
        def _(e):
            run_engine("act", e)

        @block.vector
        def _(e):
            run_engine("dve", e)

        @block.gpsimd
        def _(e):
            run_engine("pool", e)

        @block.sync
        def _(e):
            run_engine("sp", e)


def build_nc(debug=False, ntb=NTB):
    import contextlib
    nc = bass.Bass("TRN2", target_bir_lowering=False)
    P = Prog()
    dr = {}
    dr["xT"] = nc.dram_tensor("xT", [D, T], F32, kind="ExternalInput").ap()
    dr["w_in"] = nc.dram_tensor("w_in", [D, DIN], F32, kind="ExternalInput").ap()
    dr["w_bsc"] = nc.dram_tensor("w_bsc", [D, D], F32, kind="ExternalInput").ap()
    dr["w_bssm"] = nc.dram_tensor("w_bssm", [2 * D, D], F32, kind="ExternalInput").ap()
    dr["w_out"] = nc.dram_tensor("w_out", [D, D], F32, kind="ExternalInput").ap()
    dr["w_m1"] = nc.dram_tensor("w_m1", [D, DFF], F32, kind="ExternalInput").ap()
    dr["w_m2"] = nc.dram_tensor("w_m2", [DFF, D], F32, kind="ExternalInput").ap()
    dr["vecs"] = nc.dram_tensor("vecs", [128, NV], F32, kind="ExternalInput").ap()
    dr["consts"] = nc.dram_tensor("consts", [128, 512], F32, kind="ExternalInput").ap()
    outT = nc.dram_tensor("outT", [D, T], F32, kind="ExternalOutput").ap()
    s_in = nc.dram_tensor("s_in", [D, DIN], BF16, kind="Internal").ap()
    s_bsc = nc.dram_tensor("s_bsc", [D, D], BF16, kind="Internal").ap()
    s_bssm = nc.dram_tensor("s_bssm", [2 * D, D], BF16, kind="Internal").ap()
    s_out = nc.dram_tensor("s_out", [D, D], BF16, kind="Internal").ap()
    s_m1 = nc.dram_tensor("s_m1", [D, DFF], BF16, kind="Internal").ap()
    s_m2 = nc.dram_tensor("s_m2", [DFF, D], BF16, kind="Internal").ap()
    dbg_out = {}

    with contextlib.ExitStack() as st:
        def sb(name, shape, dt):
            return st.enter_context(nc.sbuf_tensor(name, shape, dt))

        vecs = sb("vecs_sb", [128, NV], F32)
        cst = sb("cst", [128, 512], F32)
        cbf = sb("cbf", [128, 256], BF16)
        Abc = sb("Abc", [128, 32], F32)
        XB = sb("XB", [128, 8, TB], F32)
        sq = sb("sq", [128, 8, TB], BF16)
        hT = sb("hT", [128, 8, TB], BF16)
        rt = sb("rt", [128, TB], F32)
        rstd = sb("rstd", [128, TB], F32)
        Bsb = sb("Bsb", [128, TB], F32)
        Csb = sb("Csb", [128, TB], F32)
        usc = sb("usc", [128, TB + 2], BF16)
        dgs = sb("dgs", [128, 3, 128], BF16)
        car_sc = sb("car_sc", [128, 8, 2], F32)
        car_ss = sb("car_ss", [128, 8, 4, 3], F32)
        yaT = sb("yaT", [128, 8, TB], BF16)
        tbl = sb("tbl", [128, 8, 128], F32)
        BIG1 = sb("BIG1", [128, 8192], F32)
        BIG2 = sb("BIG2", [128, 6152], F32)
        R_ = [sb(f"R{i}", [128, 4, 128], F32) for i in range(2)]
        LT_ = [sb(f"LT{i}", [128, 4, 128], F32) for i in range(2)]
        ecs_ = [sb(f"ecs{i}", [128, 4, 128], F32) for i in range(2)]
        CBm_ = [sb(f"CBm{i}", [128, 128], F32) for i in range(2)]
        WT_ = [sb(f"WT{i}", [128, 4, 128], BF16) for i in range(2)]
        CsT_ = [sb(f"CsT{i}", [128, 4, 128], BF16) for i in range(2)]
        xdt_ = [sb(f"xdt{i}", [128, 4, 64], BF16) for i in range(2)]
        xdtd_ = [sb(f"xdtd{i}", [128, 4, 64], BF16) for i in range(2)]
        Btok_ = [sb(f"Btok{i}", [128, 128], BF16) for i in range(2)]
        stf = sb("stf", [128, 8, 256], F32)
        stb = sb("stb", [128, 8, 256], BF16)
        ybT = sb("ybT", [128, 16, TB], BF16)
        mgT = sq
        h2T = hT
        gsb = sb("gsb", [128, TB], F32)
        m1sb = sb("m1sb", [128, TB], F32)
        m2sb = sb("m2sb", [128, TB], F32)
        rl = sb("rl", [128, TB], F32)
        wdt = sb("wdt", [128, 8, 32], BF16)
        dummy = sb("fence_t", [128, 4], F32)
        WP = [sb(f"WP{i}", [128, 4096], BF16) for i in range(NW)]
        mm = st.enter_context(nc.psum_tensor("mm", [128, 4, 512], F32))
        S0 = st.enter_context(nc.psum_tensor("S0", [128, 512], F32))
        S1 = st.enter_context(nc.psum_tensor("S1", [128, 512], F32))
        S2 = st.enter_context(nc.psum_tensor("S2", [128, 512], F32))
        S3 = st.enter_context(nc.psum_tensor("S3", [128, 512], F32))

        ident = cst[:, 0:128]
        tri = cst[:, 128:256]
        Umat = cst[:, 256:384]
        ones = cst[:, 384:512]
        ident_bf = cbf[:, 0:128]
        ones_bf = cbf[:, 128:256]
        GS = []
        for si_, big in enumerate((BIG1, BIG2)):
            GS.append(dict(
                zs=big[:, 0:1024].rearrange("p (j t) -> p j t", j=2),
                u4=big[:, 1024:1024 + 1030].bitcast(BF16).rearrange("p (b t) -> p b t", b=4),
                dg=big[:, 2056:3080].bitcast(BF16).rearrange("p (m c) -> p m c", m=16),
                xsT=big[:, 3080:4104].rearrange("p (j t) -> p j t", j=2),
                yT=big[:, 4104:5128].rearrange("p (j t) -> p j t", j=2),
                BT=big[:, 5128:5384].bitcast(BF16),
                CT=big[:, 5384:5640].bitcast(BF16),
                sq2=big[:, 5640:6152].bitcast(BF16).rearrange("p (j t) -> p j t", j=2),
                rt=(rt, gsb)[si_], rstd=(rstd, m1sb)[si_], rtk=("rt", "gsb")[si_], rstdk=("rstd", "m1sb")[si_]))
        PS = [(S0, S1, S2, S3), (S0, S1, S2, S3)]
        aT = BIG1[:, :].bitcast(BF16).rearrange("p (f t) -> p f t", f=32)

        T1, TE, TDT, TDA, TCS, T2, TDEC, TCD = [tbl[:, i, :].rearrange("p (c h) -> p c h", c=4) for i in range(8)]

        mmc = [0]

        def mmbank():
            i = mmc[0] % 4
            mmc[0] += 1
            return mm[:, i, :], ("mm", i)

        wpc = [0]

        def wload(src, shape, reads):
            i = wpc[0] % NW
            wpc[0] += 1
            n = int(np.prod(shape[1:]))
            dst = WP[i][:, 0:n]
            ka, kb = ("WP", i, "a"), ("WP", i, "b")
            if len(shape) == 3:
                dst = dst.rearrange("p (a b) -> p a b", a=shape[1])
                P.dma("sp", f"wp{i}a", lambda e, dst=dst, src=src: e.dma_start(out=dst, in_=src),
                      reads=reads, writes=[ka, kb])
            else:
                dst = dst.rearrange("p (s a b) -> p s a b", s=shape[1], a=shape[2])
                for s_, kk in ((0, ka), (1, kb)):
                    P.dma("sp", f"wp{i}" + "ab"[s_], lambda e, d=dst[:, s_], r=src[:, s_]: e.dma_start(out=d, in_=r),
                          reads=reads, writes=[kk])
            return dst, [ka, kb]

        def proj(ps, lhs, rhs, nk):
            def fn(e):
                last = None
                for k in range(nk):
                    last = e.matmul(ps, lhs(k), rhs(k), start=(k == 0), stop=(k == nk - 1))
                return last
            return fn

        def dbg(name, ap, shape, reads):
            if not debug:
                return
            d = nc.dram_tensor("dbg_" + name, list(shape), ap.dtype, kind="ExternalOutput").ap()
            dbg_out[name] = d
            P.dma("sp", "dbg_" + name, lambda e: e.dma_start(out=d, in_=ap), reads=reads, writes=[("dbg", name)])

        P.dma("sp", "ld_vecs", lambda e: e.dma_start(out=vecs[:], in_=dr["vecs"]), writes=["vecs"])
        P.dma("sp", "ld_cst", lambda e: e.dma_start(out=cst[:], in_=dr["consts"]), writes=["cst"])
        P.op("dve", lambda e: e.tensor_copy(out=cbf[:, 0:128], in_=ident), reads=["cst"], writes=["cbf0"])
        P.op("dve", lambda e: e.tensor_copy(out=cbf[:, 128:256], in_=ones), reads=["cst"], writes=["cbf1"])
        CB = ["cbf0", "cbf1", "cst"]
        P.op("act", lambda e: e.activation(out=Abc[:], in_=vecs[:, V_ALOG:V_ALOG + 32], func=AF.Exp),
             reads=["vecs"], writes=["Abc"])
        P.op("dve", lambda e: e.tensor_scalar(out=Abc[:], in0=Abc[:], scalar1=-1.0, scalar2=None, op0=ALU.mult),
             reads=["Abc"], writes=["Abc"])
        P.op("dve", lambda e: e.memset(stf[:], 0.0), writes=[("stf", g) for g in range(8)])
        P.op("dve", lambda e: e.memset(stb[:], 0.0), writes=[("stb", g) for g in range(8)])
        P.op("dve", lambda e: e.memset(car_sc[:], 0.0), writes=["car_sc"])
        P.op("dve", lambda e: e.memset(car_ss[:], 0.0), writes=["car_ss"])

        castc = [0]

        def cast(dst, src, key):
            i = castc[0]
            castc[0] += 1
            P.dma("pool", f"cast{i}",
                  lambda e: e.dma_start(out=dst, in_=src, max_dma_last_dim=4096),
                  writes=[key])

        K_SC = [("scr", "in_sc", r) for r in range(2)]
        for r in range(2):
            cast(s_in[r * 512:(r + 1) * 512, 0:3072], dr["w_in"][r * 512:(r + 1) * 512, 0:3072], K_SC[r])
        K_DT = [("scr", "in_dt")]
        cast(s_in[:, C_DT:C_DT + 32], dr["w_in"][:, C_DT:C_DT + 32], K_DT[0])
        K_SSD = [("scr", "in_ssd", r) for r in range(4)]
        for r in range(4):
            cast(s_in[r * 256:(r + 1) * 256, C_Z:C_DT], dr["w_in"][r * 256:(r + 1) * 256, C_Z:C_DT], K_SSD[r])
        K_G = [("scr", "in_g", r) for r in range(2)]
        for r in range(2):
            cast(s_in[r * 512:(r + 1) * 512, C_G:DIN], dr["w_in"][r * 512:(r + 1) * 512, C_G:DIN], K_G[r])
        K_BSC = [("scr", "bsc")]
        cast(s_bsc, dr["w_bsc"], K_BSC[0])
        K_BSSM = [("scr", "bssm", r) for r in range(2)]
        for r in range(2):
            cast(s_bssm[r * 1024:(r + 1) * 1024, :], dr["w_bssm"][r * 1024:(r + 1) * 1024, :], K_BSSM[r])
        K_OUT = [("scr", "out")]
        cast(s_out, dr["w_out"], K_OUT[0])
        K_M1 = [("scr", "m1", r) for r in range(4)]
        for r in range(4):
            cast(s_m1[r * 256:(r + 1) * 256, :], dr["w_m1"][r * 256:(r + 1) * 256, :], K_M1[r])
        K_M2 = [("scr", "m2", r) for r in range(4)]
        for r in range(4):
            cast(s_m2[r * 1024:(r + 1) * 1024, :], dr["w_m2"][r * 1024:(r + 1) * 1024, :], K_M2[r])

        v_in = s_in.rearrange("(k p) n -> p k n", p=128)
        P.dma("sp", "ld_wdt", lambda e: e.dma_start(out=wdt[:], in_=v_in[:, :, C_DT:C_DT + 32]),
              reads=K_DT, writes=["wdt"])

        xT_v = dr["xT"].rearrange("(k p) t -> p k t", p=128)
        outT_v = outT.rearrange("(k p) t -> p k t", p=128)

        def rmsnorm_to(dst_bf, vcol, ndiv, tag):
            P.op("act", lambda e: e.activation(out=sq[:], in_=XB[:], func=AF.Square),
                 reads=[("XB", k) for k in range(8)], writes=["sq"])
            ps, pk = mmbank()
            P.op("pe", proj(ps, lambda k: ones_bf, lambda k: sq[:, k, :], 8), reads=["sq"] + CB, writes=[pk])
            P.op("act", lambda e: e.activation(out=rt[:], in_=ps, func=AF.Sqrt, scale=1.0 / ndiv, bias=EPS),
                 reads=[pk], writes=["rt"])
            P.op("dve", lambda e: e.reciprocal(out=rstd[:], in_=rt[:]), reads=["rt"], writes=["rstd"])
            for k in range(8):
                P.op("dve", lambda e, k=k: e.scalar_tensor_tensor(
                    out=dst_bf[:, k, :], in0=XB[:, k, :], scalar=vecs[:, vcol + k:vcol + k + 1], in1=rstd[:],
                    op0=ALU.mult, op1=ALU.mult),
                    reads=[("XB", k), "rstd", "vecs"], writes=[(tag, k)])

        chunkc = [0]

        for tb in range(ntb):
            t0 = tb * TB
            PL = "pool" if (USE_POOL and tb > 0) else "dve"
            P.op("dve", lambda e: e.memset(dummy[:, 2:3], 0.0), writes=["G3", "dummy2"])
            P.dma("sp", "ld_x", lambda e, t0=t0: e.dma_start(out=XB[:], in_=xT_v[:, :, t0:t0 + TB]),
                  writes=[("XB", k) for k in range(8)])
            rmsnorm_to(hT, V_NMIX, float(D), "hT")
            HT = [("hT", k) for k in range(8)]
            if tb == 0:
                dbg("hT", hT[:], [128, 8, TB], HT)

            for half in range(2):
                c0 = half * 512
                wB, kB = wload(v_in[:, :, C_SC + c0:C_SC + c0 + 512], [128, 8, 512], K_SC)
                wC, kC = wload(v_in[:, :, C_SC + 1024 + c0:C_SC + 1024 + c0 + 512], [128, 8, 512], K_SC)
                wX, kX = wload(v_in[:, :, C_SC + 2048 + c0:C_SC + 2048 + c0 + 512], [128, 8, 512], K_SC)
                for c4 in range(4):
                    cb = half * 4 + c4
                    cs_ = slice(c4 * 128, (c4 + 1) * 128)
                    psB, pkB = mmbank()
                    P.op("pe", proj(psB, lambda k, w=wB, s=cs_: w[:, k, s], lambda k: hT[:, k, :], 8),
                         reads=HT + kB, writes=[pkB])
                    psC, pkC = mmbank()
                    P.op("pe", proj(psC, lambda k, w=wC, s=cs_: w[:, k, s], lambda k: hT[:, k, :], 8),
                         reads=HT + kC, writes=[pkC])
                    psX, pkX = mmbank()
                    P.op("pe", proj(psX, lambda k, w=wX, s=cs_: w[:, k, s], lambda k: hT[:, k, :], 8),
                         reads=HT + kX, writes=[pkX])
                    P.op("act", lambda e, ps=psB: e.activation(out=Bsb[:], in_=ps, func=AF.Identity),
                         reads=[pkB], writes=["Bsb"])
                    P.op("act", lambda e, ps=psC: e.activation(out=Csb[:], in_=ps, func=AF.Identity),
                         reads=[pkC], writes=["Csb"])
                    P.op("dve", lambda e, cb=cb: e.tensor_copy(out=usc[:, 0:2], in_=car_sc[:, cb, :]),
                         reads=["car_sc"], writes=["usc_h"])
                    P.op("dve", lambda e, ps=psX: e.tensor_tensor(out=usc[:, 2:TB + 2], in0=ps, in1=Csb[:], op=ALU.mult),
                         reads=[pkX, "Csb"], writes=["usc"])
                    P.op("dve", lambda e, cb=cb: e.tensor_copy(out=car_sc[:, cb, :], in_=usc[:, TB:TB + 2]),
                         reads=["usc"], writes=["car_sc"])
                    wv = V_SCW + cb * 3
                    P.op("act", lambda e, wv=wv: [e.activation(out=dgs[:, tap, :], in_=ident_bf, func=AF.Copy,
                                                               scale=vecs[:, wv + tap:wv + tap + 1]) for tap in range(3)][-1],
                         reads=["vecs"] + CB, writes=["dgs"])
                    psV, pkV = mmbank()

                    def scconv(e, psV=psV):
                        last = None
                        for tap in range(3):
                            last = e.matmul(psV, dgs[:, tap, :], usc[:, tap:tap + TB], start=(tap == 0), stop=(tap == 2))
                        return last
                    P.op("pe", scconv, reads=["dgs", "usc", "usc_h"], writes=[pkV])
                    P.op("dve", lambda e, cb=cb, psV=psV: e.tensor_tensor(out=yaT[:, cb, :], in0=psV, in1=Bsb[:], op=ALU.mult),
                         reads=[pkV, "Bsb"], writes=[("yaT", cb)])
            YA = [("yaT", k) for k in range(8)]
            if tb == 0:
                dbg("yaT", yaT[:], [128, 8, TB], YA)

            ps, pk = mmbank()

            def dtmm(e, ps=ps):
                last = None
                for cc in range(4):
                    for k in range(8):
                        last = e.matmul(ps[:, cc * 32:(cc + 1) * 32], hT[:, k, cc * 128:(cc + 1) * 128], wdt[:, k, :],
                                        start=(k == 0), stop=(k == 7))
                return last
            P.op("pe", dtmm, reads=HT + ["wdt"], writes=[pk])
            dtb_b = vecs[:, V_DTB:V_DTB + 32].unsqueeze(1).broadcast_to([128, 4, 32])
            A_b = Abc[:].unsqueeze(1).broadcast_to([128, 4, 32])
            P.op("dve", lambda e, ps=ps: e.tensor_tensor(out=T1, in0=ps[:, 0:128].rearrange("p (c h) -> p c h", c=4),
                                                         in1=dtb_b, op=ALU.add),
                 reads=[pk, "vecs"], writes=["T1"])
            P.op("act", lambda e: e.activation(out=TE, in_=T1, func=AF.Exp), reads=["T1"], writes=["TE"])
            P.op("act", lambda e: e.activation(out=TDT, in_=TE, func=AF.Ln, bias=1.0), reads=["TE"], writes=["TDT"])
            P.op("dve", lambda e: e.tensor_tensor(out=TDA, in0=TDT, in1=A_b, op=ALU.mult),
                 reads=["TDT", "Abc"], writes=["TDA"])
            ps2, pk2 = mmbank()

            def csmm(e, ps2=ps2):
                e.matmul(ps2[:, 0:128], tri, tbl[:, 3, :], start=True, stop=True)
                return e.matmul(ps2[:, 128:256], ones, tbl[:, 3, :], start=True, stop=True)
            P.op("pe", csmm, reads=["TDA", "cst"], writes=[pk2])
            P.op("act", lambda e, ps2=ps2: e.activation(out=tbl[:, 4, :], in_=ps2[:, 0:128], func=AF.Identity),
                 reads=[pk2], writes=["TCS"])
            P.op("dve", lambda e, ps2=ps2: e.tensor_tensor(out=tbl[:, 5, :], in0=ps2[:, 128:256], in1=tbl[:, 4, :],
                                                           op=ALU.subtract),
                 reads=[pk2, "TCS"], writes=["T2"])
            P.op("act", lambda e: e.activation(out=tbl[:, 6, :], in_=tbl[:, 5, :], func=AF.Exp),
                 reads=["T2"], writes=["TDEC"])
            P.op("act", lambda e, ps2=ps2: e.activation(out=tbl[:, 7, :], in_=ps2[:, 128:256], func=AF.Exp),
                 reads=[pk2], writes=["TCD"])
            if tb == 0:
                dbg("tbl", tbl[:], [128, 8, 128], ["T1", "TE", "TDT", "TDA", "TCS", "T2", "TDEC", "TCD"])

            P.op("dve", lambda e: e.memset(dummy[:, 0:1], 0.0), writes=["BIG1", "dummy0"])
            v_zx = s_in[:, C_Z:C_B].rearrange("(k p) (s n) -> p s k n", p=128, s=2)
            v_bc = s_in[:, C_B:C_DT].rearrange("(k p) (s n) -> p s k n", p=128, s=2)

            def group_gen(g, si, wBC, kBC, wZX, kZX):
                B_ = GS[si]
                zs, u4, dg, xsT, yT, BT, CT, sq2 = (B_["zs"], B_["u4"], B_["dg"], B_["xsT"], B_["yT"], B_["BT"],
                                                     B_["CT"], B_["sq2"])
                rt_, rstd_ = B_["rt"], B_["rstd"]
                S0, S1, S2, S3 = PS[si]
                kS = lambda n: "S%d" % n
                K = lambda n, *a: (n, si) + tuple(a)
                go = (g % 2) * 128
                for j in range(2):
                    ps, pk = mmbank()
                    P.op("pe", proj(ps, lambda k, j=j, w=wZX: w[:, 0, k, j * 128:(j + 1) * 128], lambda k: hT[:, k, :], 8),
                         reads=HT + kZX, writes=[pk])
                    P.op("act", lambda e, ps=ps, j=j: e.activation(out=zs[:, j, :], in_=ps, func=AF.Silu),
                         reads=[pk], writes=[K("zs", j)])
                    yield
                P.op(PL, lambda e: e.tensor_copy(out=u4[:, :, 0:3], in_=car_ss[:, g, :, :]),
                     reads=[("car_ss", g)], writes=[K("u4_h")])
                for b in range(4):
                    ps, pk = mmbank()
                    if b < 2:
                        lhs = (lambda k, b=b, w=wZX: w[:, 1, k, b * 128:(b + 1) * 128])
                        rk = kZX
                    else:
                        lhs = (lambda k, b=b, go=go, w=wBC: w[:, b - 2, k, go:go + 128])
                        rk = kBC
                    P.op("pe", proj(ps, lhs, lambda k: hT[:, k, :], 8), reads=HT + rk, writes=[pk])
                    P.op("act", lambda e, ps=ps, b=b: e.activation(out=u4[:, b, 3:TB + 3], in_=ps, func=AF.Identity),
                         reads=[pk], writes=[K("u4", b)])
                    yield
                P.op(PL, lambda e: e.tensor_copy(out=car_ss[:, g, :, :], in_=u4[:, :, TB:TB + 3]),
                     reads=[K("u4", b) for b in range(4)], writes=[("car_ss", g)])
                for b in range(4):
                    ch = (2 * g + b) if b < 2 else (16 + g if b == 2 else 24 + g)
                    wv = V_SSW + ch * 4
                    bv = V_SSB + ch
                    P.op("act", lambda e, b=b, wv=wv: [e.activation(out=dg[:, b * 4 + tap, :], in_=ident_bf, func=AF.Copy,
                                                                    scale=vecs[:, wv + tap:wv + tap + 1]) for tap in range(4)][-1],
                         reads=["vecs"] + CB, writes=[K("dg", b)])
                    ps, pk = mmbank()

                    def ssconv(e, ps=ps, b=b):
                        last = None
                        for tap in range(4):
                            last = e.matmul(ps, dg[:, b * 4 + tap, :], u4[:, b, tap:tap + TB], start=(tap == 0), stop=(tap == 3))
                        return last
                    P.op("pe", ssconv, reads=[K("dg", b), K("u4", b), K("u4_h")], writes=[pk])
                    if b < 2:
                        P.op("act", lambda e, b=b, ps=ps, bv=bv: e.activation(out=xsT[:, b, :], in_=ps, func=AF.Silu,
                                                                              bias=vecs[:, bv:bv + 1]),
                             reads=[pk, "vecs"], writes=[K("xsT", b)])
                    elif b == 2:
                        P.op("act", lambda e, ps=ps, bv=bv: e.activation(out=BT, in_=ps, func=AF.Silu, bias=vecs[:, bv:bv + 1]),
                             reads=[pk, "vecs"], writes=[K("BT")])
                    else:
                        P.op("act", lambda e, ps=ps, bv=bv: e.activation(out=CT, in_=ps, func=AF.Silu, bias=vecs[:, bv:bv + 1]),
                             reads=[pk, "vecs"], writes=[K("CT")])
                    yield
                if tb == 0 and g == 0:
                    dbg("xsT", xsT, [128, 2, TB], [K("xsT", 0), K("xsT", 1)])
                    dbg("BT", BT, [128, TB], [K("BT")])
                    dbg("CT", CT, [128, TB], [K("CT")])
                    dbg("zs", zs, [128, 2, TB], [K("zs", 0), K("zs", 1)])
                par = si
                R, LT, ecs, CBm, WT, CsT, xdt, xdtd, Btok = (R_[par], LT_[par], ecs_[par], CBm_[par], WT_[par],
                                                             CsT_[par], xdt_[par], xdtd_[par], Btok_[par])
                for cc in range(4):
                    tc_ = slice(cc * 128, (cc + 1) * 128)

                    def rbuild(e, cc=cc):
                        last = None
                        for h in range(4):
                            last = e.tensor_scalar(out=R[:, h, :], in0=tri, scalar1=TDA[:, cc, 4 * g + h:4 * g + h + 1],
                                                   scalar2=None, op0=ALU.mult)
                        return last
                    P.op(PL, rbuild, reads=["TDA", "cst"], writes=[("R", par)])

                    def pe1(e, tc_=tc_):
                        e.transpose(S2[:, 0:128], xsT[:, 0, tc_], ident)
                        e.transpose(S2[:, 128:256], xsT[:, 1, tc_], ident)
                        e.matmul(S2[:, 256:384], BT[:, tc_], CT[:, tc_], start=True, stop=True)
                        return e.matmul(S2[:, 384:512], BT[:, tc_], ident_bf, start=True, stop=True)
                    P.op("pe", pe1, reads=[K("xsT", 0), K("xsT", 1), K("BT"), K("CT")] + CB, writes=[kS(2)])

                    def pe2(e):
                        rf = R[:].rearrange("p h i -> p (h i)")
                        e.matmul(S0[:], Umat, rf, start=True, stop=True)
                        return e.matmul(S1[:], ones, rf, start=True, stop=True)
                    P.op("pe", pe2, reads=[("R", par), "cst"], writes=[kS(0), kS(1)])
                    P.op("act", lambda e: e.activation(out=LT[:].rearrange("p h i -> p (h i)"), in_=S0[:], func=AF.Exp),
                         reads=[kS(0)], writes=[("LT", par)])
                    P.op("act", lambda e: e.activation(out=ecs[:].rearrange("p h i -> p (h i)"), in_=S1[:], func=AF.Exp),
                         reads=[kS(1)], writes=[("ecs", par)])
                    P.op("dve", lambda e: e.tensor_tensor(out=CBm[:], in0=S2[:, 256:384], in1=tri, op=ALU.mult),
                         reads=[kS(2), "cst"], writes=[("CBm", par)])
                    P.op("dve", lambda e, cc=cc: e.tensor_tensor(
                        out=xdt[:], in0=S2[:, 0:256].rearrange("p (h q) -> p h q", h=4),
                        in1=TDT[:, cc, 4 * g:4 * g + 4].unsqueeze(2).broadcast_to([128, 4, 64]), op=ALU.mult),
                        reads=[kS(2), "TDT"], writes=[("xdt", par)])
                    P.op("act", lambda e: e.activation(out=Btok[:], in_=S2[:, 384:512], func=AF.Identity),
                         reads=[kS(2)], writes=[("Btok", par)])
                    yield
                    P.op("dve", lambda e: e.tensor_tensor(
                        out=WT[:], in0=LT[:], in1=CBm[:].unsqueeze(1).broadcast_to([128, 4, 128]), op=ALU.mult),
                        reads=[("LT", par), ("CBm", par)], writes=[("WT", par)])
                    P.op(PL, lambda e, tc_=tc_: e.tensor_tensor(
                        out=CsT[:], in0=ecs[:], in1=CT[:, tc_].unsqueeze(1).broadcast_to([128, 4, 128]), op=ALU.mult),
                        reads=[("ecs", par), K("CT")], writes=[("CsT", par)])
                    P.op(PL, lambda e, cc=cc: e.tensor_tensor(
                        out=xdtd[:], in0=xdt[:],
                        in1=TDEC[:, cc, 4 * g:4 * g + 4].unsqueeze(2).broadcast_to([128, 4, 64]), op=ALU.mult),
                        reads=[("xdt", par), "TDEC"], writes=[("xdtd", par)])
                    yield

                    def pe3(e):
                        for h in range(4):
                            o = S3[64 * (h % 2):64 * (h % 2) + 64, (h // 2) * 128:(h // 2) * 128 + 128]
                            e.matmul(o, xdt[:, h, :], WT[:, h, :], start=True, stop=False)
                            e.matmul(o, stb[:, g, h * 64:(h + 1) * 64], CsT[:, h, :], start=False, stop=True)
                        return e.matmul(S3[:, 256:512], Btok[:], xdtd[:].rearrange("p h q -> p (h q)"), start=True, stop=True)
                    P.op("pe", pe3, reads=[("xdt", par), ("WT", par), ("CsT", par), ("Btok", par), ("xdtd", par), ("stb", g)],
                         writes=[kS(3)])

                    def yev(e, tc_=tc_):
                        last = None
                        for j in range(2):
                            dv = V_DCOL + 2 * g + j
                            last = e.scalar_tensor_tensor(out=yT[:, j, tc_], in0=xsT[:, j, tc_], scalar=vecs[:, dv:dv + 1],
                                                          in1=S3[:, j * 128:(j + 1) * 128], op0=ALU.mult, op1=ALU.add)
                        return last
                    P.op("dve", yev, reads=[kS(3), K("xsT", 0), K("xsT", 1), "vecs"], writes=[K("yT", cc)])
                    P.op(PL, lambda e, cc=cc: e.tensor_tensor(
                        out=stf[:, g, :].rearrange("p (h q) -> p h q", h=4),
                        in0=stf[:, g, :].rearrange("p (h q) -> p h q", h=4),
                        in1=TCD[:, cc, 4 * g:4 * g + 4].unsqueeze(2).broadcast_to([128, 4, 64]), op=ALU.mult),
                        reads=[("stf", g), "TCD"], writes=[("stf", g)])
                    P.op("dve", lambda e: e.tensor_tensor(out=stf[:, g, :], in0=stf[:, g, :], in1=S3[:, 256:512], op=ALU.add),
                         reads=[("stf", g), kS(3)], writes=[("stf", g)])
                    P.op("act", lambda e: e.activation(out=stb[:, g, :], in_=stf[:, g, :], func=AF.Identity),
                         reads=[("stf", g)], writes=[("stb", g)])
                    yield
                YT = [K("yT", cc) for cc in range(4)]
                if tb == 0 and g == 0:
                    dbg("yT", yT, [128, 2, TB], YT)
                P.op(PL, lambda e: e.tensor_tensor(out=yT, in0=yT, in1=zs, op=ALU.mult),
                     reads=YT + [K("zs", 0), K("zs", 1)], writes=YT)
                P.op("act", lambda e: e.activation(out=sq2, in_=yT, func=AF.Square), reads=YT, writes=[K("sq2")])
                ps, pk = mmbank()
                P.op("pe", proj(ps, lambda k: ones_bf, lambda k: sq2[:, k, :], 2), reads=[K("sq2")] + CB, writes=[pk])
                yield
                P.op("act", lambda e, ps=ps: e.activation(out=rt_[:], in_=ps, func=AF.Sqrt, scale=1.0 / 256.0, bias=EPS),
                     reads=[pk], writes=[B_["rtk"]])
                P.op("dve", lambda e: e.reciprocal(out=rstd_[:], in_=rt_[:]), reads=[B_["rtk"]], writes=[B_["rstdk"]])
                for j in range(2):
                    nv = V_SNW + 2 * g + j
                    P.op("dve", lambda e, j=j, nv=nv: e.scalar_tensor_tensor(
                        out=ybT[:, 2 * g + j, :], in0=yT[:, j, :], scalar=vecs[:, nv:nv + 1], in1=rstd_[:],
                        op0=ALU.mult, op1=ALU.mult),
                        reads=YT + [B_["rstdk"], "vecs"], writes=[("ybT", 2 * g + j)])
                yield

            for gp in range(4):
                g0 = 2 * gp
                wBC, kBC = wload(v_bc[:, :, :, 128 * g0:128 * g0 + 256], [128, 2, 8, 256], K_SSD)
                gens = []
                for si in range(2):
                    g = g0 + si
                    wZX, kZX = wload(v_zx[:, :, :, 256 * g:256 * g + 256], [128, 2, 8, 256], K_SSD)
                    gens.append(group_gen(g, si, wBC, kBC, wZX, kZX))
                alive = list(gens)
                while alive:
                    for gn in list(alive):
                        try:
                            next(gn)
                        except StopIteration:
                            alive.remove(gn)
            YB = [("ybT", k) for k in range(16)]
            if tb == 0:
                dbg("ybT", ybT[:], [128, 16, TB], YB)

            P.op("dve", lambda e: e.memset(dummy[:, 3:4], 0.0), writes=["G2", "dummy3"])
            v_g = s_in[:, C_G:DIN].rearrange("(k p) (s n) -> p s k n", p=128, s=2)
            v_bsc = s_bsc.rearrange("(k p) n -> p k n", p=128)
            v_bssm = s_bssm.rearrange("(k p) n -> p k n", p=128)
            for obp in range(4):
                o0 = obp * 256
                wG, kG = wload(v_g[:, :, :, o0:o0 + 256], [128, 2, 8, 256], K_G)
                wA, kA = wload(v_bsc[:, :, o0:o0 + 256], [128, 8, 256], K_BSC)
                wS, kS = wload(v_bssm[:, :, o0:o0 + 256], [128, 16, 256], K_BSSM)
                for o2 in range(2):
                    ob = obp * 2 + o2
                    os_ = slice(o2 * 128, (o2 + 1) * 128)
                    for s in range(2):
                        psg, pkg = mmbank()
                        P.op("pe", proj(psg, lambda k, s=s, os_=os_, w=wG: w[:, s, k, os_], lambda k: hT[:, k, :], 8),
                             reads=HT + kG, writes=[pkg])
                        psb, pkb = mmbank()
                        if s == 0:
                            P.op("pe", proj(psb, lambda k, os_=os_, w=wA: w[:, k, os_], lambda k: yaT[:, k, :], 8),
                                 reads=YA + kA, writes=[pkb])
                        else:
                            P.op("pe", proj(psb, lambda k, os_=os_, w=wS: w[:, k, os_], lambda k: ybT[:, k, :], 16),
                                 reads=YB + kS, writes=[pkb])
                        bg = V_BG + s * 8 + ob
                        P.op("act", lambda e, psg=psg, bg=bg: e.activation(out=gsb[:], in_=psg, func=AF.Sigmoid,
                                                                            bias=vecs[:, bg:bg + 1]),
                             reads=[pkg, "vecs"], writes=["gsb"])
                        msb = m1sb if s == 0 else m2sb
                        P.op("dve", lambda e, psb=psb, msb=msb: e.tensor_tensor(out=msb[:], in0=psb, in1=gsb[:], op=ALU.mult),
                             reads=[pkb, "gsb"], writes=["m1sb" if s == 0 else "m2sb"])
                    P.op(PL, lambda e, ob=ob: e.tensor_tensor(out=mgT[:, ob, :], in0=m1sb[:], in1=m2sb[:], op=ALU.add),
                         reads=["m1sb", "m2sb"], writes=[("mgT", ob)])
            MG = [("mgT", k) for k in range(8)]
            if tb == 0:
                dbg("mgT", mgT[:], [128, 8, TB], MG)

            v_out = s_out.rearrange("(k p) n -> p k n", p=128)
            for oh in range(2):
                wO, kO = wload(v_out[:, :, oh * 512:(oh + 1) * 512], [128, 8, 512], K_OUT)
                for o4 in range(4):
                    ob = oh * 4 + o4
                    ps, pk = mmbank()
                    P.op("pe", proj(ps, lambda k, o4=o4, wO=wO: wO[:, k, o4 * 128:(o4 + 1) * 128], lambda k: mgT[:, k, :], 8),
                         reads=MG + kO, writes=[pk])
                    P.op("dve", lambda e, ps=ps, ob=ob: e.tensor_tensor(out=XB[:, ob, :], in0=ps, in1=XB[:, ob, :], op=ALU.add),
                         reads=[pk, ("XB", ob)], writes=[("XB", ob)])
            if tb == 0:
                dbg("x1T", XB[:], [128, 8, TB], [("XB", k) for k in range(8)])

            P.op("dve", lambda e: e.memset(dummy[:, 2:3], 0.0), writes=["G2", "G3", "dummy2"])
            rmsnorm_to(h2T, V_NMLP, float(D), "h2T")
            H2 = [("h2T", k) for k in range(8)]
            P.op("dve", lambda e: e.memset(dummy[:, 1:2], 0.0), writes=["BIG1", "dummy1"])
            v_m1 = s_m1.rearrange("(k p) n -> p k n", p=128)
            for fb in range(8):
                wM, kM = wload(v_m1[:, :, fb * 512:(fb + 1) * 512], [128, 8, 512], K_M1)
                for f4 in range(4):
                    f = fb * 4 + f4
                    ps, pk = mmbank()
                    P.op("pe", proj(ps, lambda k, f4=f4, wM=wM: wM[:, k, f4 * 128:(f4 + 1) * 128], lambda k: h2T[:, k, :], 8),
                         reads=H2 + kM, writes=[pk])
                    P.op("act", lambda e, ps=ps: e.activation(out=rl[:], in_=ps, func=AF.Relu), reads=[pk], writes=["rl"])
                    P.op("act", lambda e, f=f: e.activation(out=aT[:, f, :], in_=rl[:], func=AF.Square),
                         reads=["rl"], writes=[("aT", f)])
            AT = [("aT", f) for f in range(32)]
            v_m2 = s_m2.rearrange("(k p) n -> p k n", p=128)
            for obp in range(4):
                o0 = obp * 256
                wh = []
                for fh in range(2):
                    wh.append(wload(v_m2[:, fh * 16:(fh + 1) * 16, o0:o0 + 256], [128, 16, 256], K_M2))
                for o2 in range(2):
                    ob = obp * 2 + o2
                    os_ = slice(o2 * 128, (o2 + 1) * 128)
                    ps, pk = mmbank()
                    P.op("pe", proj(ps, lambda k, os_=os_, wh=wh: wh[k // 16][0][:, k % 16, os_], lambda k: aT[:, k, :], 32),
                         reads=AT + wh[0][1] + wh[1][1], writes=[pk])
                    P.op("dve", lambda e, ps=ps, ob=ob: e.tensor_tensor(out=XB[:, ob, :], in0=ps, in1=XB[:, ob, :], op=ALU.add),
                         reads=[pk, ("XB", ob)], writes=[("XB", ob)])

            P.op("act", lambda e: e.activation(out=sq[:], in_=XB[:], func=AF.Square),
                 reads=[("XB", k) for k in range(8)], writes=["sq"])
            ps, pk = mmbank()
            P.op("pe", proj(ps, lambda k: ones_bf, lambda k: sq[:, k, :], 8), reads=["sq"] + CB, writes=[pk])
            P.op("act", lambda e, ps=ps: e.activation(out=rt[:], in_=ps, func=AF.Sqrt, scale=1.0 / D, bias=EPS),
                 reads=[pk], writes=["rt"])
            P.op("dve", lambda e: e.reciprocal(out=rstd[:], in_=rt[:]), reads=["rt"], writes=["rstd"])
            for k in range(8):
                P.op("dve", lambda e, k=k: e.scalar_tensor_tensor(
                    out=XB[:, k, :], in0=XB[:, k, :], scalar=vecs[:, V_NFIN + k:V_NFIN + k + 1], in1=rstd[:],
                    op0=ALU.mult, op1=ALU.mult),
                    reads=[("XB", k), "rstd", "vecs"], writes=[("XB", k)])
            P.dma("sp", "st_out", lambda e, t0=t0: e.dma_start(out=outT_v[:, :, t0:t0 + TB], in_=XB[:]),
                  reads=[("XB", k) for k in range(8)], writes=[("outT", tb)])

        fin_reads = [("outT", tb) for tb in range(ntb)] + [("dbg", n) for n in dbg_out]
        P.op("sp", lambda e: None, reads=fin_reads, writes=["fin"])

        block = st.enter_context(nc.Block())
        P.emit(nc, block, st)
    return nc, dbg_out


def _pack_inputs(inputs):
    f = lambda a: np.ascontiguousarray(np.asarray(a, dtype=np.float32))
    vecs = np.zeros((128, NV), np.float32)
    vecs[:, V_NMIX:V_NMIX + 8] = f(inputs["norm_mix"]).reshape(8, 128).T
    vecs[:, V_NMLP:V_NMLP + 8] = f(inputs["norm_mlp"]).reshape(8, 128).T
    vecs[:, V_NFIN:V_NFIN + 8] = f(inputs["norm_final"]).reshape(8, 128).T
    vecs[:, V_BG:V_BG + 16] = f(inputs["b_gate"]).reshape(16, 128).T
    vecs[:, V_SCW:V_SCW + 24] = f(inputs["sc_conv_w"]).reshape(3, 8, 128).transpose(2, 1, 0).reshape(128, 24)
    vecs[:, V_SSW:V_SSW + 128] = f(inputs["ssm_conv_w"]).reshape(4, 32, 128).transpose(2, 1, 0).reshape(128, 128)
    vecs[:, V_SSB:V_SSB + 32] = f(inputs["ssm_conv_b"]).reshape(32, 128).T
    vecs[:, V_SNW:V_SNW + 16] = f(inputs["ssm_norm_w"]).reshape(16, 128).T
    vecs[:, V_DCOL:V_DCOL + 16] = np.repeat(f(inputs["D_skip"]), 64).reshape(16, 128).T
    vecs[:, V_DTB:V_DTB + 32] = np.tile(f(inputs["dt_bias"])[None, :], (128, 1))
    vecs[:, V_ALOG:V_ALOG + 32] = np.tile(f(inputs["A_log"])[None, :], (128, 1))
    k = np.arange(128)
    consts = np.zeros((128, 512), np.float32)
    consts[:, 0:128] = np.eye(128, dtype=np.float32)
    consts[:, 128:256] = (k[:, None] <= k[None, :]).astype(np.float32)
    consts[:, 256:384] = (k[:, None] > k[None, :]).astype(np.float32)
    consts[:, 384:512] = 1.0
    x = f(inputs["x"])
    xT = np.ascontiguousarray(x.transpose(0, 2, 1))
    common = {
        "w_in": f(inputs["w_in"]), "w_bsc": f(inputs["w_branch_sc"]), "w_bssm": f(inputs["w_branch_ssm"]),
        "w_out": f(inputs["w_out"]), "w_m1": f(inputs["w_mlp1"]), "w_m2": f(inputs["w_mlp2"]),
        "vecs": vecs, "consts": consts,
    }
    return [dict(common, xT=xT[i]) for i in range(8)]


def kernel(**inputs):
    in_maps = _pack_inputs(inputs)
    nc, _ = build_nc()
    res = run_bass_kernel_spmd(nc, in_maps, core_ids=list(range(8)))
    out = np.stack([np.ascontiguousarray(res.results[i]["outT"].T) for i in range(8)], axis=0)
    return out.astype(np.float32)
```

```python
import numpy as np
import concourse.bass as bass
import concourse.mybir as mybir
from concourse.bass_utils import run_bass_kernel_spmd

F32 = mybir.dt.float32
BF16 = mybir.dt.bfloat16
AF = mybir.ActivationFunctionType
ALU = mybir.AluOpType

D = 1024
T = 2048
TB = 512
NTB = T // TB
DFF = 4096
DIN = 11296
EPS = 1e-6
C_SC = 0
C_Z = 3072
C_X = 5120
C_B = 7168
C_C = 8192
C_DT = 9216
C_G = 9248
NW = 4
SAME_ENG_SYNC = True
USE_POOL = False

V_NMIX, V_NMLP, V_NFIN, V_BG, V_SCW, V_SSW, V_SSB, V_SNW, V_DCOL, V_DTB, V_ALOG = (
    0, 8, 16, 24, 40, 64, 192, 224, 240, 256, 288)
NV = 320

ALIAS = {"sq": "G2", "mgT": "G2", "hT": "G3", "h2T": "G3", "zs": "BIG1", "u4": "BIG1", "u4_h": "BIG1", "dg": "BIG1", "xsT": "BIG1", "yT": "BIG1",
         "BT": "BIG1", "CT": "BIG1", "sq2": "BIG1", "aT": "BIG1"}


class Prog:
    def __init__(self):
        self.ops = []

    def _norm(self, keys):
        out = []
        for k in keys:
            out.append(k)
            base = k[0] if isinstance(k, tuple) else k
            if base in ALIAS:
                out.append(ALIAS[base])
        return tuple(dict.fromkeys(out))

    def _alias(self, keys):
        out = []
        for k in keys:
            base = k[0] if isinstance(k, tuple) else k
            if base in ALIAS:
                out.append(ALIAS[base])
        return out

    def op(self, eng, fn, reads=(), writes=()):
        rd = self._norm(tuple(reads) + tuple(self._alias(writes)))
        self.ops.append(dict(eng=eng, fn=fn, reads=rd, writes=tuple(writes), sem=None))

    def dma(self, queue, sem, fn, reads=(), writes=()):
        rd = self._norm(tuple(reads) + tuple(self._alias(writes)))
        self.ops.append(dict(eng=queue, fn=fn, reads=rd, writes=tuple(writes), sem=sem))

    def emit(self, nc, block, stack):
        ops = self.ops
        last_writer, readers = {}, {}
        for i, o in enumerate(ops):
            deps = set()
            for k in o["reads"]:
                if k in last_writer:
                    deps.add(last_writer[k])
            for k in o["writes"]:
                if k in last_writer:
                    deps.add(last_writer[k])
                deps.update(readers.get(k, ()))
            deps.discard(i)
            o["deps"] = deps
            for k in o["reads"]:
                readers.setdefault(k, []).append(i)
            for k in o["writes"]:
                last_writer[k] = i
                readers[k] = []
        for o in ops:
            o["signal"] = o["sem"] is not None
        for o in ops:
            for d in o["deps"]:
                p = ops[d]
                if p["sem"] is None:
                    if p["eng"] == o["eng"] and (p["eng"] == "pe" or not SAME_ENG_SYNC) and o["sem"] is None:
                        continue
                    p["signal"] = True
        cnt = {}
        semnames = set()
        LIM = 1000
        for o in ops:
            if o["sem"] is not None:
                s = o["sem"]
                o["semname"] = s
                cnt[s] = cnt.get(s, 0) + 16
                o["tick"] = cnt[s]
                semnames.add(s)
            elif o["signal"]:
                base = "eng_" + o["eng"]
                c = cnt.get(base, 0)
                cnt[base] = c + 1
                s = "%s_%d" % (base, c // LIM)
                o["semname"] = s
                o["tick"] = c % LIM + 1
                semnames.add(s)
            else:
                o["semname"] = None
        sems = {s: stack.enter_context(nc.semaphore(s)) for s in sorted(semnames)}
        self.maxcnt = cnt

        def run_engine(engname, e):
            known = {}
            for o in ops:
                if o["eng"] != engname:
                    continue
                need = {}
                for d in o["deps"]:
                    p = ops[d]
                    if p["sem"] is None and p["eng"] == engname and o["sem"] is None:
                        if engname == "pe" or not SAME_ENG_SYNC:
                            continue
                    if not p["signal"]:
                        continue
                    s = p["semname"]
                    need[s] = max(need.get(s, 0), p["tick"])
                for s, v in need.items():
                    if known.get(s, 0) < v:
                        e.wait_ge(sems[s], v)
                        known[s] = v
                inst = o["fn"](e)
                if o["signal"]:
                    assert inst is not None
                    inst.then_inc(sems[o["semname"]], 16 if o["sem"] is not None else 1)

        @block.tensor
        def _(e):
            run_engine("pe", e)

        @block.scalar
        def _(e):
            run_engine("act", e)

        @block.vector
        def _(e):
            run_engine("dve", e)

        @block.gpsimd
        def _(e):
            run_engine("pool", e)

        @block.sync
        def _(e):
            run_engine("sp", e)


def build_nc(debug=False, ntb=NTB):
    import contextlib
    nc = bass.Bass("TRN2", target_bir_lowering=False)
    P = Prog()
    dr = {}
    dr["xT"] = nc.dram_tensor("xT", [D, T], F32, kind="ExternalInput").ap()
    dr["w_in"] = nc.dram_tensor("w_in", [D, DIN], F32, kind="ExternalInput").ap()
    dr["w_bsc"] = nc.dram_tensor("w_bsc", [D, D], F32, kind="ExternalInput").ap()
    dr["w_bssm"] = nc.dram_tensor("w_bssm", [2 * D, D], F32, kind="ExternalInput").ap()
    dr["w_out"] = nc.dram_tensor("w_out", [D, D], F32, kind="ExternalInput").ap()
    dr["w_m1"] = nc.dram_tensor("w_m1", [D, DFF], F32, kind="ExternalInput").ap()
    dr["w_m2"] = nc.dram_tensor("w_m2", [DFF, D], F32, kind="ExternalInput").ap()
    dr["vecs"] = nc.dram_tensor("vecs", [128, NV], F32, kind="ExternalInput").ap()
    dr["consts"] = nc.dram_tensor("consts", [128, 512], F32, kind="ExternalInput").ap()
    outT = nc.dram_tensor("outT", [D, T], F32, kind="ExternalOutput").ap()
    s_in = nc.dram_tensor("s_in", [D, DIN], BF16, kind="Internal").ap()
    s_bsc = nc.dram_tensor("s_bsc", [D, D], BF16, kind="Internal").ap()
    s_bssm = nc.dram_tensor("s_bssm", [2 * D, D], BF16, kind="Internal").ap()
    s_out = nc.dram_tensor("s_out", [D, D], BF16, kind="Internal").ap()
    s_m1 = nc.dram_tensor("s_m1", [D, DFF], BF16, kind="Internal").ap()
    s_m2 = nc.dram_tensor("s_m2", [DFF, D], BF16, kind="Internal").ap()
    dbg_out = {}

    with contextlib.ExitStack() as st:
        def sb(name, shape, dt):
            return st.enter_context(nc.sbuf_tensor(name, shape, dt))

        vecs = sb("vecs_sb", [128, NV], F32)
        cst = sb("cst", [128, 512], F32)
        cbf = sb("cbf", [128, 256], BF16)
        Abc = sb("Abc", [128, 32], F32)
        XB = sb("XB", [128, 8, TB], F32)
        sq = sb("sq", [128, 8, TB], BF16)
        hT = sb("hT", [128, 8, TB], BF16)
        rt = sb("rt", [128, TB], F32)
        rstd = sb("rstd", [128, TB], F32)
        Bsb = sb("Bsb", [128, TB], F32)
        Csb = sb("Csb", [128, TB], F32)
        usc = sb("usc", [128, TB + 2], BF16)
        dgs = sb("dgs", [128, 3, 128], BF16)
        car_sc = sb("car_sc", [128, 8, 2], F32)
        car_ss = sb("car_ss", [128, 8, 4, 3], F32)
        yaT = sb("yaT", [128, 8, TB], BF16)
        tbl = sb("tbl", [128, 8, 128], F32)
        BIG1 = sb("BIG1", [128, 8192], F32)
        BIG2 = sb("BIG2", [128, 6152], F32)
        R_ = [sb(f"R{i}", [128, 4, 128], F32) for i in range(2)]
        LT_ = [sb(f"LT{i}", [128, 4, 128], F32) for i in range(2)]
        ecs_ = [sb(f"ecs{i}", [128, 4, 128], F32) for i in range(2)]
        CBm_ = [sb(f"CBm{i}", [128, 128], F32) for i in range(2)]
        WT_ = [sb(f"WT{i}", [128, 4, 128], BF16) for i in range(2)]
        CsT_ = [sb(f"CsT{i}", [128, 4, 128], BF16) for i in range(2)]
        xdt_ = [sb(f"xdt{i}", [128, 4, 64], BF16) for i in range(2)]
        xdtd_ = [sb(f"xdtd{i}", [128, 4, 64], BF16) for i in range(2)]
        Btok_ = [sb(f"Btok{i}", [128, 128], BF16) for i in range(2)]
        stf = sb("stf", [128, 8, 256], F32)
        stb = sb("stb", [128, 8, 256], BF16)
        ybT = sb("ybT", [128, 16, TB], BF16)
        mgT = sq
        h2T = hT
        gsb = sb("gsb", [128, TB], F32)
        m1sb = sb("m1sb", [128, TB], F32)
        m2sb = sb("m2sb", [128, TB], F32)
        rl = sb("rl", [128, TB], F32)
        wdt = sb("wdt", [128, 8, 32], BF16)
        dummy = sb("fence_t", [128, 4], F32)
        WP = [sb(f"WP{i}", [128, 4096], BF16) for i in range(NW)]
        mm = st.enter_context(nc.psum_tensor("mm", [128, 4, 512], F32))
        S0 = st.enter_context(nc.psum_tensor("S0", [128, 512], F32))
        S1 = st.enter_context(nc.psum_tensor("S1", [128, 512], F32))
        S2 = st.enter_context(nc.psum_tensor("S2", [128, 512], F32))
        S3 = st.enter_context(nc.psum_tensor("S3", [128, 512], F32))

        ident = cst[:, 0:128]
        tri = cst[:, 128:256]
        Umat = cst[:, 256:384]
        ones = cst[:, 384:512]
        ident_bf = cbf[:, 0:128]
        ones_bf = cbf[:, 128:256]
        GS = []
        for si_, big in enumerate((BIG1, BIG2)):
            GS.append(dict(
                zs=big[:, 0:1024].rearrange("p (j t) -> p j t", j=2),
                u4=big[:, 1024:1024 + 1030].bitcast(BF16).rearrange("p (b t) -> p b t", b=4),
                dg=big[:, 2056:3080].bitcast(BF16).rearrange("p (m c) -> p m c", m=16),
                xsT=big[:, 3080:4104].rearrange("p (j t) -> p j t", j=2),
                yT=big[:, 4104:5128].rearrange("p (j t) -> p j t", j=2),
                BT=big[:, 5128:5384].bitcast(BF16),
                CT=big[:, 5384:5640].bitcast(BF16),
                sq2=big[:, 5640:6152].bitcast(BF16).rearrange("p (j t) -> p j t", j=2),
                rt=(rt, gsb)[si_], rstd=(rstd, m1sb)[si_], rtk=("rt", "gsb")[si_], rstdk=("rstd", "m1sb")[si_]))
        PS = [(S0, S1, S2, S3), (S0, S1, S2, S3)]
        aT = BIG1[:, :].bitcast(BF16).rearrange("p (f t) -> p f t", f=32)

        T1, TE, TDT, TDA, TCS, T2, TDEC, TCD = [tbl[:, i, :].rearrange("p (c h) -> p c h", c=4) for i in range(8)]

        mmc = [0]

        def mmbank():
            i = mmc[0] % 4
            mmc[0] += 1
            return mm[:, i, :], ("mm", i)

        wpc = [0]

        def wload(src, shape, reads):
            i = wpc[0] % NW
            wpc[0] += 1
            n = int(np.prod(shape[1:]))
            dst = WP[i][:, 0:n]
            ka, kb = ("WP", i, "a"), ("WP", i, "b")
            if len(shape) == 3:
                dst = dst.rearrange("p (a b) -> p a b", a=shape[1])
                P.dma("sp", f"wp{i}a", lambda e, dst=dst, src=src: e.dma_start(out=dst, in_=src),
                      reads=reads, writes=[ka, kb])
            else:
                dst = dst.rearrange("p (s a b) -> p s a b", s=shape[1], a=shape[2])
                for s_, kk in ((0, ka), (1, kb)):
                    P.dma("sp", f"wp{i}" + "ab"[s_], lambda e, d=dst[:, s_], r=src[:, s_]: e.dma_start(out=d, in_=r),
                          reads=reads, writes=[kk])
            return dst, [ka, kb]

        def proj(ps, lhs, rhs, nk):
            def fn(e):
                last = None
                for k in range(nk):
                    last = e.matmul(ps, lhs(k), rhs(k), start=(k == 0), stop=(k == nk - 1))
                return last
            return fn

        def dbg(name, ap, shape, reads):
            if not debug:
                return
            d = nc.dram_tensor("dbg_" + name, list(shape), ap.dtype, kind="ExternalOutput").ap()
            dbg_out[name] = d
            P.dma("sp", "dbg_" + name, lambda e: e.dma_start(out=d, in_=ap), reads=reads, writes=[("dbg", name)])

        P.dma("sp", "ld_vecs", lambda e: e.dma_start(out=vecs[:], in_=dr["vecs"]), writes=["vecs"])
        P.dma("sp", "ld_cst", lambda e: e.dma_start(out=cst[:], in_=dr["consts"]), writes=["cst"])
        P.op("dve", lambda e: e.tensor_copy(out=cbf[:, 0:128], in_=ident), reads=["cst"], writes=["cbf0"])
        P.op("dve", lambda e: e.tensor_copy(out=cbf[:, 128:256], in_=ones), reads=["cst"], writes=["cbf1"])
        CB = ["cbf0", "cbf1", "cst"]
        P.op("act", lambda e: e.activation(out=Abc[:], in_=vecs[:, V_ALOG:V_ALOG + 32], func=AF.Exp),
             reads=["vecs"], writes=["Abc"])
        P.op("dve", lambda e: e.tensor_scalar(out=Abc[:], in0=Abc[:], scalar1=-1.0, scalar2=None, op0=ALU.mult),
             reads=["Abc"], writes=["Abc"])
        P.op("dve", lambda e: e.memset(stf[:], 0.0), writes=[("stf", g) for g in range(8)])
        P.op("dve", lambda e: e.memset(stb[:], 0.0), writes=[("stb", g) for g in range(8)])
        P.op("dve", lambda e: e.memset(car_sc[:], 0.0), writes=["car_sc"])
        P.op("dve", lambda e: e.memset(car_ss[:], 0.0), writes=["car_ss"])

        castc = [0]

        def cast(dst, src, key):
            i = castc[0]
            castc[0] += 1
            P.dma("pool", f"cast{i}",
                  lambda e: e.dma_start(out=dst, in_=src, max_dma_last_dim=4096),
                  writes=[key])

        K_SC = [("scr", "in_sc", r) for r in range(2)]
        for r in range(2):
            cast(s_in[r * 512:(r + 1) * 512, 0:3072], dr["w_in"][r * 512:(r + 1) * 512, 0:3072], K_SC[r])
        K_DT = [("scr", "in_dt")]
        cast(s_in[:, C_DT:C_DT + 32], dr["w_in"][:, C_DT:C_DT + 32], K_DT[0])
        K_SSD = [("scr", "in_ssd", r) for r in range(4)]
        for r in range(4):
            cast(s_in[r * 256:(r + 1) * 256, C_Z:C_DT], dr["w_in"][r * 256:(r + 1) * 256, C_Z:C_DT], K_SSD[r])
        K_G = [("scr", "in_g", r) for r in range(2)]
        for r in range(2):
            cast(s_in[r * 512:(r + 1) * 512, C_G:DIN], dr["w_in"][r * 512:(r + 1) * 512, C_G:DIN], K_G[r])
        K_BSC = [("scr", "bsc")]
        cast(s_bsc, dr["w_bsc"], K_BSC[0])
        K_BSSM = [("scr", "bssm", r) for r in range(2)]
        for r in range(2):
            cast(s_bssm[r * 1024:(r + 1) * 1024, :], dr["w_bssm"][r * 1024:(r + 1) * 1024, :], K_BSSM[r])
        K_OUT = [("scr", "out")]
        cast(s_out, dr["w_out"], K_OUT[0])
        K_M1 = [("scr", "m1", r) for r in range(4)]
        for r in range(4):
            cast(s_m1[r * 256:(r + 1) * 256, :], dr["w_m1"][r * 256:(r + 1) * 256, :], K_M1[r])
        K_M2 = [("scr", "m2", r) for r in range(4)]
        for r in range(4):
            cast(s_m2[r * 1024:(r + 1) * 1024, :], dr["w_m2"][r * 1024:(r + 1) * 1024, :], K_M2[r])

        v_in = s_in.rearrange("(k p) n -> p k n", p=128)
        P.dma("sp", "ld_wdt", lambda e: e.dma_start(out=wdt[:], in_=v_in[:, :, C_DT:C_DT + 32]),
              reads=K_DT, writes=["wdt"])

        xT_v = dr["xT"].rearrange("(k p) t -> p k t", p=128)
        outT_v = outT.rearrange("(k p) t -> p k t", p=128)

        def rmsnorm_to(dst_bf, vcol, ndiv, tag):
            P.op("act", lambda e: e.activation(out=sq[:], in_=XB[:], func=AF.Square),
                 reads=[("XB", k) for k in range(8)], writes=["sq"])
            ps, pk = mmbank()
            P.op("pe", proj(ps, lambda k: ones_bf, lambda k: sq[:, k, :], 8), reads=["sq"] + CB, writes=[pk])
            P.op("act", lambda e: e.activation(out=rt[:], in_=ps, func=AF.Ln, scale=1.0 / ndiv, bias=EPS),
                 reads=[pk], writes=["rt"])
            P.op("act", lambda e: e.activation(out=rstd[:], in_=rt[:], func=AF.Exp, scale=-0.5), reads=["rt"], writes=["rstd"])
            for k in range(8):
                P.op("dve", lambda e, k=k: e.scalar_tensor_tensor(
                    out=dst_bf[:, k, :], in0=XB[:, k, :], scalar=vecs[:, vcol + k:vcol + k + 1], in1=rstd[:],
                    op0=ALU.mult, op1=ALU.mult),
                    reads=[("XB", k), "rstd", "vecs"], writes=[(tag, k)])

        chunkc = [0]

        for tb in range(ntb):
            t0 = tb * TB
            PL = "pool" if (USE_POOL and tb > 0) else "dve"
            P.op("dve", lambda e: e.memset(dummy[:, 2:3], 0.0), writes=["G3", "dummy2"])
            P.dma("sp", "ld_x", lambda e, t0=t0: e.dma_start(out=XB[:], in_=xT_v[:, :, t0:t0 + TB]),
                  writes=[("XB", k) for k in range(8)])
            rmsnorm_to(hT, V_NMIX, float(D), "hT")
            HT = [("hT", k) for k in range(8)]
            if tb == 0:
                dbg("hT", hT[:], [128, 8, TB], HT)

            for half in range(2):
                c0 = half * 512
                wB, kB = wload(v_in[:, :, C_SC + c0:C_SC + c0 + 512], [128, 8, 512], K_SC)
                wC, kC = wload(v_in[:, :, C_SC + 1024 + c0:C_SC + 1024 + c0 + 512], [128, 8, 512], K_SC)
                wX, kX = wload(v_in[:, :, C_SC + 2048 + c0:C_SC + 2048 + c0 + 512], [128, 8, 512], K_SC)
                for c4 in range(4):
                    cb = half * 4 + c4
                    cs_ = slice(c4 * 128, (c4 + 1) * 128)
                    psB, pkB = mmbank()
                    P.op("pe", proj(psB, lambda k, w=wB, s=cs_: w[:, k, s], lambda k: hT[:, k, :], 8),
                         reads=HT + kB, writes=[pkB])
                    psC, pkC = mmbank()
                    P.op("pe", proj(psC, lambda k, w=wC, s=cs_: w[:, k, s], lambda k: hT[:, k, :], 8),
                         reads=HT + kC, writes=[pkC])
                    psX, pkX = mmbank()
                    P.op("pe", proj(psX, lambda k, w=wX, s=cs_: w[:, k, s], lambda k: hT[:, k, :], 8),
                         reads=HT + kX, writes=[pkX])
                    P.op("act", lambda e, ps=psB: e.activation(out=Bsb[:], in_=ps, func=AF.Identity),
                         reads=[pkB], writes=["Bsb"])
                    P.op("act", lambda e, ps=psC: e.activation(out=Csb[:], in_=ps, func=AF.Identity),
                         reads=[pkC], writes=["Csb"])
                    P.op("dve", lambda e, cb=cb: e.tensor_copy(out=usc[:, 0:2], in_=car_sc[:, cb, :]),
                         reads=["car_sc"], writes=["usc_h"])
                    P.op("dve", lambda e, ps=psX: e.tensor_tensor(out=usc[:, 2:TB + 2], in0=ps, in1=Csb[:], op=ALU.mult),
                         reads=[pkX, "Csb"], writes=["usc"])
                    P.op("dve", lambda e, cb=cb: e.tensor_copy(out=car_sc[:, cb, :], in_=usc[:, TB:TB + 2]),
                         reads=["usc"], writes=["car_sc"])
                    wv = V_SCW + cb * 3
                    P.op("act", lambda e, wv=wv: [e.activation(out=dgs[:, tap, :], in_=ident_bf, func=AF.Copy,
                                                               scale=vecs[:, wv + tap:wv + tap + 1]) for tap in range(3)][-1],
                         reads=["vecs"] + CB, writes=["dgs"])
                    psV, pkV = mmbank()

                    def scconv(e, psV=psV):
                        last = None
                        for tap in range(3):
                            last = e.matmul(psV, dgs[:, tap, :], usc[:, tap:tap + TB], start=(tap == 0), stop=(tap == 2))
                        return last
                    P.op("pe", scconv, reads=["dgs", "usc", "usc_h"], writes=[pkV])
                    P.op("dve", lambda e, cb=cb, psV=psV: e.tensor_tensor(out=yaT[:, cb, :], in0=psV, in1=Bsb[:], op=ALU.mult),
                         reads=[pkV, "Bsb"], writes=[("yaT", cb)])
            YA = [("yaT", k) for k in range(8)]
            if tb == 0:
                dbg("yaT", yaT[:], [128, 8, TB], YA)

            ps, pk = mmbank()

            def dtmm(e, ps=ps):
                last = None
                for cc in range(4):
                    for k in range(8):
                        last = e.matmul(ps[:, cc * 32:(cc + 1) * 32], hT[:, k, cc * 128:(cc + 1) * 128], wdt[:, k, :],
                                        start=(k == 0), stop=(k == 7))
                return last
            P.op("pe", dtmm, reads=HT + ["wdt"], writes=[pk])
            dtb_b = vecs[:, V_DTB:V_DTB + 32].unsqueeze(1).broadcast_to([128, 4, 32])
            A_b = Abc[:].unsqueeze(1).broadcast_to([128, 4, 32])
            P.op("dve", lambda e, ps=ps: e.tensor_tensor(out=T1, in0=ps[:, 0:128].rearrange("p (c h) -> p c h", c=4),
                                                         in1=dtb_b, op=ALU.add),
                 reads=[pk, "vecs"], writes=["T1"])
            P.op("act", lambda e: e.activation(out=TE, in_=T1, func=AF.Exp), reads=["T1"], writes=["TE"])
            P.op("act", lambda e: e.activation(out=TDT, in_=TE, func=AF.Ln, bias=1.0), reads=["TE"], writes=["TDT"])
            P.op("dve", lambda e: e.tensor_tensor(out=TDA, in0=TDT, in1=A_b, op=ALU.mult),
                 reads=["TDT", "Abc"], writes=["TDA"])
            ps2, pk2 = mmbank()

            def csmm(e, ps2=ps2):
                e.matmul(ps2[:, 0:128], tri, tbl[:, 3, :], start=True, stop=True)
                return e.matmul(ps2[:, 128:256], ones, tbl[:, 3, :], start=True, stop=True)
            P.op("pe", csmm, reads=["TDA", "cst"], writes=[pk2])
            P.op("act", lambda e, ps2=ps2: e.activation(out=tbl[:, 4, :], in_=ps2[:, 0:128], func=AF.Identity),
                 reads=[pk2], writes=["TCS"])
            P.op("dve", lambda e, ps2=ps2: e.tensor_tensor(out=tbl[:, 5, :], in0=ps2[:, 128:256], in1=tbl[:, 4, :],
                                                           op=ALU.subtract),
                 reads=[pk2, "TCS"], writes=["T2"])
            P.op("act", lambda e: e.activation(out=tbl[:, 6, :], in_=tbl[:, 5, :], func=AF.Exp),
                 reads=["T2"], writes=["TDEC"])
            P.op("act", lambda e, ps2=ps2: e.activation(out=tbl[:, 7, :], in_=ps2[:, 128:256], func=AF.Exp),
                 reads=[pk2], writes=["TCD"])
            if tb == 0:
                dbg("tbl", tbl[:], [128, 8, 128], ["T1", "TE", "TDT", "TDA", "TCS", "T2", "TDEC", "TCD"])

            P.op("dve", lambda e: e.memset(dummy[:, 0:1], 0.0), writes=["BIG1", "dummy0"])
            v_zx = s_in[:, C_Z:C_B].rearrange("(k p) (s n) -> p s k n", p=128, s=2)
            v_bc = s_in[:, C_B:C_DT].rearrange("(k p) (s n) -> p s k n", p=128, s=2)

            def group_gen(g, si, wBC, kBC, wZX, kZX):
                B_ = GS[si]
                zs, u4, dg, xsT, yT, BT, CT, sq2 = (B_["zs"], B_["u4"], B_["dg"], B_["xsT"], B_["yT"], B_["BT"],
                                                     B_["CT"], B_["sq2"])
                rt_, rstd_ = B_["rt"], B_["rstd"]
                S0, S1, S2, S3 = PS[si]
                kS = lambda n: "S%d" % n
                K = lambda n, *a: (n, si) + tuple(a)
                go = (g % 2) * 128
                for j in range(2):
                    ps, pk = mmbank()
                    P.op("pe", proj(ps, lambda k, j=j, w=wZX: w[:, 0, k, j * 128:(j + 1) * 128], lambda k: hT[:, k, :], 8),
                         reads=HT + kZX, writes=[pk])
                    P.op("act", lambda e, ps=ps, j=j: e.activation(out=zs[:, j, :], in_=ps, func=AF.Silu),
                         reads=[pk], writes=[K("zs", j)])
                    yield
                P.op(PL, lambda e: e.tensor_copy(out=u4[:, :, 0:3], in_=car_ss[:, g, :, :]),
                     reads=[("car_ss", g)], writes=[K("u4_h")])
                for b in range(4):
                    ps, pk = mmbank()
                    if b < 2:
                        lhs = (lambda k, b=b, w=wZX: w[:, 1, k, b * 128:(b + 1) * 128])
                        rk = kZX
                    else:
                        lhs = (lambda k, b=b, go=go, w=wBC: w[:, b - 2, k, go:go + 128])
                        rk = kBC
                    P.op("pe", proj(ps, lhs, lambda k: hT[:, k, :], 8), reads=HT + rk, writes=[pk])
                    P.op("act", lambda e, ps=ps, b=b: e.activation(out=u4[:, b, 3:TB + 3], in_=ps, func=AF.Identity),
                         reads=[pk], writes=[K("u4", b)])
                    yield
                P.op(PL, lambda e: e.tensor_copy(out=car_ss[:, g, :, :], in_=u4[:, :, TB:TB + 3]),
                     reads=[K("u4", b) for b in range(4)], writes=[("car_ss", g)])
                for b in range(4):
                    ch = (2 * g + b) if b < 2 else (16 + g if b == 2 else 24 + g)
                    wv = V_SSW + ch * 4
                    bv = V_SSB + ch
                    P.op("act", lambda e, b=b, wv=wv: [e.activation(out=dg[:, b * 4 + tap, :], in_=ident_bf, func=AF.Copy,
                                                                    scale=vecs[:, wv + tap:wv + tap + 1]) for tap in range(4)][-1],
                         reads=["vecs"] + CB, writes=[K("dg", b)])
                    ps, pk = mmbank()

                    def ssconv(e, ps=ps, b=b):
                        last = None
                        for tap in range(4):
                            last = e.matmul(ps, dg[:, b * 4 + tap, :], u4[:, b, tap:tap + TB], start=(tap == 0), stop=(tap == 3))
                        return last
                    P.op("pe", ssconv, reads=[K("dg", b), K("u4", b), K("u4_h")], writes=[pk])
                    if b < 2:
                        P.op("act", lambda e, b=b, ps=ps, bv=bv: e.activation(out=xsT[:, b, :], in_=ps, func=AF.Silu,
                                                                              bias=vecs[:, bv:bv + 1]),
                             reads=[pk, "vecs"], writes=[K("xsT", b)])
                    elif b == 2:
                        P.op("act", lambda e, ps=ps, bv=bv: e.activation(out=BT, in_=ps, func=AF.Silu, bias=vecs[:, bv:bv + 1]),
                             reads=[pk, "vecs"], writes=[K("BT")])
                    else:
                        P.op("act", lambda e, ps=ps, bv=bv: e.activation(out=CT, in_=ps, func=AF.Silu, bias=vecs[:, bv:bv + 1]),
                             reads=[pk, "vecs"], writes=[K("CT")])
                    yield
                if tb == 0 and g == 0:
                    dbg("xsT", xsT, [128, 2, TB], [K("xsT", 0), K("xsT", 1)])
                    dbg("BT", BT, [128, TB], [K("BT")])
                    dbg("CT", CT, [128, TB], [K("CT")])
                    dbg("zs", zs, [128, 2, TB], [K("zs", 0), K("zs", 1)])
                par = si
                R, LT, ecs, CBm, WT, CsT, xdt, xdtd, Btok = (R_[par], LT_[par], ecs_[par], CBm_[par], WT_[par],
                                                             CsT_[par], xdt_[par], xdtd_[par], Btok_[par])
                for cc in range(4):
                    tc_ = slice(cc * 128, (cc + 1) * 128)

                    def rbuild(e, cc=cc):
                        last = None
                        for h in range(4):
                            last = e.tensor_scalar(out=R[:, h, :], in0=tri, scalar1=TDA[:, cc, 4 * g + h:4 * g + h + 1],
                                                   scalar2=None, op0=ALU.mult)
                        return last
                    P.op(PL, rbuild, reads=["TDA", "cst"], writes=[("R", par)])

                    def pe1(e, tc_=tc_):
                        e.transpose(S2[:, 0:128], xsT[:, 0, tc_], ident)
                        e.transpose(S2[:, 128:256], xsT[:, 1, tc_], ident)
                        e.matmul(S2[:, 256:384], BT[:, tc_], CT[:, tc_], start=True, stop=True)
                        return e.matmul(S2[:, 384:512], BT[:, tc_], ident_bf, start=True, stop=True)
                    P.op("pe", pe1, reads=[K("xsT", 0), K("xsT", 1), K("BT"), K("CT")] + CB, writes=[kS(2)])

                    def pe2(e):
                        rf = R[:].rearrange("p h i -> p (h i)")
                        e.matmul(S0[:], Umat, rf, start=True, stop=True)
                        return e.matmul(S1[:], ones, rf, start=True, stop=True)
                    P.op("pe", pe2, reads=[("R", par), "cst"], writes=[kS(0), kS(1)])
                    P.op("act", lambda e: e.activation(out=LT[:].rearrange("p h i -> p (h i)"), in_=S0[:], func=AF.Exp),
                         reads=[kS(0)], writes=[("LT", par)])
                    P.op("act", lambda e: e.activation(out=ecs[:].rearrange("p h i -> p (h i)"), in_=S1[:], func=AF.Exp),
                         reads=[kS(1)], writes=[("ecs", par)])
                    P.op("dve", lambda e: e.tensor_tensor(out=CBm[:], in0=S2[:, 256:384], in1=tri, op=ALU.mult),
                         reads=[kS(2), "cst"], writes=[("CBm", par)])
                    P.op("dve", lambda e, cc=cc: e.tensor_tensor(
                        out=xdt[:], in0=S2[:, 0:256].rearrange("p (h q) -> p h q", h=4),
                        in1=TDT[:, cc, 4 * g:4 * g + 4].unsqueeze(2).broadcast_to([128, 4, 64]), op=ALU.mult),
                        reads=[kS(2), "TDT"], writes=[("xdt", par)])
                    P.op("act", lambda e: e.activation(out=Btok[:], in_=S2[:, 384:512], func=AF.Identity),
                         reads=[kS(2)], writes=[("Btok", par)])
                    yield
                    P.op("dve", lambda e: e.tensor_tensor(
                        out=WT[:], in0=LT[:], in1=CBm[:].unsqueeze(1).broadcast_to([128, 4, 128]), op=ALU.mult),
                        reads=[("LT", par), ("CBm", par)], writes=[("WT", par)])
                    P.op(PL, lambda e, tc_=tc_: e.tensor_tensor(
                        out=CsT[:], in0=ecs[:], in1=CT[:, tc_].unsqueeze(1).broadcast_to([128, 4, 128]), op=ALU.mult),
                        reads=[("ecs", par), K("CT")], writes=[("CsT", par)])
                    P.op(PL, lambda e, cc=cc: e.tensor_tensor(
                        out=xdtd[:], in0=xdt[:],
                        in1=TDEC[:, cc, 4 * g:4 * g + 4].unsqueeze(2).broadcast_to([128, 4, 64]), op=ALU.mult),
                        reads=[("xdt", par), "TDEC"], writes=[("xdtd", par)])
                    yield

                    def pe3(e):
                        for h in range(4):
                            o = S3[64 * (h % 2):64 * (h % 2) + 64, (h // 2) * 128:(h // 2) * 128 + 128]
                            e.matmul(o, xdt[:, h, :], WT[:, h, :], start=True, stop=False)
                            e.matmul(o, stb[:, g, h * 64:(h + 1) * 64], CsT[:, h, :], start=False, stop=True)
                        return e.matmul(S3[:, 256:512], Btok[:], xdtd[:].rearrange("p h q -> p (h q)"), start=True, stop=True)
                    P.op("pe", pe3, reads=[("xdt", par), ("WT", par), ("CsT", par), ("Btok", par), ("xdtd", par), ("stb", g)],
                         writes=[kS(3)])

                    def yev(e, tc_=tc_):
                        last = None
                        for j in range(2):
                            dv = V_DCOL + 2 * g + j
                            last = e.scalar_tensor_tensor(out=yT[:, j, tc_], in0=xsT[:, j, tc_], scalar=vecs[:, dv:dv + 1],
                                                          in1=S3[:, j * 128:(j + 1) * 128], op0=ALU.mult, op1=ALU.add)
                        return last
                    P.op("dve", yev, reads=[kS(3), K("xsT", 0), K("xsT", 1), "vecs"], writes=[K("yT", cc)])
                    P.op(PL, lambda e, cc=cc: e.tensor_tensor(
                        out=stf[:, g, :].rearrange("p (h q) -> p h q", h=4),
                        in0=stf[:, g, :].rearrange("p (h q) -> p h q", h=4),
                        in1=TCD[:, cc, 4 * g:4 * g + 4].unsqueeze(2).broadcast_to([128, 4, 64]), op=ALU.mult),
                        reads=[("stf", g), "TCD"], writes=[("stf", g)])
                    P.op("dve", lambda e: e.tensor_tensor(out=stf[:, g, :], in0=stf[:, g, :], in1=S3[:, 256:512], op=ALU.add),
                         reads=[("stf", g), kS(3)], writes=[("stf", g)])
                    P.op("act", lambda e: e.activation(out=stb[:, g, :], in_=stf[:, g, :], func=AF.Identity),
                         reads=[("stf", g)], writes=[("stb", g)])
                    yield
                YT = [K("yT", cc) for cc in range(4)]
                if tb == 0 and g == 0:
                    dbg("yT", yT, [128, 2, TB], YT)
                P.op(PL, lambda e: e.tensor_tensor(out=yT, in0=yT, in1=zs, op=ALU.mult),
                     reads=YT + [K("zs", 0), K("zs", 1)], writes=YT)
                P.op("act", lambda e: e.activation(out=sq2, in_=yT, func=AF.Square), reads=YT, writes=[K("sq2")])
                ps, pk = mmbank()
                P.op("pe", proj(ps, lambda k: ones_bf, lambda k: sq2[:, k, :], 2), reads=[K("sq2")] + CB, writes=[pk])
                yield
                P.op("act", lambda e, ps=ps: e.activation(out=rt_[:], in_=ps, func=AF.Ln, scale=1.0 / 256.0, bias=EPS),
                     reads=[pk], writes=[B_["rtk"]])
                P.op("act", lambda e: e.activation(out=rstd_[:], in_=rt_[:], func=AF.Exp, scale=-0.5), reads=[B_["rtk"]], writes=[B_["rstdk"]])
                for j in range(2):
                    nv = V_SNW + 2 * g + j
                    P.op("dve", lambda e, j=j, nv=nv: e.scalar_tensor_tensor(
                        out=ybT[:, 2 * g + j, :], in0=yT[:, j, :], scalar=vecs[:, nv:nv + 1], in1=rstd_[:],
                        op0=ALU.mult, op1=ALU.mult),
                        reads=YT + [B_["rstdk"], "vecs"], writes=[("ybT", 2 * g + j)])
                yield

            for gp in range(4):
                g0 = 2 * gp
                wBC, kBC = wload(v_bc[:, :, :, 128 * g0:128 * g0 + 256], [128, 2, 8, 256], K_SSD)
                gens = []
                for si in range(2):
                    g = g0 + si
                    wZX, kZX = wload(v_zx[:, :, :, 256 * g:256 * g + 256], [128, 2, 8, 256], K_SSD)
                    gens.append(group_gen(g, si, wBC, kBC, wZX, kZX))
                alive = list(gens)
                while alive:
                    for gn in list(alive):
                        try:
                            next(gn)
                        except StopIteration:
                            alive.remove(gn)
            YB = [("ybT", k) for k in range(16)]
            if tb == 0:
                dbg("ybT", ybT[:], [128, 16, TB], YB)

            P.op("dve", lambda e: e.memset(dummy[:, 3:4], 0.0), writes=["G2", "dummy3"])
            v_g = s_in[:, C_G:DIN].rearrange("(k p) (s n) -> p s k n", p=128, s=2)
            v_bsc = s_bsc.rearrange("(k p) n -> p k n", p=128)
            v_bssm = s_bssm.rearrange("(k p) n -> p k n", p=128)
            for obp in range(4):
                o0 = obp * 256
                wG, kG = wload(v_g[:, :, :, o0:o0 + 256], [128, 2, 8, 256], K_G)
                wA, kA = wload(v_bsc[:, :, o0:o0 + 256], [128, 8, 256], K_BSC)
                wS, kS = wload(v_bssm[:, :, o0:o0 + 256], [128, 16, 256], K_BSSM)
                for o2 in range(2):
                    ob = obp * 2 + o2
                    os_ = slice(o2 * 128, (o2 + 1) * 128)
                    for s in range(2):
                        psg, pkg = mmbank()
                        P.op("pe", proj(psg, lambda k, s=s, os_=os_, w=wG: w[:, s, k, os_], lambda k: hT[:, k, :], 8),
                             reads=HT + kG, writes=[pkg])
                        psb, pkb = mmbank()
                        if s == 0:
                            P.op("pe", proj(psb, lambda k, os_=os_, w=wA: w[:, k, os_], lambda k: yaT[:, k, :], 8),
                                 reads=YA + kA, writes=[pkb])
                        else:
                            P.op("pe", proj(psb, lambda k, os_=os_, w=wS: w[:, k, os_], lambda k: ybT[:, k, :], 16),
                                 reads=YB + kS, writes=[pkb])
                        bg = V_BG + s * 8 + ob
                        P.op("act", lambda e, psg=psg, bg=bg: e.activation(out=gsb[:], in_=psg, func=AF.Sigmoid,
                                                                            bias=vecs[:, bg:bg + 1]),
                             reads=[pkg, "vecs"], writes=["gsb"])
                        msb = m1sb if s == 0 else m2sb
                        P.op("dve", lambda e, psb=psb, msb=msb: e.tensor_tensor(out=msb[:], in0=psb, in1=gsb[:], op=ALU.mult),
                             reads=[pkb, "gsb"], writes=["m1sb" if s == 0 else "m2sb"])
                    P.op(PL, lambda e, ob=ob: e.tensor_tensor(out=mgT[:, ob, :], in0=m1sb[:], in1=m2sb[:], op=ALU.add),
                         reads=["m1sb", "m2sb"], writes=[("mgT", ob)])
            MG = [("mgT", k) for k in range(8)]
            if tb == 0:
                dbg("mgT", mgT[:], [128, 8, TB], MG)

            v_out = s_out.rearrange("(k p) n -> p k n", p=128)
            for oh in range(2):
                wO, kO = wload(v_out[:, :, oh * 512:(oh + 1) * 512], [128, 8, 512], K_OUT)
                for o4 in range(4):
                    ob = oh * 4 + o4
                    ps, pk = mmbank()
                    P.op("pe", proj(ps, lambda k, o4=o4, wO=wO: wO[:, k, o4 * 128:(o4 + 1) * 128], lambda k: mgT[:, k, :], 8),
                         reads=MG + kO, writes=[pk])
                    P.op("dve", lambda e, ps=ps, ob=ob: e.tensor_tensor(out=XB[:, ob, :], in0=ps, in1=XB[:, ob, :], op=ALU.add),
                         reads=[pk, ("XB", ob)], writes=[("XB", ob)])
            if tb == 0:
                dbg("x1T", XB[:], [128, 8, TB], [("XB", k) for k in range(8)])

            P.op("dve", lambda e: e.memset(dummy[:, 2:3], 0.0), writes=["G2", "G3", "dummy2"])
            rmsnorm_to(h2T, V_NMLP, float(D), "h2T")
            H2 = [("h2T", k) for k in range(8)]
            P.op("dve", lambda e: e.memset(dummy[:, 1:2], 0.0), writes=["BIG1", "dummy1"])
            v_m1 = s_m1.rearrange("(k p) n -> p k n", p=128)
            for fb in range(8):
                wM, kM = wload(v_m1[:, :, fb * 512:(fb + 1) * 512], [128, 8, 512], K_M1)
                for f4 in range(4):
                    f = fb * 4 + f4
                    ps, pk = mmbank()
                    P.op("pe", proj(ps, lambda k, f4=f4, wM=wM: wM[:, k, f4 * 128:(f4 + 1) * 128], lambda k: h2T[:, k, :], 8),
                         reads=H2 + kM, writes=[pk])
                    P.op("act", lambda e, ps=ps: e.activation(out=rl[:], in_=ps, func=AF.Relu), reads=[pk], writes=["rl"])
                    P.op("act", lambda e, f=f: e.activation(out=aT[:, f, :], in_=rl[:], func=AF.Square),
                         reads=["rl"], writes=[("aT", f)])
            AT = [("aT", f) for f in range(32)]
            v_m2 = s_m2.rearrange("(k p) n -> p k n", p=128)
            for obp in range(4):
                o0 = obp * 256
                wh = []
                for fh in range(2):
                    wh.append(wload(v_m2[:, fh * 16:(fh + 1) * 16, o0:o0 + 256], [128, 16, 256], K_M2))
                for o2 in range(2):
                    ob = obp * 2 + o2
                    os_ = slice(o2 * 128, (o2 + 1) * 128)
                    ps, pk = mmbank()
                    P.op("pe", proj(ps, lambda k, os_=os_, wh=wh: wh[k // 16][0][:, k % 16, os_], lambda k: aT[:, k, :], 32),
                         reads=AT + wh[0][1] + wh[1][1], writes=[pk])
                    P.op("dve", lambda e, ps=ps, ob=ob: e.tensor_tensor(out=XB[:, ob, :], in0=ps, in1=XB[:, ob, :], op=ALU.add),
                         reads=[pk, ("XB", ob)], writes=[("XB", ob)])

            P.op("act", lambda e: e.activation(out=sq[:], in_=XB[:], func=AF.Square),
                 reads=[("XB", k) for k in range(8)], writes=["sq"])
            ps, pk = mmbank()
            P.op("pe", proj(ps, lambda k: ones_bf, lambda k: sq[:, k, :], 8), reads=["sq"] + CB, writes=[pk])
            P.op("act", lambda e, ps=ps: e.activation(out=rt[:], in_=ps, func=AF.Ln, scale=1.0 / D, bias=EPS),
                 reads=[pk], writes=["rt"])
            P.op("act", lambda e: e.activation(out=rstd[:], in_=rt[:], func=AF.Exp, scale=-0.5), reads=["rt"], writes=["rstd"])
            for k in range(8):
                P.op("dve", lambda e, k=k: e.scalar_tensor_tensor(
                    out=XB[:, k, :], in0=XB[:, k, :], scalar=vecs[:, V_NFIN + k:V_NFIN + k + 1], in1=rstd[:],
                    op0=ALU.mult, op1=ALU.mult),
                    reads=[("XB", k), "rstd", "vecs"], writes=[("XB", k)])
            P.dma("sp", "st_out", lambda e, t0=t0: e.dma_start(out=outT_v[:, :, t0:t0 + TB], in_=XB[:]),
                  reads=[("XB", k) for k in range(8)], writes=[("outT", tb)])

        fin_reads = [("outT", tb) for tb in range(ntb)] + [("dbg", n) for n in dbg_out]
        P.op("sp", lambda e: None, reads=fin_reads, writes=["fin"])

        block = st.enter_context(nc.Block())
        P.emit(nc, block, st)
    return nc, dbg_out


def _pack_inputs(inputs):
    f = lambda a: np.ascontiguousarray(np.asarray(a, dtype=np.float32))
    vecs = np.zeros((128, NV), np.float32)
    vecs[:, V_NMIX:V_NMIX + 8] = f(inputs["norm_mix"]).reshape(8, 128).T
    vecs[:, V_NMLP:V_NMLP + 8] = f(inputs["norm_mlp"]).reshape(8, 128).T
    vecs[:, V_NFIN:V_NFIN + 8] = f(inputs["norm_final"]).reshape(8, 128).T
    vecs[:, V_BG:V_BG + 16] = f(inputs["b_gate"]).reshape(16, 128).T
    vecs[:, V_SCW:V_SCW + 24] = f(inputs["sc_conv_w"]).reshape(3, 8, 128).transpose(2, 1, 0).reshape(128, 24)
    vecs[:, V_SSW:V_SSW + 128] = f(inputs["ssm_conv_w"]).reshape(4, 32, 128).transpose(2, 1, 0).reshape(128, 128)
    vecs[:, V_SSB:V_SSB + 32] = f(inputs["ssm_conv_b"]).reshape(32, 128).T
    vecs[:, V_SNW:V_SNW + 16] = f(inputs["ssm_norm_w"]).reshape(16, 128).T
    vecs[:, V_DCOL:V_DCOL + 16] = np.repeat(f(inputs["D_skip"]), 64).reshape(16, 128).T
    vecs[:, V_DTB:V_DTB + 32] = np.tile(f(inputs["dt_bias"])[None, :], (128, 1))
    vecs[:, V_ALOG:V_ALOG + 32] = np.tile(f(inputs["A_log"])[None, :], (128, 1))
    k = np.arange(128)
    consts = np.zeros((128, 512), np.float32)
    consts[:, 0:128] = np.eye(128, dtype=np.float32)
    consts[:, 128:256] = (k[:, None] <= k[None, :]).astype(np.float32)
    consts[:, 256:384] = (k[:, None] > k[None, :]).astype(np.float32)
    consts[:, 384:512] = 1.0
    x = f(inputs["x"])
    xT = np.ascontiguousarray(x.transpose(0, 2, 1))
    common = {
        "w_in": f(inputs["w_in"]), "w_bsc": f(inputs["w_branch_sc"]), "w_bssm": f(inputs["w_branch_ssm"]),
        "w_out": f(inputs["w_out"]), "w_m1": f(inputs["w_mlp1"]), "w_m2": f(inputs["w_mlp2"]),
        "vecs": vecs, "consts": consts,
    }
    return [dict(common, xT=xT[i]) for i in range(8)]


def kernel(**inputs):
    in_maps = _pack_inputs(inputs)
    nc, _ = build_nc()
    res = run_bass_kernel_spmd(nc, in_maps, core_ids=list(range(8)))
    out = np.stack([np.ascontiguousarray(res.results[i]["outT"].T) for i in range(8)], axis=0)
    return out.astype(np.float32)
```

```python
import numpy as np
import concourse.bass as bass
import concourse.mybir as mybir
from concourse.bass_utils import run_bass_kernel_spmd

F32 = mybir.dt.float32
BF16 = mybir.dt.bfloat16
AF = mybir.ActivationFunctionType
ALU = mybir.AluOpType

D = 1024
T = 2048
TB = 512
NTB = T // TB
DFF = 4096
DIN = 11296
EPS = 1e-6
C_SC = 0
C_Z = 3072
C_X = 5120
C_B = 7168
C_C = 8192
C_DT = 9216
C_G = 9248
NW = 4
SAME_ENG_SYNC = True
USE_POOL = False

V_NMIX, V_NMLP, V_NFIN, V_BG, V_SCW, V_SSW, V_SSB, V_SNW, V_DCOL, V_DTB, V_ALOG = (
    0, 8, 16, 24, 40, 64, 192, 224, 240, 256, 288)
NV = 320

ALIAS = {"sq": "G2", "mgT": "G2", "hT": "G3", "h2T": "G3", "zs": "BIG1", "u4": "BIG1", "u4_h": "BIG1", "dg": "BIG1", "xsT": "BIG1", "yT": "BIG1",
         "BT": "BIG1", "CT": "BIG1", "sq2": "BIG1", "aT": "BIG1"}


class Prog:
    def __init__(self):
        self.ops = []

    def _norm(self, keys):
        out = []
        for k in keys:
            out.append(k)
            base = k[0] if isinstance(k, tuple) else k
            if base in ALIAS:
                out.append(ALIAS[base])
        return tuple(dict.fromkeys(out))

    def _alias(self, keys):
        out = []
        for k in keys:
            base = k[0] if isinstance(k, tuple) else k
            if base in ALIAS:
                out.append(ALIAS[base])
        return out

    def op(self, eng, fn, reads=(), writes=()):
        rd = self._norm(tuple(reads) + tuple(self._alias(writes)))
        self.ops.append(dict(eng=eng, fn=fn, reads=rd, writes=tuple(writes), sem=None))

    def dma(self, queue, sem, fn, reads=(), writes=()):
        rd = self._norm(tuple(reads) + tuple(self._alias(writes)))
        self.ops.append(dict(eng=queue, fn=fn, reads=rd, writes=tuple(writes), sem=sem))

    def emit(self, nc, block, stack):
        ops = self.ops
        last_writer, readers = {}, {}
        for i, o in enumerate(ops):
            deps = set()
            for k in o["reads"]:
                if k in last_writer:
                    deps.add(last_writer[k])
            for k in o["writes"]:
                if k in last_writer:
                    deps.add(last_writer[k])
                deps.update(readers.get(k, ()))
            deps.discard(i)
            o["deps"] = deps
            for k in o["reads"]:
                readers.setdefault(k, []).append(i)
            for k in o["writes"]:
                last_writer[k] = i
                readers[k] = []
        for o in ops:
            o["signal"] = o["sem"] is not None
        for o in ops:
            for d in o["deps"]:
                p = ops[d]
                if p["sem"] is None:
                    if p["eng"] == o["eng"] and (p["eng"] == "pe" or not SAME_ENG_SYNC) and o["sem"] is None:
                        continue
                    p["signal"] = True
        cnt = {}
        semnames = set()
        LIM = 1000
        for o in ops:
            if o["sem"] is not None:
                s = o["sem"]
                o["semname"] = s
                cnt[s] = cnt.get(s, 0) + 16
                o["tick"] = cnt[s]
                semnames.add(s)
            elif o["signal"]:
                base = "eng_" + o["eng"]
                c = cnt.get(base, 0)
                cnt[base] = c + 1
                s = "%s_%d" % (base, c // LIM)
                o["semname"] = s
                o["tick"] = c % LIM + 1
                semnames.add(s)
            else:
                o["semname"] = None
        sems = {s: stack.enter_context(nc.semaphore(s)) for s in sorted(semnames)}
        self.maxcnt = cnt

        def run_engine(engname, e):
            known = {}
            for o in ops:
                if o["eng"] != engname:
                    continue
                need = {}
                for d in o["deps"]:
                    p = ops[d]
                    if p["sem"] is None and p["eng"] == engname and o["sem"] is None:
                        if engname == "pe" or not SAME_ENG_SYNC:
                            continue
                    if not p["signal"]:
                        continue
                    s = p["semname"]
                    need[s] = max(need.get(s, 0), p["tick"])
                for s, v in need.items():
                    if known.get(s, 0) < v:
                        e.wait_ge(sems[s], v)
                        known[s] = v
                inst = o["fn"](e)
                if o["signal"]:
                    assert inst is not None
                    inst.then_inc(sems[o["semname"]], 16 if o["sem"] is not None else 1)

        @block.tensor
        def _(e):
            run_engine("pe", e)

        @block.scalar
        def _(e):
            run_engine("act", e)

        @block.vector
        def _(e):
            run_engine("dve", e)

        @block.gpsimd
        def _(e):
            run_engine("pool", e)

        @block.sync
        def _(e):
            run_engine("sp", e)


def build_nc(debug=False, ntb=NTB):
    import contextlib
    nc = bass.Bass("TRN2", target_bir_lowering=False)
    P = Prog()
    dr = {}
    dr["xT"] = nc.dram_tensor("xT", [D, T], F32, kind="ExternalInput").ap()
    dr["w_in"] = nc.dram_tensor("w_in", [D, DIN], F32, kind="ExternalInput").ap()
    dr["w_bsc"] = nc.dram_tensor("w_bsc", [D, D], F32, kind="ExternalInput").ap()
    dr["w_bssm"] = nc.dram_tensor("w_bssm", [2 * D, D], F32, kind="ExternalInput").ap()
    dr["w_out"] = nc.dram_tensor("w_out", [D, D], F32, kind="ExternalInput").ap()
    dr["w_m1"] = nc.dram_tensor("w_m1", [D, DFF], F32, kind="ExternalInput").ap()
    dr["w_m2"] = nc.dram_tensor("w_m2", [DFF, D], F32, kind="ExternalInput").ap()
    dr["vecs"] = nc.dram_tensor("vecs", [128, NV], F32, kind="ExternalInput").ap()
    dr["consts"] = nc.dram_tensor("consts", [128, 512], F32, kind="ExternalInput").ap()
    outT = nc.dram_tensor("outT", [D, T], F32, kind="ExternalOutput").ap()
    s_in = nc.dram_tensor("s_in", [D, DIN], BF16, kind="Internal").ap()
    s_bsc = nc.dram_tensor("s_bsc", [D, D], BF16, kind="Internal").ap()
    s_bssm = nc.dram_tensor("s_bssm", [2 * D, D], BF16, kind="Internal").ap()
    s_out = nc.dram_tensor("s_out", [D, D], BF16, kind="Internal").ap()
    s_m1 = nc.dram_tensor("s_m1", [D, DFF], BF16, kind="Internal").ap()
    s_m2 = nc.dram_tensor("s_m2", [DFF, D], BF16, kind="Internal").ap()
    dbg_out = {}

    with contextlib.ExitStack() as st:
        def sb(name, shape, dt):
            return st.enter_context(nc.sbuf_tensor(name, shape, dt))

        vecs = sb("vecs_sb", [128, NV], F32)
        cst = sb("cst", [128, 512], F32)
        cbf = sb("cbf", [128, 384], BF16)
        Abc = sb("Abc", [128, 32], F32)
        XB = sb("XB", [128, 8, TB], F32)
        sq = sb("sq", [128, 8, TB], BF16)
        hT = sb("hT", [128, 8, TB], BF16)
        rt = sb("rt", [128, TB], F32)
        rstd = sb("rstd", [128, TB], F32)
        Bsb = sb("Bsb", [128, TB], F32)
        Csb = sb("Csb", [128, TB], F32)
        usc = sb("usc", [128, TB + 2], BF16)
        dgs = sb("dgs", [128, 3, 128], BF16)
        car_sc = sb("car_sc", [128, 8, 2], F32)
        car_ss = sb("car_ss", [128, 8, 4, 3], F32)
        yaT = sb("yaT", [128, 8, TB], BF16)
        tbl = sb("tbl", [128, 8, 128], F32)
        BIG1 = sb("BIG1", [128, 8192], F32)
        BIG2 = sb("BIG2", [128, 6152], F32)
        R_ = [sb(f"R{i}", [128, 2, 4, 128], BF16) for i in range(2)]
        dAhi = sb("dAhi", [128, 128], BF16)
        dAlo = sb("dAlo", [128, 128], BF16)
        LT_ = [sb(f"LT{i}", [128, 4, 128], F32) for i in range(2)]
        ecs_ = [sb(f"ecs{i}", [128, 4, 128], F32) for i in range(2)]
        CBm_ = [sb(f"CBm{i}", [128, 128], F32) for i in range(2)]
        WT_ = [sb(f"WT{i}", [128, 4, 128], BF16) for i in range(2)]
        CsT_ = [sb(f"CsT{i}", [128, 4, 128], BF16) for i in range(2)]
        xdt_ = [sb(f"xdt{i}", [128, 4, 64], BF16) for i in range(2)]
        xdtd_ = [sb(f"xdtd{i}", [128, 4, 64], BF16) for i in range(2)]
        Btok_ = [sb(f"Btok{i}", [128, 128], BF16) for i in range(2)]
        stf = sb("stf", [128, 8, 256], F32)
        stb = sb("stb", [128, 8, 256], BF16)
        ybT = sb("ybT", [128, 16, TB], BF16)
        mgT = sq
        h2T = hT
        gsb = sb("gsb", [128, TB], F32)
        m1sb = sb("m1sb", [128, TB], F32)
        m2sb = sb("m2sb", [128, TB], F32)
        rl = sb("rl", [128, TB], F32)
        wdt = sb("wdt", [128, 8, 32], BF16)
        dummy = sb("fence_t", [128, 4], F32)
        WP = [sb(f"WP{i}", [128, 4096], BF16) for i in range(NW)]
        mm = st.enter_context(nc.psum_tensor("mm", [128, 4, 512], F32))
        S0 = st.enter_context(nc.psum_tensor("S0", [128, 512], F32))
        S1 = st.enter_context(nc.psum_tensor("S1", [128, 512], F32))
        S2 = st.enter_context(nc.psum_tensor("S2", [128, 512], F32))
        S3 = st.enter_context(nc.psum_tensor("S3", [128, 512], F32))

        ident = cst[:, 0:128]
        tri = cst[:, 128:256]
        Umat = cst[:, 256:384]
        ones = cst[:, 384:512]
        ident_bf = cbf[:, 0:128]
        ones_bf = cbf[:, 128:256]
        U_bf = cbf[:, 256:384]
        GS = []
        for si_, big in enumerate((BIG1, BIG2)):
            GS.append(dict(
                zs=big[:, 0:1024].rearrange("p (j t) -> p j t", j=2),
                u4=big[:, 1024:1024 + 1030].bitcast(BF16).rearrange("p (b t) -> p b t", b=4),
                dg=big[:, 2056:3080].bitcast(BF16).rearrange("p (m c) -> p m c", m=16),
                xsT=big[:, 3080:4104].rearrange("p (j t) -> p j t", j=2),
                yT=big[:, 4104:5128].rearrange("p (j t) -> p j t", j=2),
                BT=big[:, 5128:5384].bitcast(BF16),
                CT=big[:, 5384:5640].bitcast(BF16),
                sq2=big[:, 5640:6152].bitcast(BF16).rearrange("p (j t) -> p j t", j=2),
                rt=(rt, gsb)[si_], rstd=(rstd, m1sb)[si_], rtk=("rt", "gsb")[si_], rstdk=("rstd", "m1sb")[si_]))
        PS = [(S0, S1, S2, S3), (S0, S1, S2, S3)]
        aT = BIG1[:, :].bitcast(BF16).rearrange("p (f t) -> p f t", f=32)

        T1, TE, TDT, TDA, TCS, T2, TDEC, TCD = [tbl[:, i, :].rearrange("p (c h) -> p c h", c=4) for i in range(8)]

        mmc = [0]

        def mmbank():
            i = mmc[0] % 4
            mmc[0] += 1
            return mm[:, i, :], ("mm", i)

        wpc = [0]

        def wload(src, shape, reads):
            i = wpc[0] % NW
            wpc[0] += 1
            n = int(np.prod(shape[1:]))
            dst = WP[i][:, 0:n]
            ka, kb = ("WP", i, "a"), ("WP", i, "b")
            if len(shape) == 3:
                dst = dst.rearrange("p (a b) -> p a b", a=shape[1])
                P.dma("sp", f"wp{i}a", lambda e, dst=dst, src=src: e.dma_start(out=dst, in_=src),
                      reads=reads, writes=[ka, kb])
            else:
                dst = dst.rearrange("p (s a b) -> p s a b", s=shape[1], a=shape[2])
                for s_, kk in ((0, ka), (1, kb)):
                    P.dma("sp", f"wp{i}" + "ab"[s_], lambda e, d=dst[:, s_], r=src[:, s_]: e.dma_start(out=d, in_=r),
                          reads=reads, writes=[kk])
            return dst, [ka, kb]

        def proj(ps, lhs, rhs, nk):
            def fn(e):
                last = None
                for k in range(nk):
                    last = e.matmul(ps, lhs(k), rhs(k), start=(k == 0), stop=(k == nk - 1))
                return last
            return fn

        def dbg(name, ap, shape, reads):
            if not debug:
                return
            d = nc.dram_tensor("dbg_" + name, list(shape), ap.dtype, kind="ExternalOutput").ap()
            dbg_out[name] = d
            P.dma("sp", "dbg_" + name, lambda e: e.dma_start(out=d, in_=ap), reads=reads, writes=[("dbg", name)])

        P.dma("sp", "ld_vecs", lambda e: e.dma_start(out=vecs[:], in_=dr["vecs"]), writes=["vecs"])
        P.dma("sp", "ld_cst", lambda e: e.dma_start(out=cst[:], in_=dr["consts"]), writes=["cst"])
        P.op("dve", lambda e: e.tensor_copy(out=cbf[:, 0:128], in_=ident), reads=["cst"], writes=["cbf0"])
        P.op("dve", lambda e: e.tensor_copy(out=cbf[:, 128:256], in_=ones), reads=["cst"], writes=["cbf1"])
        P.op("dve", lambda e: e.tensor_copy(out=cbf[:, 256:384], in_=Umat), reads=["cst"], writes=["cbf2"])
        CB = ["cbf0", "cbf1", "cbf2", "cst"]
        P.op("act", lambda e: e.activation(out=Abc[:], in_=vecs[:, V_ALOG:V_ALOG + 32], func=AF.Exp),
             reads=["vecs"], writes=["Abc"])
        P.op("dve", lambda e: e.tensor_scalar(out=Abc[:], in0=Abc[:], scalar1=-1.0, scalar2=None, op0=ALU.mult),
             reads=["Abc"], writes=["Abc"])
        P.op("dve", lambda e: e.memset(stf[:], 0.0), writes=[("stf", g) for g in range(8)])
        P.op("dve", lambda e: e.memset(stb[:], 0.0), writes=[("stb", g) for g in range(8)])
        P.op("dve", lambda e: e.memset(car_sc[:], 0.0), writes=["car_sc"])
        P.op("dve", lambda e: e.memset(car_ss[:], 0.0), writes=["car_ss"])

        castc = [0]

        def cast(dst, src, key):
            i = castc[0]
            castc[0] += 1
            P.dma("pool", f"cast{i}",
                  lambda e: e.dma_start(out=dst, in_=src, max_dma_last_dim=4096),
                  writes=[key])

        K_SC = [("scr", "in_sc", r) for r in range(2)]
        for r in range(2):
            cast(s_in[r * 512:(r + 1) * 512, 0:3072], dr["w_in"][r * 512:(r + 1) * 512, 0:3072], K_SC[r])
        K_DT = [("scr", "in_dt")]
        cast(s_in[:, C_DT:C_DT + 32], dr["w_in"][:, C_DT:C_DT + 32], K_DT[0])
        K_SSD = [("scr", "in_ssd", r) for r in range(4)]
        for r in range(4):
            cast(s_in[r * 256:(r + 1) * 256, C_Z:C_DT], dr["w_in"][r * 256:(r + 1) * 256, C_Z:C_DT], K_SSD[r])
        K_G = [("scr", "in_g", r) for r in range(2)]
        for r in range(2):
            cast(s_in[r * 512:(r + 1) * 512, C_G:DIN], dr["w_in"][r * 512:(r + 1) * 512, C_G:DIN], K_G[r])
        K_BSC = [("scr", "bsc")]
        cast(s_bsc, dr["w_bsc"], K_BSC[0])
        K_BSSM = [("scr", "bssm", r) for r in range(2)]
        for r in range(2):
            cast(s_bssm[r * 1024:(r + 1) * 1024, :], dr["w_bssm"][r * 1024:(r + 1) * 1024, :], K_BSSM[r])
        K_OUT = [("scr", "out")]
        cast(s_out, dr["w_out"], K_OUT[0])
        K_M1 = [("scr", "m1", r) for r in range(4)]
        for r in range(4):
            cast(s_m1[r * 256:(r + 1) * 256, :], dr["w_m1"][r * 256:(r + 1) * 256, :], K_M1[r])
        K_M2 = [("scr", "m2", r) for r in range(4)]
        for r in range(4):
            cast(s_m2[r * 1024:(r + 1) * 1024, :], dr["w_m2"][r * 1024:(r + 1) * 1024, :], K_M2[r])

        v_in = s_in.rearrange("(k p) n -> p k n", p=128)
        P.dma("sp", "ld_wdt", lambda e: e.dma_start(out=wdt[:], in_=v_in[:, :, C_DT:C_DT + 32]),
              reads=K_DT, writes=["wdt"])

        xT_v = dr["xT"].rearrange("(k p) t -> p k t", p=128)
        outT_v = outT.rearrange("(k p) t -> p k t", p=128)

        def rmsnorm_to(dst_bf, vcol, ndiv, tag):
            P.op("act", lambda e: e.activation(out=sq[:], in_=XB[:], func=AF.Square),
                 reads=[("XB", k) for k in range(8)], writes=["sq"])
            ps, pk = mmbank()
            P.op("pe", proj(ps, lambda k: ones_bf, lambda k: sq[:, k, :], 8), reads=["sq"] + CB, writes=[pk])
            P.op("act", lambda e: e.activation(out=rt[:], in_=ps, func=AF.Ln, scale=1.0 / ndiv, bias=EPS),
                 reads=[pk], writes=["rt"])
            P.op("act", lambda e: e.activation(out=rstd[:], in_=rt[:], func=AF.Exp, scale=-0.5), reads=["rt"], writes=["rstd"])
            for k in range(8):
                P.op("dve", lambda e, k=k: e.scalar_tensor_tensor(
                    out=dst_bf[:, k, :], in0=XB[:, k, :], scalar=vecs[:, vcol + k:vcol + k + 1], in1=rstd[:],
                    op0=ALU.mult, op1=ALU.mult),
                    reads=[("XB", k), "rstd", "vecs"], writes=[(tag, k)])

        chunkc = [0]

        for tb in range(ntb):
            t0 = tb * TB
            PL = "pool" if (USE_POOL and tb > 0) else "dve"
            P.op("dve", lambda e: e.memset(dummy[:, 2:3], 0.0), writes=["G3", "dummy2"])
            P.dma("sp", "ld_x", lambda e, t0=t0: e.dma_start(out=XB[:], in_=xT_v[:, :, t0:t0 + TB]),
                  writes=[("XB", k) for k in range(8)])
            rmsnorm_to(hT, V_NMIX, float(D), "hT")
            HT = [("hT", k) for k in range(8)]
            if tb == 0:
                dbg("hT", hT[:], [128, 8, TB], HT)

            for half in range(2):
                c0 = half * 512
                wB, kB = wload(v_in[:, :, C_SC + c0:C_SC + c0 + 512], [128, 8, 512], K_SC)
                wC, kC = wload(v_in[:, :, C_SC + 1024 + c0:C_SC + 1024 + c0 + 512], [128, 8, 512], K_SC)
                wX, kX = wload(v_in[:, :, C_SC + 2048 + c0:C_SC + 2048 + c0 + 512], [128, 8, 512], K_SC)
                for c4 in range(4):
                    cb = half * 4 + c4
                    cs_ = slice(c4 * 128, (c4 + 1) * 128)
                    psB, pkB = mmbank()
                    P.op("pe", proj(psB, lambda k, w=wB, s=cs_: w[:, k, s], lambda k: hT[:, k, :], 8),
                         reads=HT + kB, writes=[pkB])
                    psC, pkC = mmbank()
                    P.op("pe", proj(psC, lambda k, w=wC, s=cs_: w[:, k, s], lambda k: hT[:, k, :], 8),
                         reads=HT + kC, writes=[pkC])
                    psX, pkX = mmbank()
                    P.op("pe", proj(psX, lambda k, w=wX, s=cs_: w[:, k, s], lambda k: hT[:, k, :], 8),
                         reads=HT + kX, writes=[pkX])
                    P.op("act", lambda e, ps=psB: e.activation(out=Bsb[:], in_=ps, func=AF.Identity),
                         reads=[pkB], writes=["Bsb"])
                    P.op("act", lambda e, ps=psC: e.activation(out=Csb[:], in_=ps, func=AF.Identity),
                         reads=[pkC], writes=["Csb"])
                    P.op("dve", lambda e, cb=cb: e.tensor_copy(out=usc[:, 0:2], in_=car_sc[:, cb, :]),
                         reads=["car_sc"], writes=["usc_h"])
                    P.op("dve", lambda e, ps=psX: e.tensor_tensor(out=usc[:, 2:TB + 2], in0=ps, in1=Csb[:], op=ALU.mult),
                         reads=[pkX, "Csb"], writes=["usc"])
                    P.op("dve", lambda e, cb=cb: e.tensor_copy(out=car_sc[:, cb, :], in_=usc[:, TB:TB + 2]),
                         reads=["usc"], writes=["car_sc"])
                    wv = V_SCW + cb * 3
                    P.op("dve", lambda e, wv=wv: e.tensor_tensor(
                        out=dgs[:], in0=ident_bf.unsqueeze(1).broadcast_to([128, 3, 128]),
                        in1=vecs[:, wv:wv + 3].unsqueeze(2).broadcast_to([128, 3, 128]), op=ALU.mult),
                        reads=["vecs"] + CB, writes=["dgs"])
                    psV, pkV = mmbank()

                    def scconv(e, psV=psV):
                        last = None
                        for tap in range(3):
                            last = e.matmul(psV, dgs[:, tap, :], usc[:, tap:tap + TB], start=(tap == 0), stop=(tap == 2))
                        return last
                    P.op("pe", scconv, reads=["dgs", "usc", "usc_h"], writes=[pkV])
                    P.op("dve", lambda e, cb=cb, psV=psV: e.tensor_tensor(out=yaT[:, cb, :], in0=psV, in1=Bsb[:], op=ALU.mult),
                         reads=[pkV, "Bsb"], writes=[("yaT", cb)])
            YA = [("yaT", k) for k in range(8)]
            if tb == 0:
                dbg("yaT", yaT[:], [128, 8, TB], YA)

            ps, pk = mmbank()

            def dtmm(e, ps=ps):
                last = None
                for cc in range(4):
                    for k in range(8):
                        last = e.matmul(ps[:, cc * 32:(cc + 1) * 32], hT[:, k, cc * 128:(cc + 1) * 128], wdt[:, k, :],
                                        start=(k == 0), stop=(k == 7))
                return last
            P.op("pe", dtmm, reads=HT + ["wdt"], writes=[pk])
            dtb_b = vecs[:, V_DTB:V_DTB + 32].unsqueeze(1).broadcast_to([128, 4, 32])
            A_b = Abc[:].unsqueeze(1).broadcast_to([128, 4, 32])
            P.op("dve", lambda e, ps=ps: e.tensor_tensor(out=T1, in0=ps[:, 0:128].rearrange("p (c h) -> p c h", c=4),
                                                         in1=dtb_b, op=ALU.add),
                 reads=[pk, "vecs"], writes=["T1"])
            P.op("act", lambda e: e.activation(out=TE, in_=T1, func=AF.Exp), reads=["T1"], writes=["TE"])
            P.op("act", lambda e: e.activation(out=TDT, in_=TE, func=AF.Ln, bias=1.0), reads=["TE"], writes=["TDT"])
            P.op("dve", lambda e: e.tensor_tensor(out=TDA, in0=TDT, in1=A_b, op=ALU.mult),
                 reads=["TDT", "Abc"], writes=["TDA"])
            P.op("dve", lambda e: e.tensor_copy(out=dAhi[:], in_=tbl[:, 3, :]), reads=["TDA"], writes=["dAhi"])
            P.op("dve", lambda e: e.tensor_tensor(out=dAlo[:], in0=tbl[:, 3, :], in1=dAhi[:], op=ALU.subtract),
                 reads=["TDA", "dAhi"], writes=["dAlo"])
            ps2, pk2 = mmbank()

            def csmm(e, ps2=ps2):
                e.matmul(ps2[:, 0:128], tri, tbl[:, 3, :], start=True, stop=True)
                return e.matmul(ps2[:, 128:256], ones, tbl[:, 3, :], start=True, stop=True)
            P.op("pe", csmm, reads=["TDA", "cst"], writes=[pk2])
            P.op("act", lambda e, ps2=ps2: e.activation(out=tbl[:, 4, :], in_=ps2[:, 0:128], func=AF.Identity),
                 reads=[pk2], writes=["TCS"])
            P.op("dve", lambda e, ps2=ps2: e.tensor_tensor(out=tbl[:, 5, :], in0=ps2[:, 128:256], in1=tbl[:, 4, :],
                                                           op=ALU.subtract),
                 reads=[pk2, "TCS"], writes=["T2"])
            P.op("act", lambda e: e.activation(out=tbl[:, 6, :], in_=tbl[:, 5, :], func=AF.Exp),
                 reads=["T2"], writes=["TDEC"])
            P.op("act", lambda e, ps2=ps2: e.activation(out=tbl[:, 7, :], in_=ps2[:, 128:256], func=AF.Exp),
                 reads=[pk2], writes=["TCD"])
            if tb == 0:
                dbg("tbl", tbl[:], [128, 8, 128], ["T1", "TE", "TDT", "TDA", "TCS", "T2", "TDEC", "TCD"])

            P.op("dve", lambda e: e.memset(dummy[:, 0:1], 0.0), writes=["BIG1", "dummy0"])
            v_zx = s_in[:, C_Z:C_B].rearrange("(k p) (s n) -> p s k n", p=128, s=2)
            v_bc = s_in[:, C_B:C_DT].rearrange("(k p) (s n) -> p s k n", p=128, s=2)

            def group_gen(g, si, wBC, kBC, wZX, kZX):
                B_ = GS[si]
                zs, u4, dg, xsT, yT, BT, CT, sq2 = (B_["zs"], B_["u4"], B_["dg"], B_["xsT"], B_["yT"], B_["BT"],
                                                     B_["CT"], B_["sq2"])
                rt_, rstd_ = B_["rt"], B_["rstd"]
                S0, S1, S2, S3 = PS[si]
                kS = lambda n: "S%d" % n
                K = lambda n, *a: (n, si) + tuple(a)
                go = (g % 2) * 128
                for (m0, nm, chb, wk) in ((0, 8, 2 * g, [K("dg", 0), K("dg", 1)]), (8, 4, 16 + g, [K("dg", 2)]),
                                          (12, 4, 24 + g, [K("dg", 3)])):
                    wv0 = V_SSW + chb * 4
                    P.op("dve", lambda e, m0=m0, nm=nm, wv0=wv0: e.tensor_tensor(
                        out=dg[:, m0:m0 + nm, :], in0=ident_bf.unsqueeze(1).broadcast_to([128, nm, 128]),
                        in1=vecs[:, wv0:wv0 + nm].unsqueeze(2).broadcast_to([128, nm, 128]), op=ALU.mult),
                        reads=["vecs"] + CB, writes=wk)
                for j in range(2):
                    ps, pk = mmbank()
                    P.op("pe", proj(ps, lambda k, j=j, w=wZX: w[:, 0, k, j * 128:(j + 1) * 128], lambda k: hT[:, k, :], 8),
                         reads=HT + kZX, writes=[pk])
                    P.op("act", lambda e, ps=ps, j=j: e.activation(out=zs[:, j, :], in_=ps, func=AF.Silu),
                         reads=[pk], writes=[K("zs", j)])
                    yield
                P.op(PL, lambda e: e.tensor_copy(out=u4[:, :, 0:3], in_=car_ss[:, g, :, :]),
                     reads=[("car_ss", g)], writes=[K("u4_h")])
                for b in range(4):
                    ps, pk = mmbank()
                    if b < 2:
                        lhs = (lambda k, b=b, w=wZX: w[:, 1, k, b * 128:(b + 1) * 128])
                        rk = kZX
                    else:
                        lhs = (lambda k, b=b, go=go, w=wBC: w[:, b - 2, k, go:go + 128])
                        rk = kBC
                    P.op("pe", proj(ps, lhs, lambda k: hT[:, k, :], 8), reads=HT + rk, writes=[pk])
                    P.op("act", lambda e, ps=ps, b=b: e.activation(out=u4[:, b, 3:TB + 3], in_=ps, func=AF.Identity),
                         reads=[pk], writes=[K("u4", b)])
                    yield
                P.op(PL, lambda e: e.tensor_copy(out=car_ss[:, g, :, :], in_=u4[:, :, TB:TB + 3]),
                     reads=[K("u4", b) for b in range(4)], writes=[("car_ss", g)])
                for b in range(4):
                    ch = (2 * g + b) if b < 2 else (16 + g if b == 2 else 24 + g)
                    wv = V_SSW + ch * 4
                    bv = V_SSB + ch
                    ps, pk = mmbank()

                    def ssconv(e, ps=ps, b=b):
                        last = None
                        for tap in range(4):
                            last = e.matmul(ps, dg[:, b * 4 + tap, :], u4[:, b, tap:tap + TB], start=(tap == 0), stop=(tap == 3))
                        return last
                    P.op("pe", ssconv, reads=[K("dg", b), K("u4", b), K("u4_h")], writes=[pk])
                    if b < 2:
                        P.op("act", lambda e, b=b, ps=ps, bv=bv: e.activation(out=xsT[:, b, :], in_=ps, func=AF.Silu,
                                                                              bias=vecs[:, bv:bv + 1]),
                             reads=[pk, "vecs"], writes=[K("xsT", b)])
                    elif b == 2:
                        P.op("act", lambda e, ps=ps, bv=bv: e.activation(out=BT, in_=ps, func=AF.Silu, bias=vecs[:, bv:bv + 1]),
                             reads=[pk, "vecs"], writes=[K("BT")])
                    else:
                        P.op("act", lambda e, ps=ps, bv=bv: e.activation(out=CT, in_=ps, func=AF.Silu, bias=vecs[:, bv:bv + 1]),
                             reads=[pk, "vecs"], writes=[K("CT")])
                    yield
                if tb == 0 and g == 0:
                    dbg("xsT", xsT, [128, 2, TB], [K("xsT", 0), K("xsT", 1)])
                    dbg("BT", BT, [128, TB], [K("BT")])
                    dbg("CT", CT, [128, TB], [K("CT")])
                    dbg("zs", zs, [128, 2, TB], [K("zs", 0), K("zs", 1)])
                par = si
                R, LT, ecs, CBm, WT, CsT, xdt, xdtd, Btok = (R_[par], LT_[par], ecs_[par], CBm_[par], WT_[par],
                                                             CsT_[par], xdt_[par], xdtd_[par], Btok_[par])
                for cc in range(4):
                    tc_ = slice(cc * 128, (cc + 1) * 128)

                    def rbuild(e, cc=cc):
                        last = None
                        for s_, src in ((0, dAhi), (1, dAlo)):
                            for h in range(4):
                                c0 = cc * 32 + 4 * g + h
                                last = e.tensor_scalar(out=R[:, s_, h, :], in0=tri, scalar1=src[:, c0:c0 + 1],
                                                       scalar2=None, op0=ALU.mult)
                        return last
                    P.op(PL, rbuild, reads=["dAhi", "dAlo", "cst"], writes=[("R", par)])

                    def pe1(e, tc_=tc_):
                        e.transpose(S2[:, 0:128], xsT[:, 0, tc_], ident)
                        e.transpose(S2[:, 128:256], xsT[:, 1, tc_], ident)
                        e.matmul(S2[:, 256:384], BT[:, tc_], CT[:, tc_], start=True, stop=True)
                        return e.matmul(S2[:, 384:512], BT[:, tc_], ident_bf, start=True, stop=True)
                    P.op("pe", pe1, reads=[K("xsT", 0), K("xsT", 1), K("BT"), K("CT")] + CB, writes=[kS(2)])

                    def pe2(e):
                        rh = R[:, 0].rearrange("p h i -> p (h i)")
                        rl_ = R[:, 1].rearrange("p h i -> p (h i)")
                        e.matmul(S0[:], U_bf, rh, start=True, stop=False)
                        e.matmul(S0[:], U_bf, rl_, start=False, stop=True)
                        e.matmul(S1[:], ones_bf, rh, start=True, stop=False)
                        return e.matmul(S1[:], ones_bf, rl_, start=False, stop=True)
                    P.op("pe", pe2, reads=[("R", par)] + CB, writes=[kS(0), kS(1)])
                    P.op("act", lambda e: e.activation(out=LT[:].rearrange("p h i -> p (h i)"), in_=S0[:], func=AF.Exp),
                         reads=[kS(0)], writes=[("LT", par)])
                    P.op("act", lambda e: e.activation(out=ecs[:].rearrange("p h i -> p (h i)"), in_=S1[:], func=AF.Exp),
                         reads=[kS(1)], writes=[("ecs", par)])
                    P.op("dve", lambda e: e.tensor_tensor(out=CBm[:], in0=S2[:, 256:384], in1=tri, op=ALU.mult),
                         reads=[kS(2), "cst"], writes=[("CBm", par)])
                    P.op("dve", lambda e, cc=cc: e.tensor_tensor(
                        out=xdt[:], in0=S2[:, 0:256].rearrange("p (h q) -> p h q", h=4),
                        in1=TDT[:, cc, 4 * g:4 * g + 4].unsqueeze(2).broadcast_to([128, 4, 64]), op=ALU.mult),
                        reads=[kS(2), "TDT"], writes=[("xdt", par)])
                    P.op("act", lambda e: e.activation(out=Btok[:], in_=S2[:, 384:512], func=AF.Identity),
                         reads=[kS(2)], writes=[("Btok", par)])
                    yield
                    P.op("dve", lambda e: e.tensor_tensor(
                        out=WT[:], in0=LT[:], in1=CBm[:].unsqueeze(1).broadcast_to([128, 4, 128]), op=ALU.mult),
                        reads=[("LT", par), ("CBm", par)], writes=[("WT", par)])
                    P.op(PL, lambda e, tc_=tc_: e.tensor_tensor(
                        out=CsT[:], in0=ecs[:], in1=CT[:, tc_].unsqueeze(1).broadcast_to([128, 4, 128]), op=ALU.mult),
                        reads=[("ecs", par), K("CT")], writes=[("CsT", par)])
                    P.op(PL, lambda e, cc=cc: e.tensor_tensor(
                        out=xdtd[:], in0=xdt[:],
                        in1=TDEC[:, cc, 4 * g:4 * g + 4].unsqueeze(2).broadcast_to([128, 4, 64]), op=ALU.mult),
                        reads=[("xdt", par), "TDEC"], writes=[("xdtd", par)])
                    yield

                    def pe3(e):
                        for h in range(4):
                            o = S3[64 * (h % 2):64 * (h % 2) + 64, (h // 2) * 128:(h // 2) * 128 + 128]
                            e.matmul(o, xdt[:, h, :], WT[:, h, :], start=True, stop=False)
                            e.matmul(o, stb[:, g, h * 64:(h + 1) * 64], CsT[:, h, :], start=False, stop=True)
                        return e.matmul(S3[:, 256:512], Btok[:], xdtd[:].rearrange("p h q -> p (h q)"), start=True, stop=True)
                    P.op("pe", pe3, reads=[("xdt", par), ("WT", par), ("CsT", par), ("Btok", par), ("xdtd", par), ("stb", g)],
                         writes=[kS(3)])

                    def yev(e, tc_=tc_):
                        last = None
                        for j in range(2):
                            dv = V_DCOL + 2 * g + j
                            last = e.scalar_tensor_tensor(out=yT[:, j, tc_], in0=xsT[:, j, tc_], scalar=vecs[:, dv:dv + 1],
                                                          in1=S3[:, j * 128:(j + 1) * 128], op0=ALU.mult, op1=ALU.add)
                        return last
                    P.op("dve", yev, reads=[kS(3), K("xsT", 0), K("xsT", 1), "vecs"], writes=[K("yT", cc)])
                    P.op(PL, lambda e, cc=cc: e.tensor_tensor(
                        out=stf[:, g, :].rearrange("p (h q) -> p h q", h=4),
                        in0=stf[:, g, :].rearrange("p (h q) -> p h q", h=4),
                        in1=TCD[:, cc, 4 * g:4 * g + 4].unsqueeze(2).broadcast_to([128, 4, 64]), op=ALU.mult),
                        reads=[("stf", g), "TCD"], writes=[("stf", g)])
                    P.op("dve", lambda e: e.tensor_tensor(out=stf[:, g, :], in0=stf[:, g, :], in1=S3[:, 256:512], op=ALU.add),
                         reads=[("stf", g), kS(3)], writes=[("stf", g)])
                    P.op("act", lambda e: e.activation(out=stb[:, g, :], in_=stf[:, g, :], func=AF.Identity),
                         reads=[("stf", g)], writes=[("stb", g)])
                    yield
                YT = [K("yT", cc) for cc in range(4)]
                if tb == 0 and g == 0:
                    dbg("yT", yT, [128, 2, TB], YT)
                P.op(PL, lambda e: e.tensor_tensor(out=yT, in0=yT, in1=zs, op=ALU.mult),
                     reads=YT + [K("zs", 0), K("zs", 1)], writes=YT)
                P.op("act", lambda e: e.activation(out=sq2, in_=yT, func=AF.Square), reads=YT, writes=[K("sq2")])
                ps, pk = mmbank()
                P.op("pe", proj(ps, lambda k: ones_bf, lambda k: sq2[:, k, :], 2), reads=[K("sq2")] + CB, writes=[pk])
                yield
                P.op("act", lambda e, ps=ps: e.activation(out=rt_[:], in_=ps, func=AF.Ln, scale=1.0 / 256.0, bias=EPS),
                     reads=[pk], writes=[B_["rtk"]])
                P.op("act", lambda e: e.activation(out=rstd_[:], in_=rt_[:], func=AF.Exp, scale=-0.5), reads=[B_["rtk"]], writes=[B_["rstdk"]])
                for j in range(2):
                    nv = V_SNW + 2 * g + j
                    P.op("dve", lambda e, j=j, nv=nv: e.scalar_tensor_tensor(
                        out=ybT[:, 2 * g + j, :], in0=yT[:, j, :], scalar=vecs[:, nv:nv + 1], in1=rstd_[:],
                        op0=ALU.mult, op1=ALU.mult),
                        reads=YT + [B_["rstdk"], "vecs"], writes=[("ybT", 2 * g + j)])
                yield

            for gp in range(4):
                g0 = 2 * gp
                wBC, kBC = wload(v_bc[:, :, :, 128 * g0:128 * g0 + 256], [128, 2, 8, 256], K_SSD)
                gens = []
                for si in range(2):
                    g = g0 + si
                    wZX, kZX = wload(v_zx[:, :, :, 256 * g:256 * g + 256], [128, 2, 8, 256], K_SSD)
                    gens.append(group_gen(g, si, wBC, kBC, wZX, kZX))
                alive = list(gens)
                while alive:
                    for gn in list(alive):
                        try:
                            next(gn)
                        except StopIteration:
                            alive.remove(gn)
            YB = [("ybT", k) for k in range(16)]
            if tb == 0:
                dbg("ybT", ybT[:], [128, 16, TB], YB)

            P.op("dve", lambda e: e.memset(dummy[:, 3:4], 0.0), writes=["G2", "dummy3"])
            v_g = s_in[:, C_G:DIN].rearrange("(k p) (s n) -> p s k n", p=128, s=2)
            v_bsc = s_bsc.rearrange("(k p) n -> p k n", p=128)
            v_bssm = s_bssm.rearrange("(k p) n -> p k n", p=128)
            for obp in range(4):
                o0 = obp * 256
                wG, kG = wload(v_g[:, :, :, o0:o0 + 256], [128, 2, 8, 256], K_G)
                wA, kA = wload(v_bsc[:, :, o0:o0 + 256], [128, 8, 256], K_BSC)
                wS, kS = wload(v_bssm[:, :, o0:o0 + 256], [128, 16, 256], K_BSSM)
                for o2 in range(2):
                    ob = obp * 2 + o2
                    os_ = slice(o2 * 128, (o2 + 1) * 128)
                    for s in range(2):
                        psg, pkg = mmbank()
                        P.op("pe", proj(psg, lambda k, s=s, os_=os_, w=wG: w[:, s, k, os_], lambda k: hT[:, k, :], 8),
                             reads=HT + kG, writes=[pkg])
                        psb, pkb = mmbank()
                        if s == 0:
                            P.op("pe", proj(psb, lambda k, os_=os_, w=wA: w[:, k, os_], lambda k: yaT[:, k, :], 8),
                                 reads=YA + kA, writes=[pkb])
                        else:
                            P.op("pe", proj(psb, lambda k, os_=os_, w=wS: w[:, k, os_], lambda k: ybT[:, k, :], 16),
                                 reads=YB + kS, writes=[pkb])
                        bg = V_BG + s * 8 + ob
                        P.op("act", lambda e, psg=psg, bg=bg: e.activation(out=gsb[:], in_=psg, func=AF.Sigmoid,
                                                                            bias=vecs[:, bg:bg + 1]),
                             reads=[pkg, "vecs"], writes=["gsb"])
                        msb = m1sb if s == 0 else m2sb
                        P.op("dve", lambda e, psb=psb, msb=msb: e.tensor_tensor(out=msb[:], in0=psb, in1=gsb[:], op=ALU.mult),
                             reads=[pkb, "gsb"], writes=["m1sb" if s == 0 else "m2sb"])
                    P.op(PL, lambda e, ob=ob: e.tensor_tensor(out=mgT[:, ob, :], in0=m1sb[:], in1=m2sb[:], op=ALU.add),
                         reads=["m1sb", "m2sb"], writes=[("mgT", ob)])
            MG = [("mgT", k) for k in range(8)]
            if tb == 0:
                dbg("mgT", mgT[:], [128, 8, TB], MG)

            v_out = s_out.rearrange("(k p) n -> p k n", p=128)
            for oh in range(2):
                wO, kO = wload(v_out[:, :, oh * 512:(oh + 1) * 512], [128, 8, 512], K_OUT)
                for o4 in range(4):
                    ob = oh * 4 + o4
                    ps, pk = mmbank()
                    P.op("pe", proj(ps, lambda k, o4=o4, wO=wO: wO[:, k, o4 * 128:(o4 + 1) * 128], lambda k: mgT[:, k, :], 8),
                         reads=MG + kO, writes=[pk])
                    P.op("dve", lambda e, ps=ps, ob=ob: e.tensor_tensor(out=XB[:, ob, :], in0=ps, in1=XB[:, ob, :], op=ALU.add),
                         reads=[pk, ("XB", ob)], writes=[("XB", ob)])
            if tb == 0:
                dbg("x1T", XB[:], [128, 8, TB], [("XB", k) for k in range(8)])

            P.op("dve", lambda e: e.memset(dummy[:, 2:3], 0.0), writes=["G2", "G3", "dummy2"])
            rmsnorm_to(h2T, V_NMLP, float(D), "h2T")
            H2 = [("h2T", k) for k in range(8)]
            P.op("dve", lambda e: e.memset(dummy[:, 1:2], 0.0), writes=["BIG1", "dummy1"])
            v_m1 = s_m1.rearrange("(k p) n -> p k n", p=128)
            for fb in range(8):
                wM, kM = wload(v_m1[:, :, fb * 512:(fb + 1) * 512], [128, 8, 512], K_M1)
                for f4 in range(4):
                    f = fb * 4 + f4
                    ps, pk = mmbank()
                    P.op("pe", proj(ps, lambda k, f4=f4, wM=wM: wM[:, k, f4 * 128:(f4 + 1) * 128], lambda k: h2T[:, k, :], 8),
                         reads=H2 + kM, writes=[pk])
                    P.op("act", lambda e, ps=ps: e.activation(out=rl[:], in_=ps, func=AF.Relu), reads=[pk], writes=["rl"])
                    P.op("act", lambda e, f=f: e.activation(out=aT[:, f, :], in_=rl[:], func=AF.Square),
                         reads=["rl"], writes=[("aT", f)])
            AT = [("aT", f) for f in range(32)]
            v_m2 = s_m2.rearrange("(k p) n -> p k n", p=128)
            for obp in range(4):
                o0 = obp * 256
                wh = []
                for fh in range(2):
                    wh.append(wload(v_m2[:, fh * 16:(fh + 1) * 16, o0:o0 + 256], [128, 16, 256], K_M2))
                for o2 in range(2):
                    ob = obp * 2 + o2
                    os_ = slice(o2 * 128, (o2 + 1) * 128)
                    ps, pk = mmbank()
                    P.op("pe", proj(ps, lambda k, os_=os_, wh=wh: wh[k // 16][0][:, k % 16, os_], lambda k: aT[:, k, :], 32),
                         reads=AT + wh[0][1] + wh[1][1], writes=[pk])
                    P.op("dve", lambda e, ps=ps, ob=ob: e.tensor_tensor(out=XB[:, ob, :], in0=ps, in1=XB[:, ob, :], op=ALU.add),
                         reads=[pk, ("XB", ob)], writes=[("XB", ob)])

            P.op("act", lambda e: e.activation(out=sq[:], in_=XB[:], func=AF.Square),
                 reads=[("XB", k) for k in range(8)], writes=["sq"])
            ps, pk = mmbank()
            P.op("pe", proj(ps, lambda k: ones_bf, lambda k: sq[:, k, :], 8), reads=["sq"] + CB, writes=[pk])
            P.op("act", lambda e, ps=ps: e.activation(out=rt[:], in_=ps, func=AF.Ln, scale=1.0 / D, bias=EPS),
                 reads=[pk], writes=["rt"])
            P.op("act", lambda e: e.activation(out=rstd[:], in_=rt[:], func=AF.Exp, scale=-0.5), reads=["rt"], writes=["rstd"])
            for k in range(8):
                P.op("dve", lambda e, k=k: e.scalar_tensor_tensor(
                    out=XB[:, k, :], in0=XB[:, k, :], scalar=vecs[:, V_NFIN + k:V_NFIN + k + 1], in1=rstd[:],
                    op0=ALU.mult, op1=ALU.mult),
                    reads=[("XB", k), "rstd", "vecs"], writes=[("XB", k)])
            P.dma("sp", "st_out", lambda e, t0=t0: e.dma_start(out=outT_v[:, :, t0:t0 + TB], in_=XB[:]),
                  reads=[("XB", k) for k in range(8)], writes=[("outT", tb)])

        fin_reads = [("outT", tb) for tb in range(ntb)] + [("dbg", n) for n in dbg_out]
        P.op("sp", lambda e: None, reads=fin_reads, writes=["fin"])

        block = st.enter_context(nc.Block())
        P.emit(nc, block, st)
    return nc, dbg_out


def _pack_inputs(inputs):
    f = lambda a: np.ascontiguousarray(np.asarray(a, dtype=np.float32))
    vecs = np.zeros((128, NV), np.float32)
    vecs[:, V_NMIX:V_NMIX + 8] = f(inputs["norm_mix"]).reshape(8, 128).T
    vecs[:, V_NMLP:V_NMLP + 8] = f(inputs["norm_mlp"]).reshape(8, 128).T
    vecs[:, V_NFIN:V_NFIN + 8] = f(inputs["norm_final"]).reshape(8, 128).T
    vecs[:, V_BG:V_BG + 16] = f(inputs["b_gate"]).reshape(16, 128).T
    vecs[:, V_SCW:V_SCW + 24] = f(inputs["sc_conv_w"]).reshape(3, 8, 128).transpose(2, 1, 0).reshape(128, 24)
    vecs[:, V_SSW:V_SSW + 128] = f(inputs["ssm_conv_w"]).reshape(4, 32, 128).transpose(2, 1, 0).reshape(128, 128)
    vecs[:, V_SSB:V_SSB + 32] = f(inputs["ssm_conv_b"]).reshape(32, 128).T
    vecs[:, V_SNW:V_SNW + 16] = f(inputs["ssm_norm_w"]).reshape(16, 128).T
    vecs[:, V_DCOL:V_DCOL + 16] = np.repeat(f(inputs["D_skip"]), 64).reshape(16, 128).T
    vecs[:, V_DTB:V_DTB + 32] = np.tile(f(inputs["dt_bias"])[None, :], (128, 1))
    vecs[:, V_ALOG:V_ALOG + 32] = np.tile(f(inputs["A_log"])[None, :], (128, 1))
    k = np.arange(128)
    consts = np.zeros((128, 512), np.float32)
    consts[:, 0:128] = np.eye(128, dtype=np.float32)
    consts[:, 128:256] = (k[:, None] <= k[None, :]).astype(np.float32)
    consts[:, 256:384] = (k[:, None] > k[None, :]).astype(np.float32)
    consts[:, 384:512] = 1.0
    x = f(inputs["x"])
    xT = np.ascontiguousarray(x.transpose(0, 2, 1))
    common = {
        "w_in": f(inputs["w_in"]), "w_bsc": f(inputs["w_branch_sc"]), "w_bssm": f(inputs["w_branch_ssm"]),
        "w_out": f(inputs["w_out"]), "w_m1": f(inputs["w_mlp1"]), "w_m2": f(inputs["w_mlp2"]),
        "vecs": vecs, "consts": consts,
    }
    return [dict(common, xT=xT[i]) for i in range(8)]


def kernel(**inputs):
    in_maps = _pack_inputs(inputs)
    nc, _ = build_nc()
    res = run_bass_kernel_spmd(nc, in_maps, core_ids=list(range(8)))
    out = np.stack([np.ascontiguousarray(res.results[i]["outT"].T) for i in range(8)], axis=0)
    return out.astype(np.float32)
```

```python
import numpy as np
import concourse.bass as bass
import concourse.mybir as mybir
from concourse.bass_utils import run_bass_kernel_spmd

F32 = mybir.dt.float32
BF16 = mybir.dt.bfloat16
AF = mybir.ActivationFunctionType
ALU = mybir.AluOpType

D = 1024
T = 2048
TB = 512
NTB = T // TB
DFF = 4096
DIN = 11296
EPS = 1e-6
C_SC = 0
C_Z = 3072
C_X = 5120
C_B = 7168
C_C = 8192
C_DT = 9216
C_G = 9248
NW = 4
SAME_ENG_SYNC = True
USE_POOL = False

V_NMIX, V_NMLP, V_NFIN, V_BG, V_SCW, V_SSW, V_SSB, V_SNW, V_DCOL, V_DTB, V_ALOG = (
    0, 8, 16, 24, 40, 64, 192, 224, 240, 256, 288)
NV = 320

ALIAS = {"sq": "G2", "mgT": "G2", "hT": "G3", "h2T": "G3", "zs": "BIG1", "u4": "BIG1", "u4_h": "BIG1", "dg": "BIG1", "xsT": "BIG1", "yT": "BIG1",
         "BT": "BIG1", "CT": "BIG1", "sq2": "BIG1", "aT": "BIG1"}


class Prog:
    def __init__(self):
        self.ops = []

    def _norm(self, keys):
        out = []
        for k in keys:
            out.append(k)
            base = k[0] if isinstance(k, tuple) else k
            if base in ALIAS:
                out.append(ALIAS[base])
        return tuple(dict.fromkeys(out))

    def _alias(self, keys):
        out = []
        for k in keys:
            base = k[0] if isinstance(k, tuple) else k
            if base in ALIAS:
                out.append(ALIAS[base])
        return out

    def op(self, eng, fn, reads=(), writes=()):
        rd = self._norm(tuple(reads) + tuple(self._alias(writes)))
        self.ops.append(dict(eng=eng, fn=fn, reads=rd, writes=tuple(writes), sem=None))

    def dma(self, queue, sem, fn, reads=(), writes=()):
        rd = self._norm(tuple(reads) + tuple(self._alias(writes)))
        self.ops.append(dict(eng=queue, fn=fn, reads=rd, writes=tuple(writes), sem=sem))

    def emit(self, nc, block, stack):
        ops = self.ops
        last_writer, readers = {}, {}
        for i, o in enumerate(ops):
            deps = set()
            for k in o["reads"]:
                if k in last_writer:
                    deps.add(last_writer[k])
            for k in o["writes"]:
                if k in last_writer:
                    deps.add(last_writer[k])
                deps.update(readers.get(k, ()))
            deps.discard(i)
            o["deps"] = deps
            for k in o["reads"]:
                readers.setdefault(k, []).append(i)
            for k in o["writes"]:
                last_writer[k] = i
                readers[k] = []
        for o in ops:
            o["signal"] = o["sem"] is not None
        for o in ops:
            for d in o["deps"]:
                p = ops[d]
                if p["sem"] is None:
                    if p["eng"] == o["eng"] and (p["eng"] == "pe" or not SAME_ENG_SYNC) and o["sem"] is None:
                        continue
                    p["signal"] = True
        cnt = {}
        semnames = set()
        LIM = 1000
        for o in ops:
            if o["sem"] is not None:
                s = o["sem"]
                o["semname"] = s
                cnt[s] = cnt.get(s, 0) + 16
                o["tick"] = cnt[s]
                semnames.add(s)
            elif o["signal"]:
                base = "eng_" + o["eng"]
                c = cnt.get(base, 0)
                cnt[base] = c + 1
                s = "%s_%d" % (base, c // LIM)
                o["semname"] = s
                o["tick"] = c % LIM + 1
                semnames.add(s)
            else:
                o["semname"] = None
        sems = {s: stack.enter_context(nc.semaphore(s)) for s in sorted(semnames)}
        self.maxcnt = cnt

        def run_engine(engname, e):
            known = {}
            for o in ops:
                if o["eng"] != engname:
                    continue
                need = {}
                for d in o["deps"]:
                    p = ops[d]
                    if p["sem"] is None and p["eng"] == engname and o["sem"] is None:
                        if engname == "pe" or not SAME_ENG_SYNC:
                            continue
                    if not p["signal"]:
                        continue
                    s = p["semname"]
                    need[s] = max(need.get(s, 0), p["tick"])
                for s, v in need.items():
                    if known.get(s, 0) < v:
                        e.wait_ge(sems[s], v)
                        known[s] = v
                inst = o["fn"](e)
                if o["signal"]:
                    assert inst is not None
                    inst.then_inc(sems[o["semname"]], 16 if o["sem"] is not None else 1)

        @block.tensor
        def _(e):
            run_engine("pe", e)

        @block.scalar
        def _(e):
            run_engine("act", e)

        @block.vector
        def _(e):
            run_engine("dve", e)

        @block.gpsimd
        def _(e):
            run_engine("pool", e)

        @block.sync
        def _(e):
            run_engine("sp", e)


def build_nc(debug=False, ntb=NTB):
    import contextlib
    nc = bass.Bass("TRN2", target_bir_lowering=False)
    P = Prog()
    dr = {}
    dr["xT"] = nc.dram_tensor("xT", [D, T], F32, kind="ExternalInput").ap()
    dr["w_in"] = nc.dram_tensor("w_in", [D, DIN], F32, kind="ExternalInput").ap()
    dr["w_bsc"] = nc.dram_tensor("w_bsc", [D, D], F32, kind="ExternalInput").ap()
    dr["w_bssm"] = nc.dram_tensor("w_bssm", [2 * D, D], F32, kind="ExternalInput").ap()
    dr["w_out"] = nc.dram_tensor("w_out", [D, D], F32, kind="ExternalInput").ap()
    dr["w_m1"] = nc.dram_tensor("w_m1", [D, DFF], F32, kind="ExternalInput").ap()
    dr["w_m2"] = nc.dram_tensor("w_m2", [DFF, D], F32, kind="ExternalInput").ap()
    dr["vecs"] = nc.dram_tensor("vecs", [128, NV], F32, kind="ExternalInput").ap()
    dr["consts"] = nc.dram_tensor("consts", [128, 512], F32, kind="ExternalInput").ap()
    outT = nc.dram_tensor("outT", [D, T], F32, kind="ExternalOutput").ap()
    s_in = nc.dram_tensor("s_in", [D, DIN], BF16, kind="Internal").ap()
    s_bsc = nc.dram_tensor("s_bsc", [D, D], BF16, kind="Internal").ap()
    s_bssm = nc.dram_tensor("s_bssm", [2 * D, D], BF16, kind="Internal").ap()
    s_out = nc.dram_tensor("s_out", [D, D], BF16, kind="Internal").ap()
    s_m1 = nc.dram_tensor("s_m1", [D, DFF], BF16, kind="Internal").ap()
    s_m2 = nc.dram_tensor("s_m2", [DFF, D], BF16, kind="Internal").ap()
    dbg_out = {}

    with contextlib.ExitStack() as st:
        def sb(name, shape, dt):
            return st.enter_context(nc.sbuf_tensor(name, shape, dt))

        vecs = sb("vecs_sb", [128, NV], F32)
        cst = sb("cst", [128, 512], F32)
        cbf = sb("cbf", [128, 384], BF16)
        Abc = sb("Abc", [128, 32], F32)
        XB = sb("XB", [128, 8, TB], F32)
        sq = sb("sq", [128, 8, TB], BF16)
        hT = sb("hT", [128, 8, TB], BF16)
        rt = sb("rt", [128, TB], F32)
        rstd = sb("rstd", [128, TB], F32)
        Bsb = sb("Bsb", [128, TB], F32)
        Csb = sb("Csb", [128, TB], F32)
        usc = sb("usc", [128, TB + 2], BF16)
        dgs = sb("dgs", [128, 3, 128], BF16)
        car_sc = sb("car_sc", [128, 8, 2], F32)
        car_ss = sb("car_ss", [128, 8, 4, 3], F32)
        yaT = sb("yaT", [128, 8, TB], BF16)
        tbl = sb("tbl", [128, 8, 128], F32)
        BIG1 = sb("BIG1", [128, 8192], F32)
        BIG2 = sb("BIG2", [128, 6152], F32)
        R_ = [sb(f"R{i}", [128, 2, 4, 128], BF16) for i in range(2)]
        dAhb = sb("dAhb", [128, 128], BF16)
        dAhi = sb("dAhi", [128, 128], F32)
        dAlo = sb("dAlo", [128, 128], F32)
        LT_ = [sb(f"LT{i}", [128, 4, 128], F32) for i in range(2)]
        ecs_ = [sb(f"ecs{i}", [128, 4, 128], F32) for i in range(2)]
        CBm_ = [sb(f"CBm{i}", [128, 128], F32) for i in range(2)]
        WT_ = [sb(f"WT{i}", [128, 4, 128], BF16) for i in range(2)]
        CsT_ = [sb(f"CsT{i}", [128, 4, 128], BF16) for i in range(2)]
        xdt_ = [sb(f"xdt{i}", [128, 4, 64], BF16) for i in range(2)]
        xdtd_ = [sb(f"xdtd{i}", [128, 4, 64], BF16) for i in range(2)]
        Btok_ = [sb(f"Btok{i}", [128, 128], BF16) for i in range(2)]
        stf = sb("stf", [128, 8, 256], F32)
        stb = sb("stb", [128, 8, 256], BF16)
        ybT = sb("ybT", [128, 16, TB], BF16)
        mgT = sq
        h2T = hT
        gsb = sb("gsb", [128, TB], F32)
        m1sb = sb("m1sb", [128, TB], F32)
        m2sb = sb("m2sb", [128, TB], F32)
        rl = sb("rl", [128, TB], F32)
        wdt = sb("wdt", [128, 8, 32], BF16)
        dummy = sb("fence_t", [128, 4], F32)
        WP = [sb(f"WP{i}", [128, 4096], BF16) for i in range(NW)]
        mm = st.enter_context(nc.psum_tensor("mm", [128, 4, 512], F32))
        S0 = st.enter_context(nc.psum_tensor("S0", [128, 512], F32))
        S1 = st.enter_context(nc.psum_tensor("S1", [128, 512], F32))
        S2 = st.enter_context(nc.psum_tensor("S2", [128, 512], F32))
        S3 = st.enter_context(nc.psum_tensor("S3", [128, 512], F32))

        ident = cst[:, 0:128]
        tri = cst[:, 128:256]
        Umat = cst[:, 256:384]
        ones = cst[:, 384:512]
        ident_bf = cbf[:, 0:128]
        ones_bf = cbf[:, 128:256]
        U_bf = cbf[:, 256:384]
        GS = []
        for si_, big in enumerate((BIG1, BIG2)):
            GS.append(dict(
                zs=big[:, 0:1024].rearrange("p (j t) -> p j t", j=2),
                u4=big[:, 1024:1024 + 1030].bitcast(BF16).rearrange("p (b t) -> p b t", b=4),
                dg=big[:, 2056:3080].bitcast(BF16).rearrange("p (m c) -> p m c", m=16),
                xsT=big[:, 3080:4104].rearrange("p (j t) -> p j t", j=2),
                yT=big[:, 4104:5128].rearrange("p (j t) -> p j t", j=2),
                BT=big[:, 5128:5384].bitcast(BF16),
                CT=big[:, 5384:5640].bitcast(BF16),
                sq2=big[:, 5640:6152].bitcast(BF16).rearrange("p (j t) -> p j t", j=2),
                rt=(rt, gsb)[si_], rstd=(rstd, m1sb)[si_], rtk=("rt", "gsb")[si_], rstdk=("rstd", "m1sb")[si_]))
        PS = [(S0, S1, S2, S3), (S0, S1, S2, S3)]
        aT = BIG1[:, :].bitcast(BF16).rearrange("p (f t) -> p f t", f=32)

        T1, TE, TDT, TDA, TCS, T2, TDEC, TCD = [tbl[:, i, :].rearrange("p (c h) -> p c h", c=4) for i in range(8)]

        mmc = [0]

        def mmbank():
            i = mmc[0] % 4
            mmc[0] += 1
            return mm[:, i, :], ("mm", i)

        wpc = [0]

        def wload(src, shape, reads):
            i = wpc[0] % NW
            wpc[0] += 1
            n = int(np.prod(shape[1:]))
            dst = WP[i][:, 0:n]
            ka, kb = ("WP", i, "a"), ("WP", i, "b")
            if len(shape) == 3:
                dst = dst.rearrange("p (a b) -> p a b", a=shape[1])
                P.dma("sp", f"wp{i}a", lambda e, dst=dst, src=src: e.dma_start(out=dst, in_=src),
                      reads=reads, writes=[ka, kb])
            else:
                dst = dst.rearrange("p (s a b) -> p s a b", s=shape[1], a=shape[2])
                for s_, kk in ((0, ka), (1, kb)):
                    P.dma("sp", f"wp{i}" + "ab"[s_], lambda e, d=dst[:, s_], r=src[:, s_]: e.dma_start(out=d, in_=r),
                          reads=reads, writes=[kk])
            return dst, [ka, kb]

        def proj(ps, lhs, rhs, nk):
            def fn(e):
                last = None
                for k in range(nk):
                    last = e.matmul(ps, lhs(k), rhs(k), start=(k == 0), stop=(k == nk - 1))
                return last
            return fn

        def dbg(name, ap, shape, reads):
            if not debug:
                return
            d = nc.dram_tensor("dbg_" + name, list(shape), ap.dtype, kind="ExternalOutput").ap()
            dbg_out[name] = d
            P.dma("sp", "dbg_" + name, lambda e: e.dma_start(out=d, in_=ap), reads=reads, writes=[("dbg", name)])

        P.dma("sp", "ld_vecs", lambda e: e.dma_start(out=vecs[:], in_=dr["vecs"]), writes=["vecs"])
        P.dma("sp", "ld_cst", lambda e: e.dma_start(out=cst[:], in_=dr["consts"]), writes=["cst"])
        P.op("dve", lambda e: e.tensor_copy(out=cbf[:, 0:128], in_=ident), reads=["cst"], writes=["cbf0"])
        P.op("dve", lambda e: e.tensor_copy(out=cbf[:, 128:256], in_=ones), reads=["cst"], writes=["cbf1"])
        P.op("dve", lambda e: e.tensor_copy(out=cbf[:, 256:384], in_=Umat), reads=["cst"], writes=["cbf2"])
        CB = ["cbf0", "cbf1", "cbf2", "cst"]
        P.op("act", lambda e: e.activation(out=Abc[:], in_=vecs[:, V_ALOG:V_ALOG + 32], func=AF.Exp),
             reads=["vecs"], writes=["Abc"])
        P.op("dve", lambda e: e.tensor_scalar(out=Abc[:], in0=Abc[:], scalar1=-1.0, scalar2=None, op0=ALU.mult),
             reads=["Abc"], writes=["Abc"])
        P.op("dve", lambda e: e.memset(stf[:], 0.0), writes=[("stf", g) for g in range(8)])
        P.op("dve", lambda e: e.memset(stb[:], 0.0), writes=[("stb", g) for g in range(8)])
        P.op("dve", lambda e: e.memset(car_sc[:], 0.0), writes=["car_sc"])
        P.op("dve", lambda e: e.memset(car_ss[:], 0.0), writes=["car_ss"])

        castc = [0]

        def cast(dst, src, key):
            i = castc[0]
            castc[0] += 1
            P.dma("pool", f"cast{i}",
                  lambda e: e.dma_start(out=dst, in_=src, max_dma_last_dim=4096),
                  writes=[key])

        K_SC = [("scr", "in_sc", r) for r in range(2)]
        for r in range(2):
            cast(s_in[r * 512:(r + 1) * 512, 0:3072], dr["w_in"][r * 512:(r + 1) * 512, 0:3072], K_SC[r])
        K_DT = [("scr", "in_dt")]
        cast(s_in[:, C_DT:C_DT + 32], dr["w_in"][:, C_DT:C_DT + 32], K_DT[0])
        K_SSD = [("scr", "in_ssd", r) for r in range(4)]
        for r in range(4):
            cast(s_in[r * 256:(r + 1) * 256, C_Z:C_DT], dr["w_in"][r * 256:(r + 1) * 256, C_Z:C_DT], K_SSD[r])
        K_G = [("scr", "in_g", r) for r in range(2)]
        for r in range(2):
            cast(s_in[r * 512:(r + 1) * 512, C_G:DIN], dr["w_in"][r * 512:(r + 1) * 512, C_G:DIN], K_G[r])
        K_BSC = [("scr", "bsc")]
        cast(s_bsc, dr["w_bsc"], K_BSC[0])
        K_BSSM = [("scr", "bssm", r) for r in range(2)]
        for r in range(2):
            cast(s_bssm[r * 1024:(r + 1) * 1024, :], dr["w_bssm"][r * 1024:(r + 1) * 1024, :], K_BSSM[r])
        K_OUT = [("scr", "out")]
        cast(s_out, dr["w_out"], K_OUT[0])
        K_M1 = [("scr", "m1", r) for r in range(4)]
        for r in range(4):
            cast(s_m1[r * 256:(r + 1) * 256, :], dr["w_m1"][r * 256:(r + 1) * 256, :], K_M1[r])
        K_M2 = [("scr", "m2", r) for r in range(4)]
        for r in range(4):
            cast(s_m2[r * 1024:(r + 1) * 1024, :], dr["w_m2"][r * 1024:(r + 1) * 1024, :], K_M2[r])

        v_in = s_in.rearrange("(k p) n -> p k n", p=128)
        P.dma("sp", "ld_wdt", lambda e: e.dma_start(out=wdt[:], in_=v_in[:, :, C_DT:C_DT + 32]),
              reads=K_DT, writes=["wdt"])

        xT_v = dr["xT"].rearrange("(k p) t -> p k t", p=128)
        outT_v = outT.rearrange("(k p) t -> p k t", p=128)

        def rmsnorm_to(dst_bf, vcol, ndiv, tag):
            P.op("act", lambda e: e.activation(out=sq[:], in_=XB[:], func=AF.Square),
                 reads=[("XB", k) for k in range(8)], writes=["sq"])
            ps, pk = mmbank()
            P.op("pe", proj(ps, lambda k: ones_bf, lambda k: sq[:, k, :], 8), reads=["sq"] + CB, writes=[pk])
            P.op("act", lambda e: e.activation(out=rt[:], in_=ps, func=AF.Ln, scale=1.0 / ndiv, bias=EPS),
                 reads=[pk], writes=["rt"])
            P.op("act", lambda e: e.activation(out=rstd[:], in_=rt[:], func=AF.Exp, scale=-0.5), reads=["rt"], writes=["rstd"])
            for k in range(8):
                P.op("dve", lambda e, k=k: e.scalar_tensor_tensor(
                    out=dst_bf[:, k, :], in0=XB[:, k, :], scalar=vecs[:, vcol + k:vcol + k + 1], in1=rstd[:],
                    op0=ALU.mult, op1=ALU.mult),
                    reads=[("XB", k), "rstd", "vecs"], writes=[(tag, k)])

        chunkc = [0]

        for tb in range(ntb):
            t0 = tb * TB
            PL = "pool" if (USE_POOL and tb > 0) else "dve"
            P.op("dve", lambda e: e.memset(dummy[:, 2:3], 0.0), writes=["G3", "dummy2"])
            P.dma("sp", "ld_x", lambda e, t0=t0: e.dma_start(out=XB[:], in_=xT_v[:, :, t0:t0 + TB]),
                  writes=[("XB", k) for k in range(8)])
            rmsnorm_to(hT, V_NMIX, float(D), "hT")
            HT = [("hT", k) for k in range(8)]
            if tb == 0:
                dbg("hT", hT[:], [128, 8, TB], HT)

            for half in range(2):
                c0 = half * 512
                wB, kB = wload(v_in[:, :, C_SC + c0:C_SC + c0 + 512], [128, 8, 512], K_SC)
                wC, kC = wload(v_in[:, :, C_SC + 1024 + c0:C_SC + 1024 + c0 + 512], [128, 8, 512], K_SC)
                wX, kX = wload(v_in[:, :, C_SC + 2048 + c0:C_SC + 2048 + c0 + 512], [128, 8, 512], K_SC)
                for c4 in range(4):
                    cb = half * 4 + c4
                    cs_ = slice(c4 * 128, (c4 + 1) * 128)
                    psB, pkB = mmbank()
                    P.op("pe", proj(psB, lambda k, w=wB, s=cs_: w[:, k, s], lambda k: hT[:, k, :], 8),
                         reads=HT + kB, writes=[pkB])
                    psC, pkC = mmbank()
                    P.op("pe", proj(psC, lambda k, w=wC, s=cs_: w[:, k, s], lambda k: hT[:, k, :], 8),
                         reads=HT + kC, writes=[pkC])
                    psX, pkX = mmbank()
                    P.op("pe", proj(psX, lambda k, w=wX, s=cs_: w[:, k, s], lambda k: hT[:, k, :], 8),
                         reads=HT + kX, writes=[pkX])
                    P.op("act", lambda e, ps=psB: e.activation(out=Bsb[:], in_=ps, func=AF.Identity),
                         reads=[pkB], writes=["Bsb"])
                    P.op("act", lambda e, ps=psC: e.activation(out=Csb[:], in_=ps, func=AF.Identity),
                         reads=[pkC], writes=["Csb"])
                    P.op("dve", lambda e, cb=cb: e.tensor_copy(out=usc[:, 0:2], in_=car_sc[:, cb, :]),
                         reads=["car_sc"], writes=["usc_h"])
                    P.op("dve", lambda e, ps=psX: e.tensor_tensor(out=usc[:, 2:TB + 2], in0=ps, in1=Csb[:], op=ALU.mult),
                         reads=[pkX, "Csb"], writes=["usc"])
                    P.op("dve", lambda e, cb=cb: e.tensor_copy(out=car_sc[:, cb, :], in_=usc[:, TB:TB + 2]),
                         reads=["usc"], writes=["car_sc"])
                    wv = V_SCW + cb * 3
                    P.op("dve", lambda e, wv=wv: e.tensor_tensor(
                        out=dgs[:], in0=ident_bf.unsqueeze(1).broadcast_to([128, 3, 128]),
                        in1=vecs[:, wv:wv + 3].unsqueeze(2).broadcast_to([128, 3, 128]), op=ALU.mult),
                        reads=["vecs"] + CB, writes=["dgs"])
                    psV, pkV = mmbank()

                    def scconv(e, psV=psV):
                        last = None
                        for tap in range(3):
                            last = e.matmul(psV, dgs[:, tap, :], usc[:, tap:tap + TB], start=(tap == 0), stop=(tap == 2))
                        return last
                    P.op("pe", scconv, reads=["dgs", "usc", "usc_h"], writes=[pkV])
                    P.op("dve", lambda e, cb=cb, psV=psV: e.tensor_tensor(out=yaT[:, cb, :], in0=psV, in1=Bsb[:], op=ALU.mult),
                         reads=[pkV, "Bsb"], writes=[("yaT", cb)])
            YA = [("yaT", k) for k in range(8)]
            if tb == 0:
                dbg("yaT", yaT[:], [128, 8, TB], YA)

            ps, pk = mmbank()

            def dtmm(e, ps=ps):
                last = None
                for cc in range(4):
                    for k in range(8):
                        last = e.matmul(ps[:, cc * 32:(cc + 1) * 32], hT[:, k, cc * 128:(cc + 1) * 128], wdt[:, k, :],
                                        start=(k == 0), stop=(k == 7))
                return last
            P.op("pe", dtmm, reads=HT + ["wdt"], writes=[pk])
            dtb_b = vecs[:, V_DTB:V_DTB + 32].unsqueeze(1).broadcast_to([128, 4, 32])
            A_b = Abc[:].unsqueeze(1).broadcast_to([128, 4, 32])
            P.op("dve", lambda e, ps=ps: e.tensor_tensor(out=T1, in0=ps[:, 0:128].rearrange("p (c h) -> p c h", c=4),
                                                         in1=dtb_b, op=ALU.add),
                 reads=[pk, "vecs"], writes=["T1"])
            P.op("act", lambda e: e.activation(out=TE, in_=T1, func=AF.Exp), reads=["T1"], writes=["TE"])
            P.op("act", lambda e: e.activation(out=TDT, in_=TE, func=AF.Ln, bias=1.0), reads=["TE"], writes=["TDT"])
            P.op("dve", lambda e: e.tensor_tensor(out=TDA, in0=TDT, in1=A_b, op=ALU.mult),
                 reads=["TDT", "Abc"], writes=["TDA"])
            P.op("dve", lambda e: e.tensor_copy(out=dAhb[:], in_=tbl[:, 3, :]), reads=["TDA"], writes=["dAhb"])
            P.op("dve", lambda e: e.tensor_copy(out=dAhi[:], in_=dAhb[:]), reads=["dAhb"], writes=["dAhi"])
            P.op("dve", lambda e: e.tensor_tensor(out=dAlo[:], in0=tbl[:, 3, :], in1=dAhi[:], op=ALU.subtract),
                 reads=["TDA", "dAhi"], writes=["dAlo"])
            ps2, pk2 = mmbank()

            def csmm(e, ps2=ps2):
                e.matmul(ps2[:, 0:128], tri, tbl[:, 3, :], start=True, stop=True)
                return e.matmul(ps2[:, 128:256], ones, tbl[:, 3, :], start=True, stop=True)
            P.op("pe", csmm, reads=["TDA", "cst"], writes=[pk2])
            P.op("act", lambda e, ps2=ps2: e.activation(out=tbl[:, 4, :], in_=ps2[:, 0:128], func=AF.Identity),
                 reads=[pk2], writes=["TCS"])
            P.op("dve", lambda e, ps2=ps2: e.tensor_tensor(out=tbl[:, 5, :], in0=ps2[:, 128:256], in1=tbl[:, 4, :],
                                                           op=ALU.subtract),
                 reads=[pk2, "TCS"], writes=["T2"])
            P.op("act", lambda e: e.activation(out=tbl[:, 6, :], in_=tbl[:, 5, :], func=AF.Exp),
                 reads=["T2"], writes=["TDEC"])
            P.op("act", lambda e, ps2=ps2: e.activation(out=tbl[:, 7, :], in_=ps2[:, 128:256], func=AF.Exp),
                 reads=[pk2], writes=["TCD"])
            if tb == 0:
                dbg("tbl", tbl[:], [128, 8, 128], ["T1", "TE", "TDT", "TDA", "TCS", "T2", "TDEC", "TCD"])

            P.op("dve", lambda e: e.memset(dummy[:, 0:1], 0.0), writes=["BIG1", "dummy0"])
            v_zx = s_in[:, C_Z:C_B].rearrange("(k p) (s n) -> p s k n", p=128, s=2)
            v_bc = s_in[:, C_B:C_DT].rearrange("(k p) (s n) -> p s k n", p=128, s=2)

            def group_gen(g, si, wBC, kBC, wZX, kZX):
                B_ = GS[si]
                zs, u4, dg, xsT, yT, BT, CT, sq2 = (B_["zs"], B_["u4"], B_["dg"], B_["xsT"], B_["yT"], B_["BT"],
                                                     B_["CT"], B_["sq2"])
                rt_, rstd_ = B_["rt"], B_["rstd"]
                S0, S1, S2, S3 = PS[si]
                kS = lambda n: "S%d" % n
                K = lambda n, *a: (n, si) + tuple(a)
                go = (g % 2) * 128
                for (m0, nm, chb, wk) in ((0, 8, 2 * g, [K("dg", 0), K("dg", 1)]), (8, 4, 16 + g, [K("dg", 2)]),
                                          (12, 4, 24 + g, [K("dg", 3)])):
                    wv0 = V_SSW + chb * 4
                    P.op("dve", lambda e, m0=m0, nm=nm, wv0=wv0: e.tensor_tensor(
                        out=dg[:, m0:m0 + nm, :], in0=ident_bf.unsqueeze(1).broadcast_to([128, nm, 128]),
                        in1=vecs[:, wv0:wv0 + nm].unsqueeze(2).broadcast_to([128, nm, 128]), op=ALU.mult),
                        reads=["vecs"] + CB, writes=wk)
                for j in range(2):
                    ps, pk = mmbank()
                    P.op("pe", proj(ps, lambda k, j=j, w=wZX: w[:, 0, k, j * 128:(j + 1) * 128], lambda k: hT[:, k, :], 8),
                         reads=HT + kZX, writes=[pk])
                    P.op("act", lambda e, ps=ps, j=j: e.activation(out=zs[:, j, :], in_=ps, func=AF.Silu),
                         reads=[pk], writes=[K("zs", j)])
                    yield
                P.op(PL, lambda e: e.tensor_copy(out=u4[:, :, 0:3], in_=car_ss[:, g, :, :]),
                     reads=[("car_ss", g)], writes=[K("u4_h")])
                for b in range(4):
                    ps, pk = mmbank()
                    if b < 2:
                        lhs = (lambda k, b=b, w=wZX: w[:, 1, k, b * 128:(b + 1) * 128])
                        rk = kZX
                    else:
                        lhs = (lambda k, b=b, go=go, w=wBC: w[:, b - 2, k, go:go + 128])
                        rk = kBC
                    P.op("pe", proj(ps, lhs, lambda k: hT[:, k, :], 8), reads=HT + rk, writes=[pk])
                    P.op("act", lambda e, ps=ps, b=b: e.activation(out=u4[:, b, 3:TB + 3], in_=ps, func=AF.Identity),
                         reads=[pk], writes=[K("u4", b)])
                    yield
                P.op(PL, lambda e: e.tensor_copy(out=car_ss[:, g, :, :], in_=u4[:, :, TB:TB + 3]),
                     reads=[K("u4", b) for b in range(4)], writes=[("car_ss", g)])
                for b in range(4):
                    ch = (2 * g + b) if b < 2 else (16 + g if b == 2 else 24 + g)
                    wv = V_SSW + ch * 4
                    bv = V_SSB + ch
                    ps, pk = mmbank()

                    def ssconv(e, ps=ps, b=b):
                        last = None
                        for tap in range(4):
                            last = e.matmul(ps, dg[:, b * 4 + tap, :], u4[:, b, tap:tap + TB], start=(tap == 0), stop=(tap == 3))
                        return last
                    P.op("pe", ssconv, reads=[K("dg", b), K("u4", b), K("u4_h")], writes=[pk])
                    if b < 2:
                        P.op("act", lambda e, b=b, ps=ps, bv=bv: e.activation(out=xsT[:, b, :], in_=ps, func=AF.Silu,
                                                                              bias=vecs[:, bv:bv + 1]),
                             reads=[pk, "vecs"], writes=[K("xsT", b)])
                    elif b == 2:
                        P.op("act", lambda e, ps=ps, bv=bv: e.activation(out=BT, in_=ps, func=AF.Silu, bias=vecs[:, bv:bv + 1]),
                             reads=[pk, "vecs"], writes=[K("BT")])
                    else:
                        P.op("act", lambda e, ps=ps, bv=bv: e.activation(out=CT, in_=ps, func=AF.Silu, bias=vecs[:, bv:bv + 1]),
                             reads=[pk, "vecs"], writes=[K("CT")])
                    yield
                if tb == 0 and g == 0:
                    dbg("xsT", xsT, [128, 2, TB], [K("xsT", 0), K("xsT", 1)])
                    dbg("BT", BT, [128, TB], [K("BT")])
                    dbg("CT", CT, [128, TB], [K("CT")])
                    dbg("zs", zs, [128, 2, TB], [K("zs", 0), K("zs", 1)])
                par = si
                R, LT, ecs, CBm, WT, CsT, xdt, xdtd, Btok = (R_[par], LT_[par], ecs_[par], CBm_[par], WT_[par],
                                                             CsT_[par], xdt_[par], xdtd_[par], Btok_[par])
                for cc in range(4):
                    tc_ = slice(cc * 128, (cc + 1) * 128)

                    def rbuild(e, cc=cc):
                        last = None
                        for s_, src in ((0, dAhi), (1, dAlo)):
                            for h in range(4):
                                c0 = cc * 32 + 4 * g + h
                                last = e.tensor_scalar(out=R[:, s_, h, :], in0=tri, scalar1=src[:, c0:c0 + 1],
                                                       scalar2=None, op0=ALU.mult)
                        return last
                    P.op(PL, rbuild, reads=["dAhi", "dAlo", "cst"], writes=[("R", par)])

                    def pe1(e, tc_=tc_):
                        e.transpose(S2[:, 0:128], xsT[:, 0, tc_], ident)
                        e.transpose(S2[:, 128:256], xsT[:, 1, tc_], ident)
                        e.matmul(S2[:, 256:384], BT[:, tc_], CT[:, tc_], start=True, stop=True)
                        return e.matmul(S2[:, 384:512], BT[:, tc_], ident_bf, start=True, stop=True)
                    P.op("pe", pe1, reads=[K("xsT", 0), K("xsT", 1), K("BT"), K("CT")] + CB, writes=[kS(2)])

                    def pe2(e):
                        rh = R[:, 0].rearrange("p h i -> p (h i)")
                        rl_ = R[:, 1].rearrange("p h i -> p (h i)")
                        e.matmul(S0[:], U_bf, rh, start=True, stop=False)
                        e.matmul(S0[:], U_bf, rl_, start=False, stop=True)
                        e.matmul(S1[:], ones_bf, rh, start=True, stop=False)
                        return e.matmul(S1[:], ones_bf, rl_, start=False, stop=True)
                    P.op("pe", pe2, reads=[("R", par)] + CB, writes=[kS(0), kS(1)])
                    P.op("act", lambda e: e.activation(out=LT[:].rearrange("p h i -> p (h i)"), in_=S0[:], func=AF.Exp),
                         reads=[kS(0)], writes=[("LT", par)])
                    P.op("act", lambda e: e.activation(out=ecs[:].rearrange("p h i -> p (h i)"), in_=S1[:], func=AF.Exp),
                         reads=[kS(1)], writes=[("ecs", par)])
                    P.op("dve", lambda e: e.tensor_tensor(out=CBm[:], in0=S2[:, 256:384], in1=tri, op=ALU.mult),
                         reads=[kS(2), "cst"], writes=[("CBm", par)])
                    P.op("dve", lambda e, cc=cc: e.tensor_tensor(
                        out=xdt[:], in0=S2[:, 0:256].rearrange("p (h q) -> p h q", h=4),
                        in1=TDT[:, cc, 4 * g:4 * g + 4].unsqueeze(2).broadcast_to([128, 4, 64]), op=ALU.mult),
                        reads=[kS(2), "TDT"], writes=[("xdt", par)])
                    P.op("act", lambda e: e.activation(out=Btok[:], in_=S2[:, 384:512], func=AF.Identity),
                         reads=[kS(2)], writes=[("Btok", par)])
                    yield
                    P.op("dve", lambda e: e.tensor_tensor(
                        out=WT[:], in0=LT[:], in1=CBm[:].unsqueeze(1).broadcast_to([128, 4, 128]), op=ALU.mult),
                        reads=[("LT", par), ("CBm", par)], writes=[("WT", par)])
                    P.op(PL, lambda e, tc_=tc_: e.tensor_tensor(
                        out=CsT[:], in0=ecs[:], in1=CT[:, tc_].unsqueeze(1).broadcast_to([128, 4, 128]), op=ALU.mult),
                        reads=[("ecs", par), K("CT")], writes=[("CsT", par)])
                    P.op(PL, lambda e, cc=cc: e.tensor_tensor(
                        out=xdtd[:], in0=xdt[:],
                        in1=TDEC[:, cc, 4 * g:4 * g + 4].unsqueeze(2).broadcast_to([128, 4, 64]), op=ALU.mult),
                        reads=[("xdt", par), "TDEC"], writes=[("xdtd", par)])
                    yield

                    def pe3(e):
                        for h in range(4):
                            o = S3[64 * (h % 2):64 * (h % 2) + 64, (h // 2) * 128:(h // 2) * 128 + 128]
                            e.matmul(o, xdt[:, h, :], WT[:, h, :], start=True, stop=False)
                            e.matmul(o, stb[:, g, h * 64:(h + 1) * 64], CsT[:, h, :], start=False, stop=True)
                        return e.matmul(S3[:, 256:512], Btok[:], xdtd[:].rearrange("p h q -> p (h q)"), start=True, stop=True)
                    P.op("pe", pe3, reads=[("xdt", par), ("WT", par), ("CsT", par), ("Btok", par), ("xdtd", par), ("stb", g)],
                         writes=[kS(3)])

                    def yev(e, tc_=tc_):
                        last = None
                        for j in range(2):
                            dv = V_DCOL + 2 * g + j
                            last = e.scalar_tensor_tensor(out=yT[:, j, tc_], in0=xsT[:, j, tc_], scalar=vecs[:, dv:dv + 1],
                                                          in1=S3[:, j * 128:(j + 1) * 128], op0=ALU.mult, op1=ALU.add)
                        return last
                    P.op("dve", yev, reads=[kS(3), K("xsT", 0), K("xsT", 1), "vecs"], writes=[K("yT", cc)])
                    P.op(PL, lambda e, cc=cc: e.tensor_tensor(
                        out=stf[:, g, :].rearrange("p (h q) -> p h q", h=4),
                        in0=stf[:, g, :].rearrange("p (h q) -> p h q", h=4),
                        in1=TCD[:, cc, 4 * g:4 * g + 4].unsqueeze(2).broadcast_to([128, 4, 64]), op=ALU.mult),
                        reads=[("stf", g), "TCD"], writes=[("stf", g)])
                    P.op("dve", lambda e: e.tensor_tensor(out=stf[:, g, :], in0=stf[:, g, :], in1=S3[:, 256:512], op=ALU.add),
                         reads=[("stf", g), kS(3)], writes=[("stf", g)])
                    P.op("act", lambda e: e.activation(out=stb[:, g, :], in_=stf[:, g, :], func=AF.Identity),
                         reads=[("stf", g)], writes=[("stb", g)])
                    yield
                YT = [K("yT", cc) for cc in range(4)]
                if tb == 0 and g == 0:
                    dbg("yT", yT, [128, 2, TB], YT)
                P.op(PL, lambda e: e.tensor_tensor(out=yT, in0=yT, in1=zs, op=ALU.mult),
                     reads=YT + [K("zs", 0), K("zs", 1)], writes=YT)
                P.op("act", lambda e: e.activation(out=sq2, in_=yT, func=AF.Square), reads=YT, writes=[K("sq2")])
                ps, pk = mmbank()
                P.op("pe", proj(ps, lambda k: ones_bf, lambda k: sq2[:, k, :], 2), reads=[K("sq2")] + CB, writes=[pk])
                yield
                P.op("act", lambda e, ps=ps: e.activation(out=rt_[:], in_=ps, func=AF.Ln, scale=1.0 / 256.0, bias=EPS),
                     reads=[pk], writes=[B_["rtk"]])
                P.op("act", lambda e: e.activation(out=rstd_[:], in_=rt_[:], func=AF.Exp, scale=-0.5), reads=[B_["rtk"]], writes=[B_["rstdk"]])
                for j in range(2):
                    nv = V_SNW + 2 * g + j
                    P.op("dve", lambda e, j=j, nv=nv: e.scalar_tensor_tensor(
                        out=ybT[:, 2 * g + j, :], in0=yT[:, j, :], scalar=vecs[:, nv:nv + 1], in1=rstd_[:],
                        op0=ALU.mult, op1=ALU.mult),
                        reads=YT + [B_["rstdk"], "vecs"], writes=[("ybT", 2 * g + j)])
                yield

            for gp in range(4):
                g0 = 2 * gp
                wBC, kBC = wload(v_bc[:, :, :, 128 * g0:128 * g0 + 256], [128, 2, 8, 256], K_SSD)
                gens = []
                for si in range(2):
                    g = g0 + si
                    wZX, kZX = wload(v_zx[:, :, :, 256 * g:256 * g + 256], [128, 2, 8, 256], K_SSD)
                    gens.append(group_gen(g, si, wBC, kBC, wZX, kZX))
                alive = list(gens)
                while alive:
                    for gn in list(alive):
                        try:
                            next(gn)
                        except StopIteration:
                            alive.remove(gn)
            YB = [("ybT", k) for k in range(16)]
            if tb == 0:
                dbg("ybT", ybT[:], [128, 16, TB], YB)

            P.op("dve", lambda e: e.memset(dummy[:, 3:4], 0.0), writes=["G2", "dummy3"])
            v_g = s_in[:, C_G:DIN].rearrange("(k p) (s n) -> p s k n", p=128, s=2)
            v_bsc = s_bsc.rearrange("(k p) n -> p k n", p=128)
            v_bssm = s_bssm.rearrange("(k p) n -> p k n", p=128)
            for obp in range(4):
                o0 = obp * 256
                wG, kG = wload(v_g[:, :, :, o0:o0 + 256], [128, 2, 8, 256], K_G)
                wA, kA = wload(v_bsc[:, :, o0:o0 + 256], [128, 8, 256], K_BSC)
                wS, kS = wload(v_bssm[:, :, o0:o0 + 256], [128, 16, 256], K_BSSM)
                for o2 in range(2):
                    ob = obp * 2 + o2
                    os_ = slice(o2 * 128, (o2 + 1) * 128)
                    for s in range(2):
                        psg, pkg = mmbank()
                        P.op("pe", proj(psg, lambda k, s=s, os_=os_, w=wG: w[:, s, k, os_], lambda k: hT[:, k, :], 8),
                             reads=HT + kG, writes=[pkg])
                        psb, pkb = mmbank()
                        if s == 0:
                            P.op("pe", proj(psb, lambda k, os_=os_, w=wA: w[:, k, os_], lambda k: yaT[:, k, :], 8),
                                 reads=YA + kA, writes=[pkb])
                        else:
                            P.op("pe", proj(psb, lambda k, os_=os_, w=wS: w[:, k, os_], lambda k: ybT[:, k, :], 16),
                                 reads=YB + kS, writes=[pkb])
                        bg = V_BG + s * 8 + ob
                        P.op("act", lambda e, psg=psg, bg=bg: e.activation(out=gsb[:], in_=psg, func=AF.Sigmoid,
                                                                            bias=vecs[:, bg:bg + 1]),
                             reads=[pkg, "vecs"], writes=["gsb"])
                        msb = m1sb if s == 0 else m2sb
                        P.op("dve", lambda e, psb=psb, msb=msb: e.tensor_tensor(out=msb[:], in0=psb, in1=gsb[:], op=ALU.mult),
                             reads=[pkb, "gsb"], writes=["m1sb" if s == 0 else "m2sb"])
                    P.op(PL, lambda e, ob=ob: e.tensor_tensor(out=mgT[:, ob, :], in0=m1sb[:], in1=m2sb[:], op=ALU.add),
                         reads=["m1sb", "m2sb"], writes=[("mgT", ob)])
            MG = [("mgT", k) for k in range(8)]
            if tb == 0:
                dbg("mgT", mgT[:], [128, 8, TB], MG)

            v_out = s_out.rearrange("(k p) n -> p k n", p=128)
            for oh in range(2):
                wO, kO = wload(v_out[:, :, oh * 512:(oh + 1) * 512], [128, 8, 512], K_OUT)
                for o4 in range(4):
                    ob = oh * 4 + o4
                    ps, pk = mmbank()
                    P.op("pe", proj(ps, lambda k, o4=o4, wO=wO: wO[:, k, o4 * 128:(o4 + 1) * 128], lambda k: mgT[:, k, :], 8),
                         reads=MG + kO, writes=[pk])
                    P.op("dve", lambda e, ps=ps, ob=ob: e.tensor_tensor(out=XB[:, ob, :], in0=ps, in1=XB[:, ob, :], op=ALU.add),
                         reads=[pk, ("XB", ob)], writes=[("XB", ob)])
            if tb == 0:
                dbg("x1T", XB[:], [128, 8, TB], [("XB", k) for k in range(8)])

            P.op("dve", lambda e: e.memset(dummy[:, 2:3], 0.0), writes=["G2", "G3", "dummy2"])
            rmsnorm_to(h2T, V_NMLP, float(D), "h2T")
            H2 = [("h2T", k) for k in range(8)]
            P.op("dve", lambda e: e.memset(dummy[:, 1:2], 0.0), writes=["BIG1", "dummy1"])
            v_m1 = s_m1.rearrange("(k p) n -> p k n", p=128)
            for fb in range(8):
                wM, kM = wload(v_m1[:, :, fb * 512:(fb + 1) * 512], [128, 8, 512], K_M1)
                for f4 in range(4):
                    f = fb * 4 + f4
                    ps, pk = mmbank()
                    P.op("pe", proj(ps, lambda k, f4=f4, wM=wM: wM[:, k, f4 * 128:(f4 + 1) * 128], lambda k: h2T[:, k, :], 8),
                         reads=H2 + kM, writes=[pk])
                    P.op("act", lambda e, ps=ps: e.activation(out=rl[:], in_=ps, func=AF.Relu), reads=[pk], writes=["rl"])
                    P.op("act", lambda e, f=f: e.activation(out=aT[:, f, :], in_=rl[:], func=AF.Square),
                         reads=["rl"], writes=[("aT", f)])
            AT = [("aT", f) for f in range(32)]
            v_m2 = s_m2.rearrange("(k p) n -> p k n", p=128)
            for obp in range(4):
                o0 = obp * 256
                wh = []
                for fh in range(2):
                    wh.append(wload(v_m2[:, fh * 16:(fh + 1) * 16, o0:o0 + 256], [128, 16, 256], K_M2))
                for o2 in range(2):
                    ob = obp * 2 + o2
                    os_ = slice(o2 * 128, (o2 + 1) * 128)
                    ps, pk = mmbank()
                    P.op("pe", proj(ps, lambda k, os_=os_, wh=wh: wh[k // 16][0][:, k % 16, os_], lambda k: aT[:, k, :], 32),
                         reads=AT + wh[0][1] + wh[1][1], writes=[pk])
                    P.op("dve", lambda e, ps=ps, ob=ob: e.tensor_tensor(out=XB[:, ob, :], in0=ps, in1=XB[:, ob, :], op=ALU.add),
                         reads=[pk, ("XB", ob)], writes=[("XB", ob)])

            P.op("act", lambda e: e.activation(out=sq[:], in_=XB[:], func=AF.Square),
                 reads=[("XB", k) for k in range(8)], writes=["sq"])
            ps, pk = mmbank()
            P.op("pe", proj(ps, lambda k: ones_bf, lambda k: sq[:, k, :], 8), reads=["sq"] + CB, writes=[pk])
            P.op("act", lambda e, ps=ps: e.activation(out=rt[:], in_=ps, func=AF.Ln, scale=1.0 / D, bias=EPS),
                 reads=[pk], writes=["rt"])
            P.op("act", lambda e: e.activation(out=rstd[:], in_=rt[:], func=AF.Exp, scale=-0.5), reads=["rt"], writes=["rstd"])
            for k in range(8):
                P.op("dve", lambda e, k=k: e.scalar_tensor_tensor(
                    out=XB[:, k, :], in0=XB[:, k, :], scalar=vecs[:, V_NFIN + k:V_NFIN + k + 1], in1=rstd[:],
                    op0=ALU.mult, op1=ALU.mult),
                    reads=[("XB", k), "rstd", "vecs"], writes=[("XB", k)])
            P.dma("sp", "st_out", lambda e, t0=t0: e.dma_start(out=outT_v[:, :, t0:t0 + TB], in_=XB[:]),
                  reads=[("XB", k) for k in range(8)], writes=[("outT", tb)])

        fin_reads = [("outT", tb) for tb in range(ntb)] + [("dbg", n) for n in dbg_out]
        P.op("sp", lambda e: None, reads=fin_reads, writes=["fin"])

        block = st.enter_context(nc.Block())
        P.emit(nc, block, st)
    return nc, dbg_out


def _pack_inputs(inputs):
    f = lambda a: np.ascontiguousarray(np.asarray(a, dtype=np.float32))
    vecs = np.zeros((128, NV), np.float32)
    vecs[:, V_NMIX:V_NMIX + 8] = f(inputs["norm_mix"]).reshape(8, 128).T
    vecs[:, V_NMLP:V_NMLP + 8] = f(inputs["norm_mlp"]).reshape(8, 128).T
    vecs[:, V_NFIN:V_NFIN + 8] = f(inputs["norm_final"]).reshape(8, 128).T
    vecs[:, V_BG:V_BG + 16] = f(inputs["b_gate"]).reshape(16, 128).T
    vecs[:, V_SCW:V_SCW + 24] = f(inputs["sc_conv_w"]).reshape(3, 8, 128).transpose(2, 1, 0).reshape(128, 24)
    vecs[:, V_SSW:V_SSW + 128] = f(inputs["ssm_conv_w"]).reshape(4, 32, 128).transpose(2, 1, 0).reshape(128, 128)
    vecs[:, V_SSB:V_SSB + 32] = f(inputs["ssm_conv_b"]).reshape(32, 128).T
    vecs[:, V_SNW:V_SNW + 16] = f(inputs["ssm_norm_w"]).reshape(16, 128).T
    vecs[:, V_DCOL:V_DCOL + 16] = np.repeat(f(inputs["D_skip"]), 64).reshape(16, 128).T
    vecs[:, V_DTB:V_DTB + 32] = np.tile(f(inputs["dt_bias"])[None, :], (128, 1))
    vecs[:, V_ALOG:V_ALOG + 32] = np.tile(f(inputs["A_log"])[None, :], (128, 1))
    k = np.arange(128)
    consts = np.zeros((128, 512), np.float32)
    consts[:, 0:128] = np.eye(128, dtype=np.float32)
    consts[:, 128:256] = (k[:, None] <= k[None, :]).astype(np.float32)
    consts[:, 256:384] = (k[:, None] > k[None, :]).astype(np.float32)
    consts[:, 384:512] = 1.0
    x = f(inputs["x"])
    xT = np.ascontiguousarray(x.transpose(0, 2, 1))
    common = {
        "w_in": f(inputs["w_in"]), "w_bsc": f(inputs["w_branch_sc"]), "w_bssm": f(inputs["w_branch_ssm"]),
        "w_out": f(inputs["w_out"]), "w_m1": f(inputs["w_mlp1"]), "w_m2": f(inputs["w_mlp2"]),
        "vecs": vecs, "consts": consts,
    }
    return [dict(common, xT=xT[i]) for i in range(8)]


def kernel(**inputs):
    in_maps = _pack_inputs(inputs)
    nc, _ = build_nc()
    res = run_bass_kernel_spmd(nc, in_maps, core_ids=list(range(8)))
    out = np.stack([np.ascontiguousarray(res.results[i]["outT"].T) for i in range(8)], axis=0)
    return out.astype(np.float32)
```

```python
import numpy as np
import concourse.bass as bass
import concourse.mybir as mybir
from concourse.bass_utils import run_bass_kernel_spmd

F32 = mybir.dt.float32
BF16 = mybir.dt.bfloat16
AF = mybir.ActivationFunctionType
ALU = mybir.AluOpType

D = 1024
T = 2048
TB = 512
NTB = T // TB
DFF = 4096
DIN = 11296
EPS = 1e-6
C_SC = 0
C_Z = 3072
C_X = 5120
C_B = 7168
C_C = 8192
C_DT = 9216
C_G = 9248
NW = 4
SAME_ENG_SYNC = True
USE_POOL = False

V_NMIX, V_NMLP, V_NFIN, V_BG, V_SCW, V_SSW, V_SSB, V_SNW, V_DCOL, V_DTB, V_ALOG = (
    0, 8, 16, 24, 40, 64, 192, 224, 240, 256, 288)
NV = 320

ALIAS = {"sq": "G2", "mgT": "G2", "hT": "G3", "h2T": "G3", "zs": "BIG1", "u4": "BIG1", "u4_h": "BIG1", "dg": "BIG1", "xsT": "BIG1", "yT": "BIG1",
         "BT": "BIG1", "CT": "BIG1", "sq2": "BIG1", "aT": "BIG1"}


class Prog:
    def __init__(self):
        self.ops = []

    def _norm(self, keys):
        out = []
        for k in keys:
            out.append(k)
            base = k[0] if isinstance(k, tuple) else k
            if base in ALIAS:
                out.append(ALIAS[base])
        return tuple(dict.fromkeys(out))

    def _alias(self, keys):
        out = []
        for k in keys:
            base = k[0] if isinstance(k, tuple) else k
            if base in ALIAS:
                out.append(ALIAS[base])
        return out

    def op(self, eng, fn, reads=(), writes=()):
        rd = self._norm(tuple(reads) + tuple(self._alias(writes)))
        self.ops.append(dict(eng=eng, fn=fn, reads=rd, writes=tuple(writes), sem=None))

    def dma(self, queue, sem, fn, reads=(), writes=()):
        rd = self._norm(tuple(reads) + tuple(self._alias(writes)))
        self.ops.append(dict(eng=queue, fn=fn, reads=rd, writes=tuple(writes), sem=sem))

    def emit(self, nc, block, stack):
        ops = self.ops
        last_writer, readers = {}, {}
        for i, o in enumerate(ops):
            deps = set()
            for k in o["reads"]:
                if k in last_writer:
                    deps.add(last_writer[k])
            for k in o["writes"]:
                if k in last_writer:
                    deps.add(last_writer[k])
                deps.update(readers.get(k, ()))
            deps.discard(i)
            o["deps"] = deps
            for k in o["reads"]:
                readers.setdefault(k, []).append(i)
            for k in o["writes"]:
                last_writer[k] = i
                readers[k] = []
        for o in ops:
            o["signal"] = o["sem"] is not None
        for o in ops:
            for d in o["deps"]:
                p = ops[d]
                if p["sem"] is None:
                    if p["eng"] == o["eng"] and (p["eng"] == "pe" or not SAME_ENG_SYNC) and o["sem"] is None:
                        continue
                    p["signal"] = True
        cnt = {}
        semnames = set()
        LIM = 1000
        for o in ops:
            if o["sem"] is not None:
                s = o["sem"]
                o["semname"] = s
                cnt[s] = cnt.get(s, 0) + 16
                o["tick"] = cnt[s]
                semnames.add(s)
            elif o["signal"]:
                base = "eng_" + o["eng"]
                c = cnt.get(base, 0)
                cnt[base] = c + 1
                s = "%s_%d" % (base, c // LIM)
                o["semname"] = s
                o["tick"] = c % LIM + 1
                semnames.add(s)
            else:
                o["semname"] = None
        sems = {s: stack.enter_context(nc.semaphore(s)) for s in sorted(semnames)}
        self.maxcnt = cnt

        def run_engine(engname, e):
            known = {}
            for o in ops:
                if o["eng"] != engname:
                    continue
                need = {}
                for d in o["deps"]:
                    p = ops[d]
                    if p["sem"] is None and p["eng"] == engname and o["sem"] is None:
                        if engname == "pe" or not SAME_ENG_SYNC:
                            continue
                    if not p["signal"]:
                        continue
                    s = p["semname"]
                    need[s] = max(need.get(s, 0), p["tick"])
                for s, v in need.items():
                    if known.get(s, 0) < v:
                        e.wait_ge(sems[s], v)
                        known[s] = v
                inst = o["fn"](e)
                if o["signal"]:
                    assert inst is not None
                    inst.then_inc(sems[o["semname"]], 16 if o["sem"] is not None else 1)

        @block.tensor
        def _(e):
            run_engine("pe", e)

        @block.scalar
        def _(e):
            run_engine("act", e)

        @block.vector
        def _(e):
            run_engine("dve", e)

        @block.gpsimd
        def _(e):
            run_engine("pool", e)

        @block.sync
        def _(e):
            run_engine("sp", e)


def build_nc(debug=False, ntb=NTB):
    import contextlib
    nc = bass.Bass("TRN2", target_bir_lowering=False)
    P = Prog()
    dr = {}
    dr["xT"] = nc.dram_tensor("xT", [D, T], F32, kind="ExternalInput").ap()
    dr["w_in"] = nc.dram_tensor("w_in", [D, DIN], F32, kind="ExternalInput").ap()
    dr["w_bsc"] = nc.dram_tensor("w_bsc", [D, D], F32, kind="ExternalInput").ap()
    dr["w_bssm"] = nc.dram_tensor("w_bssm", [2 * D, D], F32, kind="ExternalInput").ap()
    dr["w_out"] = nc.dram_tensor("w_out", [D, D], F32, kind="ExternalInput").ap()
    dr["w_m1"] = nc.dram_tensor("w_m1", [D, DFF], F32, kind="ExternalInput").ap()
    dr["w_m2"] = nc.dram_tensor("w_m2", [DFF, D], F32, kind="ExternalInput").ap()
    dr["vecs"] = nc.dram_tensor("vecs", [128, NV], F32, kind="ExternalInput").ap()
    dr["consts"] = nc.dram_tensor("consts", [128, 512], F32, kind="ExternalInput").ap()
    outT = nc.dram_tensor("outT", [D, T], F32, kind="ExternalOutput").ap()
    s_in = nc.dram_tensor("s_in", [D, DIN], BF16, kind="Internal").ap()
    s_bsc = nc.dram_tensor("s_bsc", [D, D], BF16, kind="Internal").ap()
    s_bssm = nc.dram_tensor("s_bssm", [2 * D, D], BF16, kind="Internal").ap()
    s_out = nc.dram_tensor("s_out", [D, D], BF16, kind="Internal").ap()
    s_m1 = nc.dram_tensor("s_m1", [D, DFF], BF16, kind="Internal").ap()
    s_m2 = nc.dram_tensor("s_m2", [DFF, D], BF16, kind="Internal").ap()
    dbg_out = {}

    with contextlib.ExitStack() as st:
        def sb(name, shape, dt):
            return st.enter_context(nc.sbuf_tensor(name, shape, dt))

        vecs = sb("vecs_sb", [128, NV], F32)
        cst = sb("cst", [128, 512], F32)
        cbf = sb("cbf", [128, 384], BF16)
        Abc = sb("Abc", [128, 32], F32)
        XB = sb("XB", [128, 8, TB], F32)
        sq = sb("sq", [128, 8, TB], BF16)
        hT = sb("hT", [128, 8, TB], BF16)
        rt = sb("rt", [128, TB], F32)
        rstd = sb("rstd", [128, TB], F32)
        Bsb = sb("Bsb", [128, TB], F32)
        Csb = sb("Csb", [128, TB], F32)
        usc = sb("usc", [128, TB + 2], BF16)
        dgs = sb("dgs", [128, 3, 128], BF16)
        car_sc = sb("car_sc", [128, 8, 2], F32)
        car_ss = sb("car_ss", [128, 8, 4, 3], F32)
        yaT = sb("yaT", [128, 8, TB], BF16)
        tbl = sb("tbl", [128, 8, 128], F32)
        BIG1 = sb("BIG1", [128, 8192], F32)
        BIG2 = sb("BIG2", [128, 6152], F32)
        R_ = [sb(f"R{i}", [128, 2, 4, 128], BF16) for i in range(2)]
        dAhb = sb("dAhb", [128, 128], BF16)
        dAhi = sb("dAhi", [128, 128], F32)
        dAlo = sb("dAlo", [128, 128], F32)
        LT_ = [sb(f"LT{i}", [128, 4, 128], F32) for i in range(2)]
        ecs_ = [sb(f"ecs{i}", [128, 4, 128], F32) for i in range(2)]
        CBm_ = [sb(f"CBm{i}", [128, 128], F32) for i in range(2)]
        WT_ = [sb(f"WT{i}", [128, 4, 128], BF16) for i in range(2)]
        CsT_ = [sb(f"CsT{i}", [128, 4, 128], BF16) for i in range(2)]
        xdt_ = [sb(f"xdt{i}", [128, 4, 64], BF16) for i in range(2)]
        xdtd_ = [sb(f"xdtd{i}", [128, 4, 64], BF16) for i in range(2)]
        Btok_ = [sb(f"Btok{i}", [128, 128], BF16) for i in range(2)]
        stf = sb("stf", [128, 8, 256], F32)
        stb = sb("stb", [128, 8, 256], BF16)
        ybT = sb("ybT", [128, 16, TB], BF16)
        mgT = sq
        h2T = hT
        gsb = sb("gsb", [128, TB], F32)
        m1sb = sb("m1sb", [128, TB], F32)
        m2sb = sb("m2sb", [128, TB], F32)
        rl = sb("rl", [128, TB], F32)
        wdt = sb("wdt", [128, 8, 32], BF16)
        dummy = sb("fence_t", [128, 4], F32)
        WP = [sb(f"WP{i}", [128, 4096], BF16) for i in range(NW)]
        mm = st.enter_context(nc.psum_tensor("mm", [128, 4, 512], F32))
        S0 = st.enter_context(nc.psum_tensor("S0", [128, 512], F32))
        S1 = st.enter_context(nc.psum_tensor("S1", [128, 512], F32))
        S2 = st.enter_context(nc.psum_tensor("S2", [128, 512], F32))
        S3 = st.enter_context(nc.psum_tensor("S3", [128, 512], F32))

        ident = cst[:, 0:128]
        tri = cst[:, 128:256]
        Umat = cst[:, 256:384]
        ones = cst[:, 384:512]
        ident_bf = cbf[:, 0:128]
        ones_bf = cbf[:, 128:256]
        U_bf = cbf[:, 256:384]
        GS = []
        for si_, big in enumerate((BIG1, BIG2)):
            GS.append(dict(
                zs=big[:, 0:1024].rearrange("p (j t) -> p j t", j=2),
                u4=big[:, 1024:1024 + 1030].bitcast(BF16).rearrange("p (b t) -> p b t", b=4),
                dg=big[:, 2056:3080].bitcast(BF16).rearrange("p (m c) -> p m c", m=16),
                xsT=big[:, 3080:4104].rearrange("p (j t) -> p j t", j=2),
                yT=big[:, 4104:5128].rearrange("p (j t) -> p j t", j=2),
                BT=big[:, 5128:5384].bitcast(BF16),
                CT=big[:, 5384:5640].bitcast(BF16),
                sq2=big[:, 5640:6152].bitcast(BF16).rearrange("p (j t) -> p j t", j=2),
                rt=(rt, gsb)[si_], rstd=(rstd, m1sb)[si_], rtk=("rt", "gsb")[si_], rstdk=("rstd", "m1sb")[si_]))
        PS = [(S0, S1, S2, S3), (S0, S1, S2, S3)]
        aT = BIG1[:, :].bitcast(BF16).rearrange("p (f t) -> p f t", f=32)

        T1, TE, TDT, TDA, TCS, T2, TDEC, TCD = [tbl[:, i, :].rearrange("p (c h) -> p c h", c=4) for i in range(8)]

        mmc = [0]

        def mmbank():
            i = mmc[0] % 4
            mmc[0] += 1
            return mm[:, i, :], ("mm", i)

        wpc = [0]

        def wload(src, shape, reads):
            i = wpc[0] % NW
            wpc[0] += 1
            n = int(np.prod(shape[1:]))
            dst = WP[i][:, 0:n]
            ka, kb = ("WP", i, "a"), ("WP", i, "b")
            if len(shape) == 3:
                dst = dst.rearrange("p (a b) -> p a b", a=shape[1])
                P.dma("sp", f"wp{i}a", lambda e, dst=dst, src=src: e.dma_start(out=dst, in_=src),
                      reads=reads, writes=[ka, kb])
            else:
                dst = dst.rearrange("p (s a b) -> p s a b", s=shape[1], a=shape[2])
                for s_, kk in ((0, ka), (1, kb)):
                    P.dma("sp", f"wp{i}" + "ab"[s_], lambda e, d=dst[:, s_], r=src[:, s_]: e.dma_start(out=d, in_=r),
                          reads=reads, writes=[kk])
            return dst, [ka, kb]

        def proj(ps, lhs, rhs, nk):
            def fn(e):
                last = None
                for k in range(nk):
                    last = e.matmul(ps, lhs(k), rhs(k), start=(k == 0), stop=(k == nk - 1))
                return last
            return fn

        def dbg(name, ap, shape, reads):
            if not debug:
                return
            d = nc.dram_tensor("dbg_" + name, list(shape), ap.dtype, kind="ExternalOutput").ap()
            dbg_out[name] = d
            P.dma("sp", "dbg_" + name, lambda e: e.dma_start(out=d, in_=ap), reads=reads, writes=[("dbg", name)])

        P.dma("sp", "ld_vecs", lambda e: e.dma_start(out=vecs[:], in_=dr["vecs"]), writes=["vecs"])
        P.dma("sp", "ld_cst", lambda e: e.dma_start(out=cst[:], in_=dr["consts"]), writes=["cst"])
        P.op("dve", lambda e: e.tensor_copy(out=cbf[:, 0:128], in_=ident), reads=["cst"], writes=["cbf0"])
        P.op("dve", lambda e: e.tensor_copy(out=cbf[:, 128:256], in_=ones), reads=["cst"], writes=["cbf1"])
        P.op("dve", lambda e: e.tensor_copy(out=cbf[:, 256:384], in_=Umat), reads=["cst"], writes=["cbf2"])
        CB = ["cbf0", "cbf1", "cbf2", "cst"]
        P.op("act", lambda e: e.activation(out=Abc[:], in_=vecs[:, V_ALOG:V_ALOG + 32], func=AF.Exp),
             reads=["vecs"], writes=["Abc"])
        P.op("dve", lambda e: e.tensor_scalar(out=Abc[:], in0=Abc[:], scalar1=-1.0, scalar2=None, op0=ALU.mult),
             reads=["Abc"], writes=["Abc"])
        P.op("dve", lambda e: e.memset(stf[:], 0.0), writes=[("stf", g) for g in range(8)])
        P.op("dve", lambda e: e.memset(stb[:], 0.0), writes=[("stb", g) for g in range(8)])
        P.op("dve", lambda e: e.memset(car_sc[:], 0.0), writes=["car_sc"])
        P.op("dve", lambda e: e.memset(car_ss[:], 0.0), writes=["car_ss"])

        castc = [0]

        def cast(dst, src, key):
            i = castc[0]
            castc[0] += 1
            P.dma("pool", f"cast{i}",
                  lambda e: e.dma_start(out=dst, in_=src, max_dma_last_dim=4096),
                  writes=[key])

        K_SC = [("scr", "in_sc", r) for r in range(2)]
        for r in range(2):
            cast(s_in[r * 512:(r + 1) * 512, 0:3072], dr["w_in"][r * 512:(r + 1) * 512, 0:3072], K_SC[r])
        K_DT = [("scr", "in_dt")]
        cast(s_in[:, C_DT:C_DT + 32], dr["w_in"][:, C_DT:C_DT + 32], K_DT[0])
        K_SSD = [("scr", "in_ssd", r) for r in range(4)]
        for r in range(4):
            cast(s_in[r * 256:(r + 1) * 256, C_Z:C_DT], dr["w_in"][r * 256:(r + 1) * 256, C_Z:C_DT], K_SSD[r])
        K_G = [("scr", "in_g", r) for r in range(2)]
        for r in range(2):
            cast(s_in[r * 512:(r + 1) * 512, C_G:DIN], dr["w_in"][r * 512:(r + 1) * 512, C_G:DIN], K_G[r])
        K_BSC = [("scr", "bsc")]
        cast(s_bsc, dr["w_bsc"], K_BSC[0])
        K_BSSM = [("scr", "bssm", r) for r in range(2)]
        for r in range(2):
            cast(s_bssm[r * 1024:(r + 1) * 1024, :], dr["w_bssm"][r * 1024:(r + 1) * 1024, :], K_BSSM[r])
        K_OUT = [("scr", "out")]
        cast(s_out, dr["w_out"], K_OUT[0])
        K_M1 = [("scr", "m1", r) for r in range(4)]
        for r in range(4):
            cast(s_m1[r * 256:(r + 1) * 256, :], dr["w_m1"][r * 256:(r + 1) * 256, :], K_M1[r])
        K_M2 = [("scr", "m2", r) for r in range(4)]
        for r in range(4):
            cast(s_m2[r * 1024:(r + 1) * 1024, :], dr["w_m2"][r * 1024:(r + 1) * 1024, :], K_M2[r])

        v_in = s_in.rearrange("(k p) n -> p k n", p=128)
        P.dma("sp", "ld_wdt", lambda e: e.dma_start(out=wdt[:], in_=v_in[:, :, C_DT:C_DT + 32]),
              reads=K_DT, writes=["wdt"])

        xT_v = dr["xT"].rearrange("(k p) t -> p k t", p=128)
        outT_v = outT.rearrange("(k p) t -> p k t", p=128)

        def rmsnorm_to(dst_bf, vcol, ndiv, tag):
            P.op("act", lambda e: e.activation(out=sq[:], in_=XB[:], func=AF.Square),
                 reads=[("XB", k) for k in range(8)], writes=["sq"])
            ps, pk = mmbank()
            P.op("pe", proj(ps, lambda k: ones_bf, lambda k: sq[:, k, :], 8), reads=["sq"] + CB, writes=[pk])
            P.op("act", lambda e: e.activation(out=rt[:], in_=ps, func=AF.Ln, scale=1.0 / ndiv, bias=EPS),
                 reads=[pk], writes=["rt"])
            P.op("act", lambda e: e.activation(out=rstd[:], in_=rt[:], func=AF.Exp, scale=-0.5), reads=["rt"], writes=["rstd"])
            for k in range(8):
                P.op("dve", lambda e, k=k: e.scalar_tensor_tensor(
                    out=dst_bf[:, k, :], in0=XB[:, k, :], scalar=vecs[:, vcol + k:vcol + k + 1], in1=rstd[:],
                    op0=ALU.mult, op1=ALU.mult),
                    reads=[("XB", k), "rstd", "vecs"], writes=[(tag, k)])

        chunkc = [0]

        for tb in range(ntb):
            t0 = tb * TB
            PL = "pool" if (USE_POOL and tb > 0) else "dve"
            P.op("dve", lambda e: e.memset(dummy[:, 2:3], 0.0), writes=["G3", "dummy2"])
            P.dma("sp", "ld_x", lambda e, t0=t0: e.dma_start(out=XB[:], in_=xT_v[:, :, t0:t0 + TB]),
                  writes=[("XB", k) for k in range(8)])
            rmsnorm_to(hT, V_NMIX, float(D), "hT")
            HT = [("hT", k) for k in range(8)]
            if tb == 0:
                dbg("hT", hT[:], [128, 8, TB], HT)

            for half in range(2):
                c0 = half * 512
                wB, kB = wload(v_in[:, :, C_SC + c0:C_SC + c0 + 512], [128, 8, 512], K_SC)
                wC, kC = wload(v_in[:, :, C_SC + 1024 + c0:C_SC + 1024 + c0 + 512], [128, 8, 512], K_SC)
                wX, kX = wload(v_in[:, :, C_SC + 2048 + c0:C_SC + 2048 + c0 + 512], [128, 8, 512], K_SC)
                for c4 in range(4):
                    cb = half * 4 + c4
                    cs_ = slice(c4 * 128, (c4 + 1) * 128)
                    psB, pkB = mmbank()
                    P.op("pe", proj(psB, lambda k, w=wB, s=cs_: w[:, k, s], lambda k: hT[:, k, :], 8),
                         reads=HT + kB, writes=[pkB])
                    psC, pkC = mmbank()
                    P.op("pe", proj(psC, lambda k, w=wC, s=cs_: w[:, k, s], lambda k: hT[:, k, :], 8),
                         reads=HT + kC, writes=[pkC])
                    psX, pkX = mmbank()
                    P.op("pe", proj(psX, lambda k, w=wX, s=cs_: w[:, k, s], lambda k: hT[:, k, :], 8),
                         reads=HT + kX, writes=[pkX])
                    P.op("act", lambda e, ps=psB: e.activation(out=Bsb[:], in_=ps, func=AF.Identity),
                         reads=[pkB], writes=["Bsb"])
                    P.op("act", lambda e, ps=psC: e.activation(out=Csb[:], in_=ps, func=AF.Identity),
                         reads=[pkC], writes=["Csb"])
                    P.op("dve", lambda e, cb=cb: e.tensor_copy(out=usc[:, 0:2], in_=car_sc[:, cb, :]),
                         reads=["car_sc"], writes=["usc_h"])
                    P.op("dve", lambda e, ps=psX: e.tensor_tensor(out=usc[:, 2:TB + 2], in0=ps, in1=Csb[:], op=ALU.mult),
                         reads=[pkX, "Csb"], writes=["usc"])
                    P.op("dve", lambda e, cb=cb: e.tensor_copy(out=car_sc[:, cb, :], in_=usc[:, TB:TB + 2]),
                         reads=["usc"], writes=["car_sc"])
                    wv = V_SCW + cb * 3
                    P.op("dve", lambda e, wv=wv: e.tensor_tensor(
                        out=dgs[:], in0=ident_bf.unsqueeze(1).broadcast_to([128, 3, 128]),
                        in1=vecs[:, wv:wv + 3].unsqueeze(2).broadcast_to([128, 3, 128]), op=ALU.mult),
                        reads=["vecs"] + CB, writes=["dgs"])
                    psV, pkV = mmbank()

                    def scconv(e, psV=psV):
                        last = None
                        for tap in range(3):
                            last = e.matmul(psV, dgs[:, tap, :], usc[:, tap:tap + TB], start=(tap == 0), stop=(tap == 2))
                        return last
                    P.op("pe", scconv, reads=["dgs", "usc", "usc_h"], writes=[pkV])
                    P.op("dve", lambda e, cb=cb, psV=psV: e.tensor_tensor(out=yaT[:, cb, :], in0=psV, in1=Bsb[:], op=ALU.mult),
                         reads=[pkV, "Bsb"], writes=[("yaT", cb)])
            YA = [("yaT", k) for k in range(8)]
            if tb == 0:
                dbg("yaT", yaT[:], [128, 8, TB], YA)

            ps, pk = mmbank()

            def dtmm(e, ps=ps):
                last = None
                for cc in range(4):
                    for k in range(8):
                        last = e.matmul(ps[:, cc * 32:(cc + 1) * 32], hT[:, k, cc * 128:(cc + 1) * 128], wdt[:, k, :],
                                        start=(k == 0), stop=(k == 7))
                return last
            P.op("pe", dtmm, reads=HT + ["wdt"], writes=[pk])
            dtb_b = vecs[:, V_DTB:V_DTB + 32].unsqueeze(1).broadcast_to([128, 4, 32])
            A_b = Abc[:].unsqueeze(1).broadcast_to([128, 4, 32])
            P.op("dve", lambda e, ps=ps: e.tensor_tensor(out=T1, in0=ps[:, 0:128].rearrange("p (c h) -> p c h", c=4),
                                                         in1=dtb_b, op=ALU.add),
                 reads=[pk, "vecs"], writes=["T1"])
            P.op("act", lambda e: e.activation(out=TE, in_=T1, func=AF.Exp), reads=["T1"], writes=["TE"])
            P.op("act", lambda e: e.activation(out=TDT, in_=TE, func=AF.Ln, bias=1.0), reads=["TE"], writes=["TDT"])
            P.op("dve", lambda e: e.tensor_tensor(out=TDA, in0=TDT, in1=A_b, op=ALU.mult),
                 reads=["TDT", "Abc"], writes=["TDA"])
            P.op("dve", lambda e: e.tensor_copy(out=dAhb[:], in_=tbl[:, 3, :]), reads=["TDA"], writes=["dAhb"])
            P.op("dve", lambda e: e.tensor_copy(out=dAhi[:], in_=dAhb[:]), reads=["dAhb"], writes=["dAhi"])
            P.op("dve", lambda e: e.tensor_tensor(out=dAlo[:], in0=tbl[:, 3, :], in1=dAhi[:], op=ALU.subtract),
                 reads=["TDA", "dAhi"], writes=["dAlo"])
            ps2, pk2 = mmbank()

            def csmm(e, ps2=ps2):
                e.matmul(ps2[:, 0:128], tri, tbl[:, 3, :], start=True, stop=True)
                return e.matmul(ps2[:, 128:256], ones, tbl[:, 3, :], start=True, stop=True)
            P.op("pe", csmm, reads=["TDA", "cst"], writes=[pk2])
            P.op("act", lambda e, ps2=ps2: e.activation(out=tbl[:, 4, :], in_=ps2[:, 0:128], func=AF.Identity),
                 reads=[pk2], writes=["TCS"])
            P.op("dve", lambda e, ps2=ps2: e.tensor_tensor(out=tbl[:, 5, :], in0=ps2[:, 128:256], in1=tbl[:, 4, :],
                                                           op=ALU.subtract),
                 reads=[pk2, "TCS"], writes=["T2"])
            P.op("act", lambda e: e.activation(out=tbl[:, 6, :], in_=tbl[:, 5, :], func=AF.Exp),
                 reads=["T2"], writes=["TDEC"])
            P.op("act", lambda e, ps2=ps2: e.activation(out=tbl[:, 7, :], in_=ps2[:, 128:256], func=AF.Exp),
                 reads=[pk2], writes=["TCD"])
            if tb == 0:
                dbg("tbl", tbl[:], [128, 8, 128], ["T1", "TE", "TDT", "TDA", "TCS", "T2", "TDEC", "TCD"])

            P.op("dve", lambda e: e.memset(dummy[:, 0:1], 0.0), writes=["BIG1", "dummy0"])
            v_zx = s_in[:, C_Z:C_B].rearrange("(k p) (s n) -> p s k n", p=128, s=2)
            v_bc = s_in[:, C_B:C_DT].rearrange("(k p) (s n) -> p s k n", p=128, s=2)

            def group_gen(g, si, wBC, kBC, wZX, kZX):
                B_ = GS[si]
                zs, u4, dg, xsT, yT, BT, CT, sq2 = (B_["zs"], B_["u4"], B_["dg"], B_["xsT"], B_["yT"], B_["BT"],
                                                     B_["CT"], B_["sq2"])
                rt_, rstd_ = B_["rt"], B_["rstd"]
                S0, S1, S2, S3 = PS[si]
                kS = lambda n: "S%d" % n
                K = lambda n, *a: (n, si) + tuple(a)
                go = (g % 2) * 128
                for (m0, nm, chb, wk) in ((0, 8, 2 * g, [K("dg", 0), K("dg", 1)]), (8, 4, 16 + g, [K("dg", 2)]),
                                          (12, 4, 24 + g, [K("dg", 3)])):
                    wv0 = V_SSW + chb * 4
                    P.op("dve", lambda e, m0=m0, nm=nm, wv0=wv0: e.tensor_tensor(
                        out=dg[:, m0:m0 + nm, :], in0=ident_bf.unsqueeze(1).broadcast_to([128, nm, 128]),
                        in1=vecs[:, wv0:wv0 + nm].unsqueeze(2).broadcast_to([128, nm, 128]), op=ALU.mult),
                        reads=["vecs"] + CB, writes=wk)
                for j in range(2):
                    ps, pk = mmbank()
                    P.op("pe", proj(ps, lambda k, j=j, w=wZX: w[:, 0, k, j * 128:(j + 1) * 128], lambda k: hT[:, k, :], 8),
                         reads=HT + kZX, writes=[pk])
                    P.op("act", lambda e, ps=ps, j=j: e.activation(out=zs[:, j, :], in_=ps, func=AF.Silu),
                         reads=[pk], writes=[K("zs", j)])
                    yield
                P.op(PL, lambda e: e.tensor_copy(out=u4[:, :, 0:3], in_=car_ss[:, g, :, :]),
                     reads=[("car_ss", g)], writes=[K("u4_h")])
                for b in range(4):
                    ps, pk = mmbank()
                    if b < 2:
                        lhs = (lambda k, b=b, w=wZX: w[:, 1, k, b * 128:(b + 1) * 128])
                        rk = kZX
                    else:
                        lhs = (lambda k, b=b, go=go, w=wBC: w[:, b - 2, k, go:go + 128])
                        rk = kBC
                    P.op("pe", proj(ps, lhs, lambda k: hT[:, k, :], 8), reads=HT + rk, writes=[pk])
                    P.op("act", lambda e, ps=ps, b=b: e.activation(out=u4[:, b, 3:TB + 3], in_=ps, func=AF.Identity),
                         reads=[pk], writes=[K("u4", b)])
                    yield
                P.op(PL, lambda e: e.tensor_copy(out=car_ss[:, g, :, :], in_=u4[:, :, TB:TB + 3]),
                     reads=[K("u4", b) for b in range(4)], writes=[("car_ss", g)])
                for b in range(4):
                    ch = (2 * g + b) if b < 2 else (16 + g if b == 2 else 24 + g)
                    wv = V_SSW + ch * 4
                    bv = V_SSB + ch
                    ps, pk = mmbank()

                    def ssconv(e, ps=ps, b=b):
                        last = None
                        for tap in range(4):
                            last = e.matmul(ps, dg[:, b * 4 + tap, :], u4[:, b, tap:tap + TB], start=(tap == 0), stop=(tap == 3))
                        return last
                    P.op("pe", ssconv, reads=[K("dg", b), K("u4", b), K("u4_h")], writes=[pk])
                    if b < 2:
                        P.op("act", lambda e, b=b, ps=ps, bv=bv: e.activation(out=xsT[:, b, :], in_=ps, func=AF.Silu,
                                                                              bias=vecs[:, bv:bv + 1]),
                             reads=[pk, "vecs"], writes=[K("xsT", b)])
                    elif b == 2:
                        P.op("act", lambda e, ps=ps, bv=bv: e.activation(out=BT, in_=ps, func=AF.Silu, bias=vecs[:, bv:bv + 1]),
                             reads=[pk, "vecs"], writes=[K("BT")])
                    else:
                        P.op("act", lambda e, ps=ps, bv=bv: e.activation(out=CT, in_=ps, func=AF.Silu, bias=vecs[:, bv:bv + 1]),
                             reads=[pk, "vecs"], writes=[K("CT")])
                    yield
                if tb == 0 and g == 0:
                    dbg("xsT", xsT, [128, 2, TB], [K("xsT", 0), K("xsT", 1)])
                    dbg("BT", BT, [128, TB], [K("BT")])
                    dbg("CT", CT, [128, TB], [K("CT")])
                    dbg("zs", zs, [128, 2, TB], [K("zs", 0), K("zs", 1)])
                par = si
                R, LT, ecs, CBm, WT, CsT, xdt, xdtd, Btok = (R_[par], LT_[par], ecs_[par], CBm_[par], WT_[par],
                                                             CsT_[par], xdt_[par], xdtd_[par], Btok_[par])
                for cc in range(4):
                    tc_ = slice(cc * 128, (cc + 1) * 128)

                    def rbuild(e, cc=cc):
                        last = None
                        for s_, src in ((0, dAhi), (1, dAlo)):
                            for h in range(4):
                                c0 = cc * 32 + 4 * g + h
                                last = e.tensor_scalar(out=R[:, s_, h, :], in0=tri, scalar1=src[:, c0:c0 + 1],
                                                       scalar2=None, op0=ALU.mult)
                        return last
                    P.op(PL, rbuild, reads=["dAhi", "dAlo", "cst"], writes=[("R", par)])

                    def pe1(e, tc_=tc_):
                        e.transpose(S2[:, 0:128], xsT[:, 0, tc_], ident)
                        e.transpose(S2[:, 128:256], xsT[:, 1, tc_], ident)
                        e.matmul(S2[:, 256:384], BT[:, tc_], CT[:, tc_], start=True, stop=True)
                        return e.matmul(S2[:, 384:512], BT[:, tc_], ident_bf, start=True, stop=True)
                    P.op("pe", pe1, reads=[K("xsT", 0), K("xsT", 1), K("BT"), K("CT")] + CB, writes=[kS(2)])

                    def pe2(e):
                        rh = R[:, 0].rearrange("p h i -> p (h i)")
                        rl_ = R[:, 1].rearrange("p h i -> p (h i)")
                        e.matmul(S0[:], U_bf, rh, start=True, stop=False)
                        e.matmul(S0[:], U_bf, rl_, start=False, stop=True)
                        e.matmul(S1[:], ones_bf, rh, start=True, stop=False)
                        return e.matmul(S1[:], ones_bf, rl_, start=False, stop=True)
                    P.op("pe", pe2, reads=[("R", par)] + CB, writes=[kS(0), kS(1)])
                    P.op("act", lambda e: e.activation(out=LT[:].rearrange("p h i -> p (h i)"), in_=S0[:], func=AF.Exp),
                         reads=[kS(0)], writes=[("LT", par)])
                    P.op("act", lambda e: e.activation(out=ecs[:].rearrange("p h i -> p (h i)"), in_=S1[:], func=AF.Exp),
                         reads=[kS(1)], writes=[("ecs", par)])
                    P.op("dve", lambda e: e.tensor_tensor(out=CBm[:], in0=S2[:, 256:384], in1=tri, op=ALU.mult),
                         reads=[kS(2), "cst"], writes=[("CBm", par)])
                    P.op("dve", lambda e, cc=cc: e.tensor_tensor(
                        out=xdt[:], in0=S2[:, 0:256].rearrange("p (h q) -> p h q", h=4),
                        in1=TDT[:, cc, 4 * g:4 * g + 4].unsqueeze(2).broadcast_to([128, 4, 64]), op=ALU.mult),
                        reads=[kS(2), "TDT"], writes=[("xdt", par)])
                    P.op("dve", lambda e: e.tensor_copy(out=Btok[:], in_=S2[:, 384:512]),
                         reads=[kS(2)], writes=[("Btok", par)])
                    yield
                    P.op("dve", lambda e: e.tensor_tensor(
                        out=WT[:], in0=LT[:], in1=CBm[:].unsqueeze(1).broadcast_to([128, 4, 128]), op=ALU.mult),
                        reads=[("LT", par), ("CBm", par)], writes=[("WT", par)])
                    P.op(PL, lambda e, tc_=tc_: e.tensor_tensor(
                        out=CsT[:], in0=ecs[:], in1=CT[:, tc_].unsqueeze(1).broadcast_to([128, 4, 128]), op=ALU.mult),
                        reads=[("ecs", par), K("CT")], writes=[("CsT", par)])
                    P.op(PL, lambda e, cc=cc: e.tensor_tensor(
                        out=xdtd[:], in0=xdt[:],
                        in1=TDEC[:, cc, 4 * g:4 * g + 4].unsqueeze(2).broadcast_to([128, 4, 64]), op=ALU.mult),
                        reads=[("xdt", par), "TDEC"], writes=[("xdtd", par)])
                    yield

                    def pe3(e):
                        for h in range(4):
                            o = S3[64 * (h % 2):64 * (h % 2) + 64, (h // 2) * 128:(h // 2) * 128 + 128]
                            e.matmul(o, xdt[:, h, :], WT[:, h, :], start=True, stop=False)
                            e.matmul(o, stb[:, g, h * 64:(h + 1) * 64], CsT[:, h, :], start=False, stop=True)
                        return e.matmul(S3[:, 256:512], Btok[:], xdtd[:].rearrange("p h q -> p (h q)"), start=True, stop=True)
                    P.op("pe", pe3, reads=[("xdt", par), ("WT", par), ("CsT", par), ("Btok", par), ("xdtd", par), ("stb", g)],
                         writes=[kS(3)])

                    def yev(e, tc_=tc_):
                        last = None
                        for j in range(2):
                            dv = V_DCOL + 2 * g + j
                            last = e.scalar_tensor_tensor(out=yT[:, j, tc_], in0=xsT[:, j, tc_], scalar=vecs[:, dv:dv + 1],
                                                          in1=S3[:, j * 128:(j + 1) * 128], op0=ALU.mult, op1=ALU.add)
                        return last
                    P.op("dve", yev, reads=[kS(3), K("xsT", 0), K("xsT", 1), "vecs"], writes=[K("yT", cc)])
                    P.op(PL, lambda e, cc=cc: e.tensor_tensor(
                        out=stf[:, g, :].rearrange("p (h q) -> p h q", h=4),
                        in0=stf[:, g, :].rearrange("p (h q) -> p h q", h=4),
                        in1=TCD[:, cc, 4 * g:4 * g + 4].unsqueeze(2).broadcast_to([128, 4, 64]), op=ALU.mult),
                        reads=[("stf", g), "TCD"], writes=[("stf", g)])
                    P.op("dve", lambda e: e.tensor_tensor(out=stf[:, g, :], in0=stf[:, g, :], in1=S3[:, 256:512], op=ALU.add),
                         reads=[("stf", g), kS(3)], writes=[("stf", g)])
                    P.op("act", lambda e: e.activation(out=stb[:, g, :], in_=stf[:, g, :], func=AF.Identity),
                         reads=[("stf", g)], writes=[("stb", g)])
                    yield
                YT = [K("yT", cc) for cc in range(4)]
                if tb == 0 and g == 0:
                    dbg("yT", yT, [128, 2, TB], YT)
                P.op(PL, lambda e: e.tensor_tensor(out=yT, in0=yT, in1=zs, op=ALU.mult),
                     reads=YT + [K("zs", 0), K("zs", 1)], writes=YT)
                P.op("act", lambda e: e.activation(out=sq2, in_=yT, func=AF.Square), reads=YT, writes=[K("sq2")])
                ps, pk = mmbank()
                P.op("pe", proj(ps, lambda k: ones_bf, lambda k: sq2[:, k, :], 2), reads=[K("sq2")] + CB, writes=[pk])
                yield
                P.op("act", lambda e, ps=ps: e.activation(out=rt_[:], in_=ps, func=AF.Ln, scale=1.0 / 256.0, bias=EPS),
                     reads=[pk], writes=[B_["rtk"]])
                P.op("act", lambda e: e.activation(out=rstd_[:], in_=rt_[:], func=AF.Exp, scale=-0.5), reads=[B_["rtk"]], writes=[B_["rstdk"]])
                for j in range(2):
                    nv = V_SNW + 2 * g + j
                    P.op("dve", lambda e, j=j, nv=nv: e.scalar_tensor_tensor(
                        out=ybT[:, 2 * g + j, :], in0=yT[:, j, :], scalar=vecs[:, nv:nv + 1], in1=rstd_[:],
                        op0=ALU.mult, op1=ALU.mult),
                        reads=YT + [B_["rstdk"], "vecs"], writes=[("ybT", 2 * g + j)])
                yield

            for gp in range(4):
                g0 = 2 * gp
                wBC, kBC = wload(v_bc[:, :, :, 128 * g0:128 * g0 + 256], [128, 2, 8, 256], K_SSD)
                gens = []
                for si in range(2):
                    g = g0 + si
                    wZX, kZX = wload(v_zx[:, :, :, 256 * g:256 * g + 256], [128, 2, 8, 256], K_SSD)
                    gens.append(group_gen(g, si, wBC, kBC, wZX, kZX))
                alive = list(gens)
                while alive:
                    for gn in list(alive):
                        try:
                            next(gn)
                        except StopIteration:
                            alive.remove(gn)
            YB = [("ybT", k) for k in range(16)]
            if tb == 0:
                dbg("ybT", ybT[:], [128, 16, TB], YB)

            P.op("dve", lambda e: e.memset(dummy[:, 3:4], 0.0), writes=["G2", "dummy3"])
            v_g = s_in[:, C_G:DIN].rearrange("(k p) (s n) -> p s k n", p=128, s=2)
            v_bsc = s_bsc.rearrange("(k p) n -> p k n", p=128)
            v_bssm = s_bssm.rearrange("(k p) n -> p k n", p=128)
            for obp in range(4):
                o0 = obp * 256
                wG, kG = wload(v_g[:, :, :, o0:o0 + 256], [128, 2, 8, 256], K_G)
                wA, kA = wload(v_bsc[:, :, o0:o0 + 256], [128, 8, 256], K_BSC)
                wS, kS = wload(v_bssm[:, :, o0:o0 + 256], [128, 16, 256], K_BSSM)
                for o2 in range(2):
                    ob = obp * 2 + o2
                    os_ = slice(o2 * 128, (o2 + 1) * 128)
                    for s in range(2):
                        psg, pkg = mmbank()
                        P.op("pe", proj(psg, lambda k, s=s, os_=os_, w=wG: w[:, s, k, os_], lambda k: hT[:, k, :], 8),
                             reads=HT + kG, writes=[pkg])
                        psb, pkb = mmbank()
                        if s == 0:
                            P.op("pe", proj(psb, lambda k, os_=os_, w=wA: w[:, k, os_], lambda k: yaT[:, k, :], 8),
                                 reads=YA + kA, writes=[pkb])
                        else:
                            P.op("pe", proj(psb, lambda k, os_=os_, w=wS: w[:, k, os_], lambda k: ybT[:, k, :], 16),
                                 reads=YB + kS, writes=[pkb])
                        bg = V_BG + s * 8 + ob
                        P.op("act", lambda e, psg=psg, bg=bg: e.activation(out=gsb[:], in_=psg, func=AF.Sigmoid,
                                                                            bias=vecs[:, bg:bg + 1]),
                             reads=[pkg, "vecs"], writes=["gsb"])
                        msb = m1sb if s == 0 else m2sb
                        P.op("dve", lambda e, psb=psb, msb=msb: e.tensor_tensor(out=msb[:], in0=psb, in1=gsb[:], op=ALU.mult),
                             reads=[pkb, "gsb"], writes=["m1sb" if s == 0 else "m2sb"])
                    P.op(PL, lambda e, ob=ob: e.tensor_tensor(out=mgT[:, ob, :], in0=m1sb[:], in1=m2sb[:], op=ALU.add),
                         reads=["m1sb", "m2sb"], writes=[("mgT", ob)])
            MG = [("mgT", k) for k in range(8)]
            if tb == 0:
                dbg("mgT", mgT[:], [128, 8, TB], MG)

            v_out = s_out.rearrange("(k p) n -> p k n", p=128)
            for oh in range(2):
                wO, kO = wload(v_out[:, :, oh * 512:(oh + 1) * 512], [128, 8, 512], K_OUT)
                for o4 in range(4):
                    ob = oh * 4 + o4
                    ps, pk = mmbank()
                    P.op("pe", proj(ps, lambda k, o4=o4, wO=wO: wO[:, k, o4 * 128:(o4 + 1) * 128], lambda k: mgT[:, k, :], 8),
                         reads=MG + kO, writes=[pk])
                    P.op("dve", lambda e, ps=ps, ob=ob: e.tensor_tensor(out=XB[:, ob, :], in0=ps, in1=XB[:, ob, :], op=ALU.add),
                         reads=[pk, ("XB", ob)], writes=[("XB", ob)])
            if tb == 0:
                dbg("x1T", XB[:], [128, 8, TB], [("XB", k) for k in range(8)])

            P.op("dve", lambda e: e.memset(dummy[:, 2:3], 0.0), writes=["G2", "G3", "dummy2"])
            rmsnorm_to(h2T, V_NMLP, float(D), "h2T")
            H2 = [("h2T", k) for k in range(8)]
            P.op("dve", lambda e: e.memset(dummy[:, 1:2], 0.0), writes=["BIG1", "dummy1"])
            v_m1 = s_m1.rearrange("(k p) n -> p k n", p=128)
            for fb in range(8):
                wM, kM = wload(v_m1[:, :, fb * 512:(fb + 1) * 512], [128, 8, 512], K_M1)
                for f4 in range(4):
                    f = fb * 4 + f4
                    ps, pk = mmbank()
                    P.op("pe", proj(ps, lambda k, f4=f4, wM=wM: wM[:, k, f4 * 128:(f4 + 1) * 128], lambda k: h2T[:, k, :], 8),
                         reads=H2 + kM, writes=[pk])
                    P.op("act", lambda e, ps=ps: e.activation(out=rl[:], in_=ps, func=AF.Relu), reads=[pk], writes=["rl"])
                    P.op("act", lambda e, f=f: e.activation(out=aT[:, f, :], in_=rl[:], func=AF.Square),
                         reads=["rl"], writes=[("aT", f)])
            AT = [("aT", f) for f in range(32)]
            v_m2 = s_m2.rearrange("(k p) n -> p k n", p=128)
            for obp in range(4):
                o0 = obp * 256
                wh = []
                for fh in range(2):
                    wh.append(wload(v_m2[:, fh * 16:(fh + 1) * 16, o0:o0 + 256], [128, 16, 256], K_M2))
                for o2 in range(2):
                    ob = obp * 2 + o2
                    os_ = slice(o2 * 128, (o2 + 1) * 128)
                    ps, pk = mmbank()
                    P.op("pe", proj(ps, lambda k, os_=os_, wh=wh: wh[k // 16][0][:, k % 16, os_], lambda k: aT[:, k, :], 32),
                         reads=AT + wh[0][1] + wh[1][1], writes=[pk])
                    P.op("dve", lambda e, ps=ps, ob=ob: e.tensor_tensor(out=XB[:, ob, :], in0=ps, in1=XB[:, ob, :], op=ALU.add),
                         reads=[pk, ("XB", ob)], writes=[("XB", ob)])

            P.op("act", lambda e: e.activation(out=sq[:], in_=XB[:], func=AF.Square),
                 reads=[("XB", k) for k in range(8)], writes=["sq"])
            ps, pk = mmbank()
            P.op("pe", proj(ps, lambda k: ones_bf, lambda k: sq[:, k, :], 8), reads=["sq"] + CB, writes=[pk])
            P.op("act", lambda e, ps=ps: e.activation(out=rt[:], in_=ps, func=AF.Ln, scale=1.0 / D, bias=EPS),
                 reads=[pk], writes=["rt"])
            P.op("act", lambda e: e.activation(out=rstd[:], in_=rt[:], func=AF.Exp, scale=-0.5), reads=["rt"], writes=["rstd"])
            for k in range(8):
                P.op("dve", lambda e, k=k: e.scalar_tensor_tensor(
                    out=XB[:, k, :], in0=XB[:, k, :], scalar=vecs[:, V_NFIN + k:V_NFIN + k + 1], in1=rstd[:],
                    op0=ALU.mult, op1=ALU.mult),
                    reads=[("XB", k), "rstd", "vecs"], writes=[("XB", k)])
            P.dma("sp", "st_out", lambda e, t0=t0: e.dma_start(out=outT_v[:, :, t0:t0 + TB], in_=XB[:]),
                  reads=[("XB", k) for k in range(8)], writes=[("outT", tb)])

        fin_reads = [("outT", tb) for tb in range(ntb)] + [("dbg", n) for n in dbg_out]
        P.op("sp", lambda e: None, reads=fin_reads, writes=["fin"])

        block = st.enter_context(nc.Block())
        P.emit(nc, block, st)
    return nc, dbg_out


def _pack_inputs(inputs):
    f = lambda a: np.ascontiguousarray(np.asarray(a, dtype=np.float32))
    vecs = np.zeros((128, NV), np.float32)
    vecs[:, V_NMIX:V_NMIX + 8] = f(inputs["norm_mix"]).reshape(8, 128).T
    vecs[:, V_NMLP:V_NMLP + 8] = f(inputs["norm_mlp"]).reshape(8, 128).T
    vecs[:, V_NFIN:V_NFIN + 8] = f(inputs["norm_final"]).reshape(8, 128).T
    vecs[:, V_BG:V_BG + 16] = f(inputs["b_gate"]).reshape(16, 128).T
    vecs[:, V_SCW:V_SCW + 24] = f(inputs["sc_conv_w"]).reshape(3, 8, 128).transpose(2, 1, 0).reshape(128, 24)
    vecs[:, V_SSW:V_SSW + 128] = f(inputs["ssm_conv_w"]).reshape(4, 32, 128).transpose(2, 1, 0).reshape(128, 128)
    vecs[:, V_SSB:V_SSB + 32] = f(inputs["ssm_conv_b"]).reshape(32, 128).T
    vecs[:, V_SNW:V_SNW + 16] = f(inputs["ssm_norm_w"]).reshape(16, 128).T
    vecs[:, V_DCOL:V_DCOL + 16] = np.repeat(f(inputs["D_skip"]), 64).reshape(16, 128).T
    vecs[:, V_DTB:V_DTB + 32] = np.tile(f(inputs["dt_bias"])[None, :], (128, 1))
    vecs[:, V_ALOG:V_ALOG + 32] = np.tile(f(inputs["A_log"])[None, :], (128, 1))
    k = np.arange(128)
    consts = np.zeros((128, 512), np.float32)
    consts[:, 0:128] = np.eye(128, dtype=np.float32)
    consts[:, 128:256] = (k[:, None] <= k[None, :]).astype(np.float32)
    consts[:, 256:384] = (k[:, None] > k[None, :]).astype(np.float32)
    consts[:, 384:512] = 1.0
    x = f(inputs["x"])
    xT = np.ascontiguousarray(x.transpose(0, 2, 1))
    common = {
        "w_in": f(inputs["w_in"]), "w_bsc": f(inputs["w_branch_sc"]), "w_bssm": f(inputs["w_branch_ssm"]),
        "w_out": f(inputs["w_out"]), "w_m1": f(inputs["w_mlp1"]), "w_m2": f(inputs["w_mlp2"]),
        "vecs": vecs, "consts": consts,
    }
    return [dict(common, xT=xT[i]) for i in range(8)]


def kernel(**inputs):
    in_maps = _pack_inputs(inputs)
    nc, _ = build_nc()
    res = run_bass_kernel_spmd(nc, in_maps, core_ids=list(range(8)))
    out = np.stack([np.ascontiguousarray(res.results[i]["outT"].T) for i in range(8)], axis=0)
    return out.astype(np.float32)
```
